# Optimizing a Trainium2 kernel written in Bass

```python
import math
import jax, jax.numpy as jnp
from jax import lax
import numpy as np

D_MODEL = 2048
BATCH = 8
SEQ = 2048
DEPTH = 1

N_META = 16
GRID_W = 64
Q_BLOCK = 128
HEAD_DIM = 128
MIX_WIDTH = D_MODEL
A_WIDTH = MIX_WIDTH // 2
A_HEADS = A_WIDTH // HEAD_DIM
A_KV_HEADS = 2
A_GROUP = A_HEADS // A_KV_HEADS
B_WIDTH = MIX_WIDTH - A_WIDTH
B_V_DIM = HEAD_DIM
B_QK_DIM = HEAD_DIM // 2
B_HEADS = B_WIDTH // B_V_DIM

ROPE_THETA = 10000.0
NORM_EPS = 1e-6

A_Q_COLS = A_HEADS * HEAD_DIM
A_KV_COLS = A_KV_HEADS * HEAD_DIM
A_GATE_COLS = A_WIDTH
B_QK_COLS = B_HEADS * 2 * B_QK_DIM
B_V_COLS = B_HEADS * B_V_DIM
B_GATE_COLS = B_WIDTH
IN_PROJ_SIZES = (A_Q_COLS, A_KV_COLS, A_KV_COLS, A_GATE_COLS,
                 B_QK_COLS, B_QK_COLS, B_V_COLS, B_GATE_COLS)
IN_PROJ_WIDTH = sum(IN_PROJ_SIZES)
SPLIT_POINTS = tuple(int(c) for c in np.cumsum(IN_PROJ_SIZES)[:-1])

kernel_name = "hybrid_gqa_axial_diffattn_alibi_sandwich"


def rmsnorm(x, g):
    xf = x.astype(jnp.float32)
    r = lax.rsqrt(jnp.mean(xf * xf, axis=-1, keepdims=True) + NORM_EPS)
    return (xf * r * g.astype(jnp.float32)).astype(x.dtype)


def alibi_slopes(n_heads):
    return 2.0 ** (-8.0 * (jnp.arange(n_heads, dtype=jnp.float32) + 1.0) / n_heads)


def axial_angles(grid_pos):
    axis_dim = HEAD_DIM // 2
    inv_freq = ROPE_THETA ** (-jnp.arange(0, axis_dim, 2, dtype=jnp.float32) / axis_dim)
    return grid_pos.astype(jnp.float32)[:, None] * inv_freq[None, :]


def rope_half(x, ang):
    x1, x2 = jnp.split(x, 2, axis=-1)
    c, s = jnp.cos(ang), jnp.sin(ang)
    return jnp.concatenate([x1 * c - x2 * s, x2 * c + x1 * s], axis=-1)


def axial_rope(x, ang_row, ang_col):
    xf = x.astype(jnp.float32)
    xr, xc = jnp.split(xf, 2, axis=-1)
    return jnp.concatenate([rope_half(xr, ang_row), rope_half(xc, ang_col)], axis=-1).astype(x.dtype)


def sweep_query_blocks(attend, qs, qpos):
    meta_out = attend(tuple(q[..., :N_META, :] for q in qs), qpos[:N_META])
    n_real = qpos.shape[0] - N_META
    n_blk = n_real // Q_BLOCK

    def to_blocks(a):
        a = a[..., N_META:, :]
        a = a.reshape(a.shape[:-2] + (n_blk, Q_BLOCK, a.shape[-1]))
        return jnp.moveaxis(a, -3, 0)

    blk_q = tuple(to_blocks(q) for q in qs)
    blk_pos = qpos[N_META:].reshape(n_blk, Q_BLOCK)
    out = lax.map(lambda args: attend(args[0], args[1]), (blk_q, blk_pos))
    out = jnp.moveaxis(out, 0, -3)
    out = out.reshape(out.shape[:-3] + (n_real, out.shape[-1]))
    return jnp.concatenate([meta_out, out], axis=-2)


def heads_first(t, n_heads, d):
    b, l, _ = t.shape
    return t.reshape(b, l, n_heads, d).transpose(0, 2, 1, 3)


def gqa_axial_mixer(qa, ka, va, q_g, k_g, ang_row, ang_col, pos):
    b, l, _ = qa.shape
    q = axial_rope(rmsnorm(heads_first(qa, A_HEADS, HEAD_DIM), q_g), ang_row, ang_col)
    k = axial_rope(rmsnorm(heads_first(ka, A_KV_HEADS, HEAD_DIM), k_g), ang_row, ang_col)
    v = heads_first(va, A_KV_HEADS, HEAD_DIM)
    q = q.reshape(b, A_KV_HEADS, A_GROUP, l, HEAD_DIM)
    scale = 1.0 / math.sqrt(HEAD_DIM)

    def attend(qs, qpos):
        s = jnp.einsum('bkgqd,bksd->bkgqs', qs[0], k).astype(jnp.float32) * scale
        p = jax.nn.softmax(s, axis=-1)
        return jnp.einsum('bkgqs,bksd->bkgqd', p.astype(v.dtype), v)

    o = sweep_query_blocks(attend, (q,), pos)
    return o.transpose(0, 3, 1, 2, 4).reshape(b, l, A_WIDTH)


def diff_attn_mixer(qb, kb, vb, lq1, lk1, lq2, lk2, sub_g, lambda_init, pos):
    b, l, _ = qb.shape
    q = qb.reshape(b, l, B_HEADS, 2, B_QK_DIM).transpose(0, 2, 3, 1, 4)
    k = kb.reshape(b, l, B_HEADS, 2, B_QK_DIM).transpose(0, 2, 3, 1, 4)
    q1, q2 = q[:, :, 0], q[:, :, 1]
    k1, k2 = k[:, :, 0], k[:, :, 1]
    v = heads_first(vb, B_HEADS, B_V_DIM)
    lam = (jnp.exp(jnp.sum(lq1.astype(jnp.float32) * lk1.astype(jnp.float32)))
           - jnp.exp(jnp.sum(lq2.astype(jnp.float32) * lk2.astype(jnp.float32)))
           + lambda_init)
    slopes = alibi_slopes(B_HEADS)
    kpos = pos.astype(jnp.float32)
    scale = 1.0 / math.sqrt(B_QK_DIM)

    def attend(qs, qpos):
        dist = jnp.abs(qpos.astype(jnp.float32)[:, None] - kpos[None, :])
        bias = -slopes[:, None, None] * dist[None]
        s1 = jnp.einsum('bhqd,bhkd->bhqk', qs[0], k1).astype(jnp.float32) * scale + bias
        s2 = jnp.einsum('bhqd,bhkd->bhqk', qs[1], k2).astype(jnp.float32) * scale + bias
        a = jax.nn.softmax(s1, axis=-1) - lam * jax.nn.softmax(s2, axis=-1)
        return jnp.einsum('bhqk,bhkd->bhqd', a.astype(v.dtype), v)

    o = sweep_query_blocks(attend, (q1, q2), pos)
    o = rmsnorm(o, sub_g) * (1.0 - lambda_init)
    return o.transpose(0, 2, 1, 3).reshape(b, l, B_WIDTH)


def setup_inputs(seed: int = 0) -> dict:
    key = jax.random.key(seed)
    ks = jax.random.split(key, 13)
    f32 = jnp.float32
    return {
        "x": jax.random.normal(ks[0], (BATCH, SEQ, D_MODEL), f32),
        "meta_tokens": jax.random.normal(ks[1], (N_META, D_MODEL), f32),
        "pre_norm_g": 1.0 + 0.02 * jax.random.normal(ks[2], (DEPTH, D_MODEL), f32),
        "w_in": jax.random.normal(ks[3], (DEPTH, D_MODEL, IN_PROJ_WIDTH), f32) * D_MODEL ** -0.5,
        "q_norm_g": 1.0 + 0.02 * jax.random.normal(ks[4], (DEPTH, HEAD_DIM), f32),
        "k_norm_g": 1.0 + 0.02 * jax.random.normal(ks[5], (DEPTH, HEAD_DIM), f32),
        "lambda_q1": 0.1 * jax.random.normal(ks[6], (DEPTH, B_QK_DIM), f32),
        "lambda_k1": 0.1 * jax.random.normal(ks[7], (DEPTH, B_QK_DIM), f32),
        "lambda_q2": 0.1 * jax.random.normal(ks[8], (DEPTH, B_QK_DIM), f32),
        "lambda_k2": 0.1 * jax.random.normal(ks[9], (DEPTH, B_QK_DIM), f32),
        "subln_g": 1.0 + 0.02 * jax.random.normal(ks[10], (DEPTH, B_V_DIM), f32),
        "w_out": jax.random.normal(ks[11], (DEPTH, MIX_WIDTH, D_MODEL), f32) * MIX_WIDTH ** -0.5,
        "post_norm_g": 1.0 + 0.02 * jax.random.normal(ks[12], (DEPTH, D_MODEL), f32),
    }


def reference(x, meta_tokens, pre_norm_g, w_in, q_norm_g, k_norm_g, lambda_q1, lambda_k1,
              lambda_q2, lambda_k2, subln_g, w_out, post_norm_g):
    b, n_real, _ = x.shape
    meta = jnp.broadcast_to(meta_tokens.astype(x.dtype)[None], (b, N_META, D_MODEL))
    h = jnp.concatenate([meta, x], axis=1)
    seq_len = N_META + n_real
    pos = jnp.arange(seq_len, dtype=jnp.int32)

    rows = n_real // GRID_W
    grid_row = jnp.concatenate([jnp.zeros((N_META,), jnp.int32),
                                jnp.repeat(jnp.arange(rows, dtype=jnp.int32), GRID_W)])
    grid_col = jnp.concatenate([jnp.zeros((N_META,), jnp.int32),
                                jnp.tile(jnp.arange(GRID_W, dtype=jnp.int32), rows)])
    ang_row = axial_angles(grid_row)
    ang_col = axial_angles(grid_col)

    for layer in range(DEPTH):
        lambda_init = 0.8 - 0.6 * math.exp(-0.3 * layer)
        hn = rmsnorm(h, pre_norm_g[layer])
        proj = jnp.einsum('bld,dc->blc', hn, w_in[layer])
        qa, ka, va, ga, qb, kb, vb, gb = jnp.split(proj, SPLIT_POINTS, axis=-1)
        ya = gqa_axial_mixer(qa, ka, va, q_norm_g[layer], k_norm_g[layer],
                             ang_row, ang_col, pos) * jax.nn.silu(ga)
        yb = diff_attn_mixer(qb, kb, vb, lambda_q1[layer], lambda_k1[layer], lambda_q2[layer],
                             lambda_k2[layer], subln_g[layer], lambda_init, pos) * jax.nn.silu(gb)
        y = jnp.einsum('blc,cd->bld', jnp.concatenate([ya, yb], axis=-1), w_out[layer])
        h = h + rmsnorm(y, post_norm_g[layer])

    return h[:, N_META:, :]
```

```python
import math
import numpy as np
import concourse.bass as bass
import concourse.mybir as mybir
from concourse.bass_utils import run_bass_kernel_spmd

F32 = mybir.dt.float32
BF16 = mybir.dt.bfloat16
AF = mybir.ActivationFunctionType
ALU = mybir.AluOpType

NORM_EPS = 1e-6
ROPE_THETA = 10000.0
LAMBDA_INIT = 0.8 - 0.6 * math.exp(-0.3 * 0)


class Cfg:
    def __init__(self, D=2048, SEQ=2048):
        self.D, self.SEQ = D, SEQ
        self.NM = 16
        self.AW = D // 2
        self.AH = self.AW // 128
        self.AKV = 2
        self.AG = self.AH // self.AKV
        self.BW = D - self.AW
        self.BH = self.BW // 128
        self.DC = D // 128
        self.NT = SEQ // 128
        self.QC = min(512, SEQ)
        self.NQC = SEQ // self.QC
        self.QT = self.QC // 128
        self.L = SEQ + self.NM
        self.o_qa = 0
        self.o_ka = self.o_qa + self.AH * 128
        self.o_va = self.o_ka + self.AKV * 128
        self.o_ga = self.o_va + self.AKV * 128
        self.o_qb = self.o_ga + self.AW
        self.o_kb = self.o_qb + self.BH * 128
        self.o_vb = self.o_kb + self.BH * 128
        self.o_gb = self.o_vb + self.BH * 128
        self.INW = self.o_gb + self.BW
        self.chunks = []
        for hq in range(self.AH):
            if hq % self.AG == 0:
                g = hq // self.AG
                self.chunks.append(("akv", g))
            self.chunks.append(("aqg", hq))
        for h in range(self.BH):
            self.chunks.append(("bqk", h))
            self.chunks.append(("bvg", h))
        self.NCH = len(self.chunks)

    def chunk_cols(self, kind, i):
        r = np.arange(128)
        if kind == "akv":
            return np.concatenate([self.o_ka + 128 * i + r, self.o_va + 128 * i + r])
        if kind == "aqg":
            return np.concatenate([self.o_qa + 128 * i + r, self.o_ga + 128 * i + r])
        if kind == "bqk":
            return np.concatenate([self.o_qb + 128 * i + r, self.o_kb + 128 * i + r])
        if kind == "bvg":
            return np.concatenate([self.o_vb + 128 * i + r, self.o_gb + 128 * i + r])
        raise ValueError(kind)


class Buf:
    __slots__ = ("name", "wr", "rd", "sem", "cnt", "excl")

    def __init__(self, name):
        self.name = name
        self.excl = False
        self.wr = []
        self.rd = []
        self.sem = None
        self.cnt = 0


class Op:
    __slots__ = ("eng", "fn", "deps", "ticket", "needed", "dma", "idx")


ENGS = ("pe", "act", "dve", "pool", "sp")


class Prog:
    def __init__(self):
        self.ops = []
        self.dma_bufs = []

    def add(self, eng, fn, reads=(), writes=(), dma=None):
        op = Op()
        op.eng, op.fn, op.dma, op.needed, op.ticket = eng, fn, dma, False, None
        op.idx = len(self.ops)
        deps = []
        for b in reads:
            deps += b.wr
            if b.excl:
                deps += [r for r in b.rd if r.eng != eng]
        for b in writes:
            deps += b.wr
            deps += b.rd
        seen = set()
        op.deps = []
        for d in deps:
            if id(d) not in seen and d is not op:
                seen.add(id(d))
                if eng == "pe" and dma is None and d.eng == "pe" and d.dma is None:
                    continue
                op.deps.append(d)
        for d in op.deps:
            d.needed = True
        for b in reads:
            if dma is None:
                b.rd = [r for r in b.rd if not (r.dma is None and r.eng == eng)]
            b.rd.append(op)
        for b in writes:
            b.wr = [op]
            b.rd = []
        if dma is not None:
            if dma not in self.dma_bufs:
                self.dma_bufs.append(dma)
            dma.cnt += 16
            op.ticket = (dma, dma.cnt)
            op.needed = True
        self.ops.append(op)
        return op

    def barrier_bufs(self, bufs_all):
        pass

    def emit(self, nc, engines, eng_sems, dma_sems):
        cnt = {e: 0 for e in ENGS}
        for op in self.ops:
            if op.dma is None and op.needed:
                cnt[op.eng] += 1
                op.ticket = (op.eng, cnt[op.eng])

        def sem_of(t):
            return eng_sems[t[0]] if isinstance(t[0], str) else dma_sems[t[0].name]

        for e in ENGS:
            eng = engines[e]
            waited = {}
            for op in self.ops:
                if op.eng != e:
                    continue
                for d in op.deps:
                    if d.dma is None and d.eng == "pe" and e == "pe":
                        continue
                    key = d.ticket[0] if isinstance(d.ticket[0], str) else d.ticket[0].name
                    if waited.get(key, 0) >= d.ticket[1]:
                        continue
                    eng.wait_ge(sem_of(d.ticket), d.ticket[1])
                    waited[key] = d.ticket[1]
                inst = op.fn(eng)
                if op.dma is not None:
                    inst.then_inc(dma_sems[op.dma.name], 16)
                elif op.needed:
                    inst.then_inc(eng_sems[e], 1)
        return cnt


def build_nc(cfg: Cfg):
    c = cfg
    D, SEQ, DC, NT, QC, NQC, QT = c.D, c.SEQ, c.DC, c.NT, c.QC, c.NQC, c.QT
    NTT = NT + 1
    nc = bass.Bass("TRN2", target_bir_lowering=False)

    def din(name, shape, dt=F32):
        return nc.dram_tensor(name, list(shape), dt, kind="ExternalInput").ap()

    x_d = din("x", [SEQ, D])
    meta_d = din("meta", [c.NM, D])
    gpre_d = din("gpre", [128, DC])
    wch_d = din("wch", [c.NCH, 128, DC * 256])
    wout_d = din("wout", [128, DC * D])
    gvec_d = din("gvec", [128, 3 * 128])
    gpost_d = din("gpost", [128, D])
    lam_d = din("lamv", [128, 4 * 64])
    ident_d = din("ident", [128, 128])
    rope_d = din("rope", [NTT, 128, 256])
    ta_d = din("ta", [128, 2 * QC - 1])
    out_d = nc.dram_tensor("out", [SEQ, D], F32, kind="ExternalOutput").ap()

    P = Prog()
    from contextlib import ExitStack
    es = ExitStack()

    def sb(name, shape, dt):
        return es.enter_context(nc.sbuf_tensor(name, list(shape), dt))

    hn_raw = sb("hn_raw", [128, max(DC * c.L, DC * D)], BF16)
    yt_raw = sb("yt_raw", [128, max(DC * SEQ, 6 * D)], BF16)
    WS = 3
    wk_elems = max(WS * DC * 256 + (2 * SEQ + c.L + NTT * 129 + NT * 128), 2 * 2 * D * 2 + (2 * D if D > 512 else 0))
    wk_elems = (wk_elems + 63) // 64 * 64
    wk_raw = sb("wk_raw", [128, wk_elems], BF16)
    pT = [sb(f"pT{i}", [128, QC], BF16) for i in range(5)]
    osb = sb("osb", [128, 516], F32)
    ta_sb = sb("ta_sb", [128, 2 * QC - 1], F32)
    rope_sb = [sb(f"rope{i}", [128, 256], F32) for i in range(2)]
    ident = sb("ident_sb", [128, 128], BF16)
    gpre = sb("gpre_sb", [128, DC], F32)
    gvec = sb("gvec_sb", [128, 384], F32)
    sgs = sb("sgs", [128, 128], F32)
    lamv = sb("lamv_sb", [128, 256], F32)
    lamt = sb("lamt", [128, 8], F32)
    stat = sb("stat", [128, 3 * (NTT + 1)], F32)
    stat2 = sb("stat2", [128, 3 * (NTT + 1)], F32)
    NTMP = 4
    tA = [sb(f"tA{i}", [128, 128], F32) for i in range(NTMP)]
    tB = [sb(f"tB{i}", [128, 128], F32) for i in range(NTMP)]
    tC = [sb(f"tC{i}", [128, 128], F32) for i in range(NTMP)]
    tN = [sb(f"tN{i}", [128, 128], BF16) for i in range(NTMP)]
    tS = [sb(f"tS{i}", [128, 4], F32) for i in range(NTMP)]
    tY = [sb(f"tY{i}", [128, 128], BF16) for i in range(4)]
    tD = [sb(f"tD{i}", [128, 128], F32) for i in range(NTMP)]
    eA = [sb(f"eA{i}", [128, 128], F32) for i in range(4)]
    eS = [sb(f"eS{i}", [128, 4], F32) for i in range(4)]
    junk = sb("junk", [128, 128], BF16)
    o1buf = sb("o1buf", [128, QT * 128], F32)
    tE = [sb(f"tE{i}", [128, 4], F32) for i in range(4)]
    psA = es.enter_context(nc.psum_tensor("psA", [128, 2048], F32))
    psB = es.enter_context(nc.psum_tensor("psB", [128, 2048], F32))

    def bank(i):
        t = psA if i < 4 else psB
        return t[:, 512 * (i % 4):512 * (i % 4) + 512]

    hnT = hn_raw[:, 0:DC * c.L].rearrange("p (c t) -> p c t", c=DC)
    YT = yt_raw[:, 0:DC * SEQ].rearrange("p (c t) -> p c t", c=DC)
    wslot = [wk_raw[:, i * DC * 256:(i + 1) * DC * 256].rearrange("p (c n) -> p c n", c=DC) for i in range(WS)]
    o = WS * DC * 256
    qT = wk_raw[:, o:o + SEQ]; o += SEQ
    qT2 = wk_raw[:, o:o + SEQ]; o += SEQ
    kT = wk_raw[:, o:o + c.L]; o += c.L
    vU = wk_raw[:, o:o + NTT * 129].rearrange("p (t n) -> p t n", n=129); o += NTT * 129
    gate = wk_raw[:, o:o + NT * 128].rearrange("p (t n) -> p t n", n=128); o += NT * 128
    vAv = vU
    kTA = kT
    xs = [yt_raw[:, i * 2 * D:(i + 1) * 2 * D].bitcast(F32) for i in range(2)]
    xn = [yt_raw[:, 4 * D + i * D:4 * D + (i + 1) * D] for i in range(2)]
    xr = [wk_raw[:, i * 2 * D:(i + 1) * 2 * D].bitcast(F32) for i in range(2)]
    ot = [wk_raw[:, 4 * D + i * 2 * D:4 * D + (i + 1) * 2 * D].bitcast(F32) for i in range(2)]
    gpost = None
    wout = hn_raw[:, 0:DC * D].rearrange("p (c n) -> p c n", c=DC)

    def tok(t):
        if t < NT:
            return slice(128 * t, 128 * t + 128), 128
        return slice(SEQ, SEQ + c.NM), c.NM

    B = {}

    def buf(name):
        if name not in B:
            B[name] = Buf(name)
        return B[name]

    b_bank = [buf(f"bank{i}") for i in range(8)]
    for b_ in b_bank:
        b_.excl = True
    b_hn = [buf(f"hn{t}") for t in range(NTT)]
    b_yt = [buf(f"yt{t}") for t in range(NT)]
    b_w = [buf(f"w{i}") for i in range(WS)]
    b_const = buf("const")

    def dma(eng, out, in_, buf_, reads=(), writes=()):
        return P.add(eng, lambda e, out=out, in_=in_: e.dma_start(out=out, in_=in_), reads=reads, writes=writes, dma=buf_)

    b_c = [buf(f"c{i}") for i in range(6)]
    dma("sp", gpre[:], gpre_d, b_c[0], writes=[b_c[0]])
    dma("sp", gvec[:], gvec_d, b_c[1], writes=[b_c[1]])
    dma("sp", lamv[:], lam_d, b_c[2], writes=[b_c[2]])
    dma("sp", ta_sb[:], ta_d, b_c[3], writes=[b_c[3]])
    dma("pool", ident[:], ident_d, b_c[4], writes=[b_c[4]])
    b_gpre, b_gvec, b_lamv, b_ta, b_ident = b_c[0], b_c[1], b_c[2], b_c[3], b_c[4]

    b_vU = buf("vU"); b_vA = b_vU; b_lamt = buf("lamt"); b_sgs = buf("sgs")
    P.add("pool", lambda e: e.memset(vAv[:, :, 128:129], 1.0), writes=[b_vA])
    lv = lamv[:].rearrange("p (a n) -> p a n", a=4)
    P.add("dve", lambda e: e.tensor_tensor(out=tA[0][:, 0:64], in0=lv[:, 0, :], in1=lv[:, 1, :], op=ALU.mult), reads=[b_lamv], writes=[buf("tA0")])
    P.add("dve", lambda e: e.tensor_tensor(out=tA[0][:, 64:128], in0=lv[:, 2, :], in1=lv[:, 3, :], op=ALU.mult), reads=[b_lamv, buf("tA0")], writes=[buf("tA0")])
    P.add("dve", lambda e: e.tensor_reduce(out=lamt[:, 0:2], in_=tA[0][:].rearrange("p (a n) -> p a n", a=2), axis=mybir.AxisListType.X, op=ALU.add), reads=[buf("tA0")], writes=[b_lamt])
    P.add("act", lambda e: e.activation(out=lamt[:, 2:4], in_=lamt[:, 0:2], func=AF.Exp), reads=[b_lamt], writes=[b_lamt])
    P.add("dve", lambda e: e.tensor_tensor(out=lamt[:, 4:5], in0=lamt[:, 3:4], in1=lamt[:, 2:3], op=ALU.subtract), reads=[b_lamt], writes=[b_lamt])
    P.add("dve", lambda e: e.tensor_scalar(out=lamt[:, 5:6], in0=lamt[:, 4:5], scalar1=-LAMBDA_INIT, scalar2=None, op0=ALU.add), reads=[b_lamt], writes=[b_lamt])
    neglam = lamt[:, 5:6]
    P.add("dve", lambda e: e.tensor_scalar(out=sgs[:], in0=gvec[:, 256:384], scalar1=(1.0 - LAMBDA_INIT), scalar2=None, op0=ALU.mult), reads=[b_gvec], writes=[b_sgs])

    b_xs = [buf(f"xs{i}") for i in range(2)]
    b_xn = [buf(f"xn{i}") for i in range(2)]
    b_stat = buf("stat")
    b_ph0 = buf("ph0")
    HB = max(1, DC // 8)
    for t in range(NTT):
        sl, rows = tok(t)
        s = t % 2
        src = x_d[128 * t:128 * t + 128, :] if t < NT else meta_d
        dma("sp", xs[s][0:rows, :], src, b_xs[s], writes=[b_xs[s]])
        ss = stat[0:rows, 3 * t:3 * t + 1]; ln_ = stat[0:rows, 3 * t + 1:3 * t + 2]; rs = stat[0:rows, 3 * t + 2:3 * t + 3]
        P.add("act", lambda e, s=s, rows=rows, ss=ss: e.activation(out=xn[s][0:rows, :], in_=xs[s][0:rows, :], func=AF.Square, accum_out=ss),
              reads=[b_xs[s], b_ph0], writes=[b_xn[s], b_stat])
        P.add("act", lambda e, ss=ss, ln_=ln_: e.activation(out=ln_, in_=ss, func=AF.Ln, scale=1.0 / D, bias=NORM_EPS), reads=[b_stat], writes=[b_stat])
        P.add("act", lambda e, rs=rs, ln_=ln_: e.activation(out=rs, in_=ln_, func=AF.Exp, scale=-0.5), reads=[b_stat], writes=[b_stat])
        P.add("dve", lambda e, s=s, rows=rows, rs=rs: e.tensor_scalar(out=xn[s][0:rows, :], in0=xs[s][0:rows, :], scalar1=rs, scalar2=None, op0=ALU.mult),
              reads=[b_xs[s], b_stat, b_ph0], writes=[b_xn[s]])
        for hb in range(HB):
            bk = (2 * (t % 2) + hb) % 8 if HB <= 2 else hb % 8
            bkv = bank(bk).bitcast(BF16)
            nchb = min(8, DC - 8 * hb)
            for cc in range(nchb):
                ch = 8 * hb + cc
                P.add("pe", lambda e, bkv=bkv, cc=cc, ch=ch, s=s, rows=rows: e.transpose(bkv[:, cc * 128:cc * 128 + rows], xn[s][0:rows, ch * 128:(ch + 1) * 128], ident[0:rows, 0:rows]),
                      reads=[b_xn[s], b_ident, b_ph0], writes=[b_bank[bk]])
            srcv = bkv[:, 0:nchb * 128].rearrange("p (c r) -> p c r", c=nchb)[:, :, 0:rows]
            gv = gpre[:, 8 * hb:8 * hb + nchb].unsqueeze(2).to_broadcast([128, nchb, rows])
            P.add("dve", lambda e, srcv=srcv, gv=gv, hb=hb, nchb=nchb, sl=sl: e.tensor_tensor(out=hnT[:, 8 * hb:8 * hb + nchb, sl], in0=srcv, in1=gv, op=ALU.mult),
                  reads=[b_bank[bk], b_gpre], writes=[b_hn[t]])

    MARK = {}
    MARK['ph0'] = len(P.ops)
    wstate = {"next": 0}
    b_rope = [buf(f"rope{i}") for i in range(2)]
    rope_ctr = {"n": 0}
    tmp_ctr = {"n": 0}
    tn_ctr = {"n": 0}
    ty_ctr = {"n": 0}
    b_tY = [buf(f"tY{i}") for i in range(4)]
    b_tA = [buf(f"tA{i}") for i in range(NTMP)]; b_tB = [buf(f"tB{i}") for i in range(NTMP)]
    b_tC = [buf(f"tC{i}") for i in range(NTMP)]; b_tN = [buf(f"tN{i}") for i in range(NTMP)]
    b_tS = [buf(f"tS{i}") for i in range(NTMP)]
    b_junk = buf("junk")
    proj_banks = [0, 1, 2, 3, 4]
    pb_ctr = {"n": 0}
    tr_ctr = {"n": 0}
    b_qT = buf("qT"); b_qT2 = buf("qT2"); b_kT = buf("kT"); b_kTA = b_kT; b_gate = buf("gate")

    pend = []
    tk = {"n": 0, "seq": 0}

    def defer(fn, delay, keys=()):
        tk["seq"] += 1
        pend.append((tk["n"] + delay, tk["seq"], fn, tuple(keys)))
        pend.sort(key=lambda p: (p[0], p[1]))

    def acquire(key):
        while any(key in p[3] for p in pend):
            pend.pop(0)[2]()

    def tick():
        tk["n"] += 1
        while pend and pend[0][0] <= tk["n"]:
            pend.pop(0)[2]()

    def flush_all():
        while pend:
            pend.pop(0)[2]()

    def load_chunk(ci):
        s = ci % WS
        dst = wslot[s].rearrange("p c n -> p (c n)").rearrange("p (a b) -> p a b", b=512)
        srcv = wch_d[ci].rearrange("p (a b) -> p a b", b=512)
        dma("pool", dst, srcv, b_w[s], writes=[b_w[s]])

    PREFETCH = WS - 1
    for ci in range(min(PREFETCH, c.NCH)):
        load_chunk(ci)
    wstate["loaded"] = min(PREFETCH, c.NCH)

    def next_chunk():
        ci = wstate["next"]
        wstate["next"] += 1
        return ci % WS

    def prefetch_more():
        if wstate["loaded"] < c.NCH:
            load_chunk(wstate["loaded"])
            wstate["loaded"] += 1

    def proj_token_major(ws, t, ncols=256):
        sl, rows = tok(t)
        bk = proj_banks[pb_ctr["n"] % len(proj_banks)]; pb_ctr["n"] += 1
        for ch in range(DC):
            P.add("pe", lambda e, bk=bk, ch=ch, ws=ws, sl=sl, rows=rows: e.matmul(bank(bk)[0:rows, 0:ncols], hnT[:, ch, sl], wslot[ws][:, ch, 0:ncols], start=(ch == 0), stop=(ch == DC - 1)),
                  reads=[b_hn[t], b_w[ws]], writes=[b_bank[bk]])
        return bk

    def y_transpose(ybf, b_y, mix_chunk, tq, on_dve=False):
        k = tr_ctr["n"] % 4; tr_ctr["n"] += 1
        tv = bank(7).bitcast(BF16)[:, k * 256:k * 256 + 128]
        P.add("pe", lambda e: e.transpose(tv, ybf, ident[:, :]), reads=[b_y, b_ident], writes=[b_bank[7]])
        first = not ytr_ctr.get("started")
        ytr_ctr["started"] = True
        wr = [b_yt[tq]] + ([b_ph0, b_xs[0], b_xs[1], b_xn[0], b_xn[1]] if first else [])
        if on_dve:
            P.add("dve", lambda e: e.tensor_copy(out=YT[:, mix_chunk, 128 * tq:128 * tq + 128], in_=tv), reads=[b_bank[7]], writes=wr)
        else:
            P.add("act", lambda e: e.activation(out=YT[:, mix_chunk, 128 * tq:128 * tq + 128], in_=tv, func=AF.Copy), reads=[b_bank[7]], writes=wr)

    qk_ctr = {"n": 0}
    b_tD = [buf(f"tD{i}") for i in range(NTMP)]

    def qk_tile(bk, rows, t, gcol, dstT, b_dst, sl, other):
        i = qk_ctr["n"] % NTMP; qk_ctr["n"] += 1
        key = ("qk", i)
        acquire(key)
        if other == "gate":
            acquire(("gate",))
        acquire(("sg", i))
        r = rope_ctr["n"] % 2; rope_ctr["n"] += 1
        srcp = bank(bk)[0:rows, 0:128]
        osrc = bank(bk)[0:rows, 128:256]
        dma("sp", rope_sb[r][0:rows, :], rope_d[t, 0:rows, :], b_rope[r], writes=[b_rope[r]])
        ss = tS[i][0:rows, 0:1]; ln_ = tS[i][0:rows, 1:2]; rs = tS[i][0:rows, 2:3]
        xg = tA[i][0:rows, :]
        t1 = tB[i][0:rows, :]
        t2 = tD[i][0:rows, :]
        xb = tN[i][0:rows, :]

        def st2():
            if other == "gate":
                P.add("act", lambda e: e.activation(out=tC[i][:], in_=osrc, func=AF.Exp, scale=-1.0), reads=[b_bank[bk]], writes=[b_tC[i]])
            else:
                _, vview, b_v = other
                P.add("act", lambda e: e.activation(out=vview[0:rows, t, 0:128], in_=osrc, func=AF.Copy), reads=[b_bank[bk]], writes=[b_v])
            P.add("act", lambda e: e.activation(out=junk[0:rows, :], in_=srcp, func=AF.Square, accum_out=ss), reads=[b_bank[bk]], writes=[b_junk, b_tS[i]])
            P.add("act", lambda e: e.activation(out=ln_, in_=ss, func=AF.Ln, scale=1.0 / 128, bias=NORM_EPS), reads=[b_tS[i]], writes=[b_tS[i]])
            P.add("act", lambda e: e.activation(out=rs, in_=ln_, func=AF.Exp, scale=-0.5), reads=[b_tS[i]], writes=[b_tS[i]])
            P.add("dve", lambda e: e.tensor_tensor(out=xg, in0=srcp, in1=gvec[0:rows, gcol:gcol + 128], op=ALU.mult), reads=[b_bank[bk], b_gvec], writes=[b_tA[i]])

        def st3():
            if other == "gate":
                P.add("act", lambda e: e.activation(out=tC[i][:], in_=tC[i][:], func=AF.Ln, bias=1.0), reads=[b_tC[i]], writes=[b_tC[i]])
                P.add("act", lambda e: e.activation(out=tC[i][:], in_=tC[i][:], func=AF.Exp, scale=-1.0), reads=[b_tC[i]], writes=[b_tC[i]])
            P.add("dve", lambda e: e.tensor_tensor(out=t1, in0=xg, in1=rope_sb[r][0:rows, 0:128], op=ALU.mult), reads=[b_tA[i], b_rope[r]], writes=[b_tB[i]])
            xsw = xg.rearrange("p (a s j) -> p a s j", a=2, s=2)[:, :, ::-1, :]
            P.add("pool", lambda e: e.tensor_tensor(out=t2.rearrange("p (a s j) -> p a s j", a=2, s=2), in0=xsw, in1=rope_sb[r][0:rows, 128:256].rearrange("p (a s j) -> p a s j", a=2, s=2), op=ALU.mult),
                  reads=[b_tA[i], b_rope[r]], writes=[b_tD[i]])

        def st4():
            if other == "gate":
                P.add("dve", lambda e: e.tensor_tensor(out=gate[:, t, :], in0=osrc, in1=tC[i][:], op=ALU.mult), reads=[b_bank[bk], b_tC[i]], writes=[b_gate])
            P.add("pool", lambda e: e.tensor_tensor(out=t1, in0=t1, in1=t2, op=ALU.add), reads=[b_tB[i], b_tD[i]], writes=[b_tB[i]])
            P.add("dve", lambda e: e.tensor_scalar(out=xb, in0=t1, scalar1=rs, scalar2=None, op0=ALU.mult), reads=[b_tB[i], b_tS[i]], writes=[b_tN[i]])

        def st5():
            k = tr_ctr["n"] % 4; tr_ctr["n"] += 1
            tv = bank(7).bitcast(BF16)[:, k * 256:k * 256 + rows]
            P.add("pe", lambda e: e.transpose(tv, xb, ident[0:rows, 0:rows]), reads=[b_tN[i], b_ident], writes=[b_bank[7]])
            P.add("dve", lambda e: e.tensor_copy(out=dstT[:, sl], in_=tv), reads=[b_bank[7]], writes=[b_dst])

        st2()
        defer(st3, 1, [key])
        defer(st4, 2, [key])
        defer(st5, 3, [key])
        tick()

    sg_ctr = {"n": 0}

    def silu_gate(bk, col0, t):
        i = sg_ctr["n"] % NTMP; sg_ctr["n"] += 1
        key = ("sg", i)
        acquire(key)
        acquire(("qk", i))
        acquire(("gate",))
        gsrc = bank(bk)[:, col0:col0 + 128]
        P.add("act", lambda e: e.activation(out=tC[i][:], in_=gsrc, func=AF.Exp, scale=-1.0), reads=[b_bank[bk]], writes=[b_tC[i]])

        def st():
            P.add("act", lambda e: e.activation(out=tC[i][:], in_=tC[i][:], func=AF.Ln, bias=1.0), reads=[b_tC[i]], writes=[b_tC[i]])
            P.add("act", lambda e: e.activation(out=tC[i][:], in_=tC[i][:], func=AF.Exp, scale=-1.0), reads=[b_tC[i]], writes=[b_tC[i]])

        def st_b():
            P.add("dve", lambda e: e.tensor_tensor(out=gate[:, t, :], in0=gsrc, in1=tC[i][:], op=ALU.mult), reads=[b_bank[bk], b_tC[i]], writes=[b_gate])
        defer(st, 1, [key])
        defer(st_b, 2, [key])

    SCALE_A = 1.0 / math.sqrt(128.0)
    NPT = len(pT)
    b_pT = [buf(f"pT{i}") for i in range(NPT)]
    b_o1s = [buf(f"o1_{j}") for j in range(QT)]
    b_osbs = [buf("osb0"), buf("osb1")]
    e_ctr = {"n": 0}
    b_eA = [buf(f"eA{i}") for i in range(4)]
    b_eS = [buf(f"eS{i}") for i in range(4)]
    b_tE = [buf(f"tE{i}") for i in range(4)]
    te_ctr = {"n": 0}
    ytr_ctr = {}
    SBANKS = [0, 1, 2, 3, 4]
    OBANKS = [5, 6]

    def attention(kq_aps, b_k, b_qs, v_ap, b_v, tiles_fn, epilogue, dve_heavy=False):
        nm = len(kq_aps)
        seq = [(cq, m, kt) for cq in range(NQC) for m in range(nm) for kt in range(NTT)]
        LOOK = len(SBANKS) - 1
        issued = {}

        def issue_S(n):
            cq, m, kt = seq[n]
            sl, rows = tok(kt)
            bk = SBANKS[n % len(SBANKS)]
            kT_ap, qT_ap = kq_aps[m]
            P.add("pe", lambda e: e.matmul(bank(bk)[0:rows, 0:QC], kT_ap[:, sl], qT_ap[:, cq * QC:(cq + 1) * QC], start=True, stop=True),
                  reads=[b_k, b_qs[m]], writes=[b_bank[bk]])
            issued[n] = bk

        LA = 2

        def tile_ops(n):
            cq, m, kt = seq[n]
            sl, rows = tok(kt)
            pi = n % NPT
            tiles_fn(m, issued[n], rows, cq, kt, pT[pi][0:rows, :], b_pT[pi])

        for n in range(min(LOOK, len(seq))):
            issue_S(n)
        for n in range(min(LA, len(seq))):
            tile_ops(n)
        for n, (cq, m, kt) in enumerate(seq):
            sl, rows = tok(kt)
            pi = n % NPT
            if n + LA < len(seq):
                tile_ops(n + LA)
            for j in range(QT):
                ob = OBANKS[j // 2]
                oc = (j % 2) * 129
                first_in_bank = (kt == 0 and (j % 2 == 0))
                P.add("pe", lambda e, ob=ob, oc=oc, j=j, rows=rows, kt=kt, pi=pi, fib=first_in_bank: e.matmul(
                    bank(ob)[:, oc:oc + 129], pT[pi][0:rows, j * 128:(j + 1) * 128], v_ap[0:rows, kt, :],
                    start=fib, stop=(kt == NTT - 1), skip_group_check=True),
                    reads=[b_pT[pi], b_v], writes=[b_bank[ob]])
            if n + LOOK < len(seq):
                issue_S(n + LOOK)
            if kt == NTT - 1:
                acquire(("osb",))
                nob = (QT + 1) // 2
                for bi in range(nob):
                    w = 129 * min(2, QT - 2 * bi)
                    if bi == 0:
                        P.add("act", lambda e, bi=bi, w=w: e.activation(out=osb[:, bi * 258:bi * 258 + w], in_=bank(OBANKS[bi])[:, 0:w], func=AF.Copy),
                              reads=[b_bank[OBANKS[bi]]], writes=[b_osbs[bi]])
                    elif dve_heavy:
                        P.add("act", lambda e, bi=bi, w=w: e.activation(out=osb[:, bi * 258:bi * 258 + w], in_=bank(OBANKS[bi])[:, 0:w], func=AF.Copy),
                              reads=[b_bank[OBANKS[bi]]], writes=[b_osbs[bi]])
                    else:
                        P.add("dve", lambda e, bi=bi, w=w: e.tensor_copy(out=osb[:, bi * 258:bi * 258 + w], in_=bank(OBANKS[bi])[:, 0:w]),
                              reads=[b_bank[OBANKS[bi]]], writes=[b_osbs[bi]])
                for j in range(QT):
                    epilogue(m, cq, j, osb[:, j * 129:(j + 1) * 129], b_osbs[j // 2])
            tick()

    for hq in range(c.AH):
        g = hq // c.AG
        if hq % c.AG == 0:
            ws = next_chunk()
            for t in range(NTT):
                sl, rows = tok(t)
                bk = proj_token_major(ws, t)
                qk_tile(bk, rows, t, 128, kTA, b_kTA, sl, ("v", vAv, b_vA))
            prefetch_more()
        ws = next_chunk()
        for t in range(NT):
            sl, rows = tok(t)
            bk = proj_token_major(ws, t)
            qk_tile(bk, rows, t, 0, qT, b_qT, sl, "gate")
        prefetch_more()
        flush_all()

        def tiles_A(m, bk, rows, cq, kt, pdst, b_p):
            P.add("act", lambda e: e.activation(out=pdst, in_=bank(bk)[0:rows, 0:QC], func=AF.Exp, scale=SCALE_A), reads=[b_bank[bk]], writes=[b_p])

        def epi_A(m, cq, j, ops, b_ops, hq=hq):
            tq = cq * QT + j
            i = e_ctr["n"] % 4; e_ctr["n"] += 1
            key = ("e", i)
            acquire(key)

            def e2():
                P.add("dve", lambda e: e.reciprocal(out=tE[i][:, 0:1], in_=ops[:, 128:129]), reads=[b_ops], writes=[b_tE[i]])
                P.add("dve", lambda e: e.scalar_tensor_tensor(out=tY[i][:], in0=ops[:, 0:128], scalar=tE[i][:, 0:1], in1=gate[:, tq, :], op0=ALU.mult, op1=ALU.mult),
                      reads=[b_ops, b_tE[i], b_gate], writes=[b_tY[i]])
            defer(e2, 1 + j, [key, ("osb",), ("gate",)])
            defer(lambda: y_transpose(tY[i][:], b_tY[i], hq, tq, on_dve=True), 3 + j, [key])

        attention([(kTA, qT)], b_kTA, [b_qT], vAv, b_vA, tiles_A, epi_A)

    MARK['A'] = len(P.ops)
    WO = {}

    def load_wout():
        b_wout = buf("wout")
        NWD = 4 if DC >= 4 else 1
        cpd = DC // NWD
        b_wo = [buf(f"wo{i}") for i in range(NWD)]
        WO['b_wo'] = b_wo; WO['cpd'] = cpd
        for i in range(NWD):
            dstv = wout[:, i * cpd:(i + 1) * cpd, :].rearrange("p c n -> p (c n)").rearrange("p (a b) -> p a b", b=512)
            srcv = wout_d[:, i * cpd * D:(i + 1) * cpd * D].rearrange("p (a b) -> p a b", b=512)
            dma("pool", dstv, srcv, b_wo[i], reads=[], writes=[b_wo[i]] + b_hn)

    P.add("pool", lambda e: e.memset(qT[64:128, :], 0.0), writes=[b_qT])
    P.add("pool", lambda e: e.memset(qT2[0:64, :], 0.0), writes=[b_qT2])
    YW = 2 * QC - 1
    Y0 = QC - 128
    WL = Y0 + 127
    for h in range(c.BH):
        slope = 2.0 ** (-8.0 * (h + 1) / c.BH)
        ws = next_chunk()
        ncht = NQC
        for cq in range(NQC):
            bk = proj_banks[pb_ctr["n"] % len(proj_banks)]; pb_ctr["n"] += 1
            for ch in range(DC):
                P.add("pe", lambda e, bk=bk, ch=ch, ws=ws, cq=cq: e.matmul(bank(bk)[:, 0:QC], wslot[ws][:, ch, 0:128], hnT[:, ch, cq * QC:(cq + 1) * QC], start=(ch == 0), stop=(ch == DC - 1)),
                      reads=[b_hn[tt] for tt in range(cq * QT, (cq + 1) * QT)] + [b_w[ws]], writes=[b_bank[bk]])
            P.add("act", lambda e, bk=bk, cq=cq: e.activation(out=qT[0:64, cq * QC:(cq + 1) * QC], in_=bank(bk)[0:64, 0:QC], func=AF.Copy, scale=0.125), reads=[b_bank[bk]], writes=[b_qT])
            P.add("act", lambda e, bk=bk, cq=cq: e.activation(out=qT2[64:128, cq * QC:(cq + 1) * QC], in_=bank(bk)[64:128, 0:QC], func=AF.Copy, scale=0.125), reads=[b_bank[bk]], writes=[b_qT2])
            tick()
        for cq in range(NQC + 1):
            bk = proj_banks[pb_ctr["n"] % len(proj_banks)]; pb_ctr["n"] += 1
            csl = slice(cq * QC, (cq + 1) * QC) if cq < NQC else slice(SEQ, SEQ + c.NM)
            n = QC if cq < NQC else c.NM
            rd = [b_hn[tt] for tt in range(cq * QT, (cq + 1) * QT)] if cq < NQC else [b_hn[NT]]
            for ch in range(DC):
                P.add("pe", lambda e, bk=bk, ch=ch, ws=ws, csl=csl, n=n: e.matmul(bank(bk)[:, 0:n], wslot[ws][:, ch, 128:256], hnT[:, ch, csl], start=(ch == 0), stop=(ch == DC - 1)),
                      reads=rd + [b_w[ws]], writes=[b_bank[bk]])
            P.add("dve", lambda e, bk=bk, csl=csl, n=n: e.tensor_copy(out=kT[:, csl], in_=bank(bk)[:, 0:n]), reads=[b_bank[bk]], writes=[b_kT])
            tick()
        prefetch_more()
        ws = next_chunk()
        for t in range(NTT):
            sl, rows = tok(t)
            ncols = 256 if t < NT else 128
            bk = proj_token_major(ws, t, ncols)
            P.add("act", lambda e, bk=bk, rows=rows, t=t: e.activation(out=vU[0:rows, t, 0:128], in_=bank(bk)[0:rows, 0:128], func=AF.Copy), reads=[b_bank[bk]], writes=[b_vU])
            if t < NT:
                silu_gate(bk, 128, t)
            tick()
        prefetch_more()
        flush_all()
        if h == c.BH - 1:
            load_wout()

        def tiles_B(m, bk, rows, cq, kt, pdst, b_p, slope=slope):
            if kt < NT:
                delta = QC * cq - 128 * kt
                if -QC < delta < 128:
                    j = (-delta) // 128
                    ws_ = Y0 - 128 * j
                    coef, cst = -slope, 0.0
                elif delta >= 128:
                    ws_ = WL
                    coef, cst = -slope, -slope * (delta - 127)
                else:
                    ws_ = WL
                    coef, cst = slope, slope * (delta - 127)
            else:
                ws_ = WL
                delta = QC * cq + c.NM
                coef, cst = -slope, -slope * (delta - 127)
            sv = bank(bk)[0:rows, 0:QC]
            P.add("dve", lambda e: e.scalar_tensor_tensor(out=sv, in0=ta_sb[0:rows, ws_:ws_ + QC], scalar=coef, in1=sv, op0=ALU.mult, op1=ALU.add),
                  reads=[b_bank[bk], b_ta], writes=[b_bank[bk]])
            P.add("act", lambda e: e.activation(out=pdst, in_=sv, func=AF.Exp, bias=float(cst)), reads=[b_bank[bk]], writes=[b_p])

        def epi_B(m, cq, j, ops, b_ops, h=h):
            tq = cq * QT + j
            i = e_ctr["n"] % 4; e_ctr["n"] += 1
            key = ("e", i)
            acquire(key)
            od = eA[i][:]
            ss = eS[i][:, 0:1]; ln_ = eS[i][:, 1:2]; rs = eS[i][:, 2:3]

            def e2():
                P.add("dve", lambda e: e.reciprocal(out=tE[i][:, 0:1], in_=ops[:, 128:129]), reads=[b_ops], writes=[b_tE[i]])
                if m == 0:
                    P.add("pool", lambda e: e.tensor_scalar(out=o1buf[:, j * 128:(j + 1) * 128], in0=ops[:, 0:128], scalar1=tE[i][:, 0:1], scalar2=1.0, op0=ALU.mult, op1=ALU.mult),
                          reads=[b_ops, b_tE[i]], writes=[b_o1s[j]])
                    return
                P.add("dve", lambda e: e.tensor_tensor(out=tE[i][:, 1:2], in0=tE[i][:, 0:1], in1=neglam, op=ALU.mult), reads=[b_tE[i], b_lamt], writes=[b_tE[i]])
                P.add("pool", lambda e: e.tensor_scalar(out=od, in0=ops[:, 0:128], scalar1=tE[i][:, 1:2], scalar2=1.0, op0=ALU.mult, op1=ALU.mult),
                      reads=[b_ops, b_tE[i]], writes=[b_eA[i]])
                P.add("pool", lambda e: e.tensor_tensor(out=od, in0=od, in1=o1buf[:, j * 128:(j + 1) * 128], op=ALU.add),
                      reads=[b_eA[i], b_o1s[j]], writes=[b_eA[i]])

            def e3():
                P.add("act", lambda e: e.activation(out=junk[:], in_=od, func=AF.Square, accum_out=ss), reads=[b_eA[i]], writes=[b_junk, b_eS[i]])
                P.add("act", lambda e: e.activation(out=ln_, in_=ss, func=AF.Ln, scale=1.0 / 128, bias=NORM_EPS), reads=[b_eS[i]], writes=[b_eS[i]])
                P.add("act", lambda e: e.activation(out=rs, in_=ln_, func=AF.Exp, scale=-0.5), reads=[b_eS[i]], writes=[b_eS[i]])

            def e4():
                P.add("pool", lambda e: e.tensor_tensor(out=od, in0=od, in1=sgs[:], op=ALU.mult), reads=[b_eA[i], b_sgs], writes=[b_eA[i]])
                P.add("pool", lambda e: e.tensor_scalar(out=od, in0=od, scalar1=rs, scalar2=1.0, op0=ALU.mult, op1=ALU.mult),
                      reads=[b_eA[i], b_eS[i]], writes=[b_eA[i]])
                P.add("pool", lambda e: e.tensor_tensor(out=tY[i][:], in0=od, in1=gate[:, tq, :], op=ALU.mult),
                      reads=[b_eA[i], b_gate], writes=[b_tY[i]])

            defer(e2, 1 + j, [key, ("osb",)])
            if m == 1:
                defer(e3, 4 + j, [key])
                defer(e4, 7 + j, [key, ("gate",)])
                defer(lambda: y_transpose(tY[i][:], b_tY[i], c.AH + h, tq, on_dve=True), 10 + j, [key])

        attention([(kT, qT), (kT, qT2)], b_kT, [b_qT, b_qT2], vU, b_vU, tiles_B, epi_B, dve_heavy=False)

    flush_all()
    MARK['B'] = len(P.ops)
    gpost_sb = None
    b_xr = [buf(f"xr{i}") for i in range(2)]
    b_ot = [buf(f"ot{i}") for i in range(2)]
    b_unit = [b_w[i] for i in range(WS)] + [b_qT, b_qT2, b_kT, b_vU, b_gate]
    b_gpost = buf("gpost")
    gpost_sb = sb("gpost_sb", [128, D], F32) if D <= 512 else None
    if gpost_sb is None:
        off = 8 * D
        assert wk_elems >= off + 2 * D, (wk_elems, off + 2 * D)
        gpost_v = wk_raw[:, off:off + 2 * D].bitcast(F32)
    else:
        gpost_v = gpost_sb[:]
    dma("sp", gpost_v, gpost_d, b_gpost, writes=[b_gpost] + b_unit)
    NB = D // 512 if D >= 512 else 1
    BW_ = min(512, D)
    for t in range(NT):
        s = t % 2
        sl, rows = tok(t)
        dma("sp", xr[s][:], x_d[128 * t:128 * t + 128, :], b_xr[s], writes=[b_xr[s]] + (b_unit if t < 2 else []))
        base = 4 * (t % 2)
        if NB * 1 > 4:
            raise NotImplementedError
        for nb in range(NB):
            bk = base + nb
            for ch in range(DC):
                P.add("pe", lambda e, bk=bk, ch=ch, nb=nb, sl=sl: e.matmul(bank(bk)[:, 0:BW_], YT[:, ch, sl], wout[:, ch, nb * BW_:(nb + 1) * BW_], start=(ch == 0), stop=(ch == DC - 1)),
                      reads=[b_yt[t], WO['b_wo'][ch // WO['cpd']]], writes=[b_bank[bk]])
        pst = (psA if base == 0 else psB)[:, 0:D] if D >= 512 else (psA if base == 0 else psB)[:, 0:D]
        bks = [b_bank[base + nb] for nb in range(NB)]
        ss = stat2[:, 3 * t:3 * t + 1]; ln_ = stat2[:, 3 * t + 1:3 * t + 2]; rs = stat2[:, 3 * t + 2:3 * t + 3]
        b_st2 = buf(f"st2_{t % 2}")
        P.add("act", lambda e, s=s, pst=pst, ss=ss: e.activation(out=ot[s][:], in_=pst, func=AF.Square, accum_out=ss), reads=bks, writes=[b_ot[s], b_st2] + (b_unit if t < 2 else []))
        P.add("act", lambda e, ss=ss, ln_=ln_: e.activation(out=ln_, in_=ss, func=AF.Ln, scale=1.0 / D, bias=NORM_EPS), reads=[b_st2], writes=[b_st2])
        P.add("act", lambda e, rs=rs, ln_=ln_: e.activation(out=rs, in_=ln_, func=AF.Exp, scale=-0.5), reads=[b_st2], writes=[b_st2])
        P.add("dve", lambda e, s=s, pst=pst: e.tensor_tensor(out=ot[s][:], in0=pst, in1=gpost_v, op=ALU.mult), reads=bks + [b_gpost], writes=[b_ot[s]])
        P.add("dve", lambda e, s=s, rs=rs: e.scalar_tensor_tensor(out=ot[s][:], in0=ot[s][:], scalar=rs, in1=xr[s][:], op0=ALU.mult, op1=ALU.add),
              reads=[b_ot[s], b_st2, b_xr[s]], writes=[b_ot[s]])
        dma("sp", out_d[128 * t:128 * t + 128, :], ot[s][:], b_ot[s], reads=[b_ot[s]], writes=[buf(f"outd{t}")])
    fin_reads = [buf(f"outd{t}") for t in range(NT)]
    P.add("sp", lambda e: e.nop(), reads=fin_reads)

    MARK['end'] = len(P.ops)
    import os as _os
    _tr = _os.environ.get('KTRUNC')
    if _tr:
        P.ops = P.ops[:(MARK[_tr] if _tr in MARK else int(_tr))]
        print('TRUNC', MARK, len(P.ops))
    eng_sems = {e: es.enter_context(nc.semaphore(f"s_{e}")) for e in ENGS}
    dma_sems = {b.name: es.enter_context(nc.semaphore(f"d_{b.name}")) for b in P.dma_bufs}
    block = es.enter_context(nc.Block())
    engines = {}

    def make(ename):
        def body(eng):
            engines_local = {ename: eng}
            emit_one(ename, eng)
        return body

    cnt = {e: 0 for e in ENGS}
    for op in P.ops:
        if op.dma is None and op.needed:
            cnt[op.eng] += 1
            op.ticket = (op.eng, cnt[op.eng])

    def sem_of(t):
        return eng_sems[t[0]] if isinstance(t[0], str) else dma_sems[t[0].name]

    def emit_one(e, eng):
        waited = {}
        for op in P.ops:
            if op.eng != e:
                continue
            for d in op.deps:
                if d.dma is None and d.eng == "pe" and e == "pe":
                    continue
                key = d.ticket[0] if isinstance(d.ticket[0], str) else d.ticket[0].name
                if waited.get(key, 0) >= d.ticket[1]:
                    continue
                eng.wait_ge(sem_of(d.ticket), d.ticket[1])
                waited[key] = d.ticket[1]
            inst = op.fn(eng)
            if op.dma is not None:
                inst.then_inc(dma_sems[op.dma.name], 16)
            elif op.needed:
                inst.then_inc(eng_sems[e], 1)

    @block.tensor
    def _(eng):
        emit_one("pe", eng)

    @block.scalar
    def _(eng):
        emit_one("act", eng)

    @block.vector
    def _(eng):
        emit_one("dve", eng)

    @block.gpsimd
    def _(eng):
        emit_one("pool", eng)

    @block.sync
    def _(eng):
        emit_one("sp", eng)

    es.close()
    return nc, cnt


def host_consts(cfg: Cfg):
    c = cfg
    NTT = c.NT + 1
    inv_freq = ROPE_THETA ** (-np.arange(0, 64, 2, dtype=np.float64) / 64.0)
    rope = np.zeros((NTT, 128, 256), np.float32)
    n = np.arange(c.SEQ)
    gr = (n // 64).astype(np.float64)[:, None] * inv_freq[None, :]
    gc = (n % 64).astype(np.float64)[:, None] * inv_freq[None, :]
    cr, sr, cc, sc = np.cos(gr), np.sin(gr), np.cos(gc), np.sin(gc)
    COS = np.concatenate([cr, cr, cc, cc], axis=1).astype(np.float32)
    SINS = np.concatenate([-sr, sr, -sc, sc], axis=1).astype(np.float32)
    rope[:c.NT, :, 0:128] = COS.reshape(c.NT, 128, 128)
    rope[:c.NT, :, 128:256] = SINS.reshape(c.NT, 128, 128)
    rope[c.NT, :, 0:128] = 1.0
    QC = c.QC
    Y0 = QC - 128
    y = np.arange(2 * QC - 1)[None, :]
    k = np.arange(128)[:, None]
    ta = np.abs(y - Y0 - k).astype(np.float32)
    ident = np.eye(128, dtype=np.float32)
    return rope, ta, ident


def host_inputs(cfg: Cfg, x, meta_tokens, pre_norm_g, w_in, q_norm_g, k_norm_g, lambda_q1, lambda_k1,
                lambda_q2, lambda_k2, subln_g, w_out, post_norm_g):
    c = cfg
    f = np.float32
    rope, ta, ident = host_consts(c)
    w_in0 = np.asarray(w_in[0], f)
    wr = w_in0.reshape(c.DC, 128, c.INW)
    wch = np.empty((c.NCH, 128, c.DC * 256), f)
    for ci, (kind, i) in enumerate(c.chunks):
        cols = c.chunk_cols(kind, i)
        wch[ci] = wr[:, :, cols].transpose(1, 0, 2).reshape(128, c.DC * 256)
    wout = np.ascontiguousarray(np.asarray(w_out[0], f).reshape(c.DC, 128, c.D).transpose(1, 0, 2).reshape(128, c.DC * c.D))
    gpre = np.ascontiguousarray(np.asarray(pre_norm_g[0], f).reshape(c.DC, 128).T)
    gvec = np.ascontiguousarray(np.broadcast_to(np.concatenate([np.asarray(q_norm_g[0], f), np.asarray(k_norm_g[0], f), np.asarray(subln_g[0], f)])[None, :], (128, 384)))
    gpost = np.ascontiguousarray(np.broadcast_to(np.asarray(post_norm_g[0], f)[None, :], (128, c.D)))
    lamv = np.ascontiguousarray(np.broadcast_to(np.concatenate([np.asarray(lambda_q1[0], f), np.asarray(lambda_k1[0], f), np.asarray(lambda_q2[0], f), np.asarray(lambda_k2[0], f)])[None, :], (128, 256)))
    shared = {"meta": np.ascontiguousarray(np.asarray(meta_tokens, f)), "gpre": gpre, "wch": wch, "wout": wout, "gvec": gvec,
              "gpost": gpost, "lamv": lamv, "ident": ident, "rope": rope, "ta": ta}
    xs = np.asarray(x, f)
    return [dict(shared, x=np.ascontiguousarray(xs[b])) for b in range(xs.shape[0])]


_NC_CACHE = {}


def kernel(x, meta_tokens, pre_norm_g, w_in, q_norm_g, k_norm_g, lambda_q1, lambda_k1,
           lambda_q2, lambda_k2, subln_g, w_out, post_norm_g):
    cfg = Cfg(2048, 2048)
    in_maps = host_inputs(cfg, x, meta_tokens, pre_norm_g, w_in, q_norm_g, k_norm_g, lambda_q1, lambda_k1,
                          lambda_q2, lambda_k2, subln_g, w_out, post_norm_g)
    nc, _ = build_nc(cfg)
    res = run_bass_kernel_spmd(nc, in_maps, core_ids=list(range(len(in_maps))))
    return np.stack([np.asarray(r["out"], np.float32) for r in res.results], axis=0)
```

```python
import math
import numpy as np
import concourse.bass as bass
import concourse.mybir as mybir
from concourse.bass_utils import run_bass_kernel_spmd

F32 = mybir.dt.float32
BF16 = mybir.dt.bfloat16
AF = mybir.ActivationFunctionType
ALU = mybir.AluOpType

NORM_EPS = 1e-6
ROPE_THETA = 10000.0
LAMBDA_INIT = 0.8 - 0.6 * math.exp(-0.3 * 0)


class Cfg:
    def __init__(self, D=2048, SEQ=2048):
        self.D, self.SEQ = D, SEQ
        self.NM = 16
        self.AW = D // 2
        self.AH = self.AW // 128
        self.AKV = 2
        self.AG = self.AH // self.AKV
        self.BW = D - self.AW
        self.BH = self.BW // 128
        self.DC = D // 128
        self.NT = SEQ // 128
        self.QC = min(512, SEQ)
        self.NQC = SEQ // self.QC
        self.QT = self.QC // 128
        self.L = SEQ + self.NM
        self.o_qa = 0
        self.o_ka = self.o_qa + self.AH * 128
        self.o_va = self.o_ka + self.AKV * 128
        self.o_ga = self.o_va + self.AKV * 128
        self.o_qb = self.o_ga + self.AW
        self.o_kb = self.o_qb + self.BH * 128
        self.o_vb = self.o_kb + self.BH * 128
        self.o_gb = self.o_vb + self.BH * 128
        self.INW = self.o_gb + self.BW
        self.chunks = []
        for hq in range(self.AH):
            if hq % self.AG == 0:
                g = hq // self.AG
                self.chunks.append(("akv", g))
            self.chunks.append(("aqg", hq))
        for h in range(self.BH):
            self.chunks.append(("bqk", h))
            self.chunks.append(("bvg", h))
        self.NCH = len(self.chunks)

    def chunk_cols(self, kind, i):
        r = np.arange(128)
        if kind == "akv":
            return np.concatenate([self.o_ka + 128 * i + r, self.o_va + 128 * i + r])
        if kind == "aqg":
            return np.concatenate([self.o_qa + 128 * i + r, self.o_ga + 128 * i + r])
        if kind == "bqk":
            return np.concatenate([self.o_qb + 128 * i + r, self.o_kb + 128 * i + r])
        if kind == "bvg":
            return np.concatenate([self.o_vb + 128 * i + r, self.o_gb + 128 * i + r])
        raise ValueError(kind)


class Buf:
    __slots__ = ("name", "wr", "rd", "sem", "cnt", "excl")

    def __init__(self, name):
        self.name = name
        self.excl = False
        self.wr = []
        self.rd = []
        self.sem = None
        self.cnt = 0


class Op:
    __slots__ = ("eng", "fn", "deps", "ticket", "needed", "dma", "idx")


ENGS = ("pe", "act", "dve", "pool", "sp")


class Prog:
    def __init__(self):
        self.ops = []
        self.dma_bufs = []

    def add(self, eng, fn, reads=(), writes=(), dma=None):
        op = Op()
        op.eng, op.fn, op.dma, op.needed, op.ticket = eng, fn, dma, False, None
        op.idx = len(self.ops)
        deps = []
        for b in reads:
            deps += b.wr
            if b.excl:
                deps += [r for r in b.rd if r.eng != eng]
        for b in writes:
            deps += b.wr
            deps += b.rd
        seen = set()
        op.deps = []
        for d in deps:
            if id(d) not in seen and d is not op:
                seen.add(id(d))
                if eng == "pe" and dma is None and d.eng == "pe" and d.dma is None:
                    continue
                op.deps.append(d)
        for d in op.deps:
            d.needed = True
        for b in reads:
            if dma is None:
                b.rd = [r for r in b.rd if not (r.dma is None and r.eng == eng)]
            b.rd.append(op)
        for b in writes:
            b.wr = [op]
            b.rd = []
        if dma is not None:
            if dma not in self.dma_bufs:
                self.dma_bufs.append(dma)
            dma.cnt += 16
            op.ticket = (dma, dma.cnt)
            op.needed = True
        self.ops.append(op)
        return op

    def barrier_bufs(self, bufs_all):
        pass

    def emit(self, nc, engines, eng_sems, dma_sems):
        cnt = {e: 0 for e in ENGS}
        for op in self.ops:
            if op.dma is None and op.needed:
                cnt[op.eng] += 1
                op.ticket = (op.eng, cnt[op.eng])

        def sem_of(t):
            return eng_sems[t[0]] if isinstance(t[0], str) else dma_sems[t[0].name]

        for e in ENGS:
            eng = engines[e]
            waited = {}
            for op in self.ops:
                if op.eng != e:
                    continue
                for d in op.deps:
                    if d.dma is None and d.eng == "pe" and e == "pe":
                        continue
                    key = d.ticket[0] if isinstance(d.ticket[0], str) else d.ticket[0].name
                    if waited.get(key, 0) >= d.ticket[1]:
                        continue
                    eng.wait_ge(sem_of(d.ticket), d.ticket[1])
                    waited[key] = d.ticket[1]
                inst = op.fn(eng)
                if op.dma is not None:
                    inst.then_inc(dma_sems[op.dma.name], 16)
                elif op.needed:
                    inst.then_inc(eng_sems[e], 1)
        return cnt


def build_nc(cfg: Cfg):
    c = cfg
    D, SEQ, DC, NT, QC, NQC, QT = c.D, c.SEQ, c.DC, c.NT, c.QC, c.NQC, c.QT
    NTT = NT + 1
    nc = bass.Bass("TRN2", target_bir_lowering=False)

    def din(name, shape, dt=F32):
        return nc.dram_tensor(name, list(shape), dt, kind="ExternalInput").ap()

    x_d = din("x", [SEQ, D])
    meta_d = din("meta", [c.NM, D])
    gpre_d = din("gpre", [128, DC])
    wch_d = din("wch", [c.NCH, 128, DC * 256])
    wout_d = din("wout", [128, DC * D])
    gvec_d = din("gvec", [128, 3 * 128])
    gpost_d = din("gpost", [128, D])
    lam_d = din("lamv", [128, 4 * 64])
    ident_d = din("ident", [128, 128])
    rope_d = din("rope", [NTT, 128, 256])
    ta_d = din("ta", [128, 2 * QC - 1])
    out_d = nc.dram_tensor("out", [SEQ, D], F32, kind="ExternalOutput").ap()

    P = Prog()
    from contextlib import ExitStack
    es = ExitStack()

    def sb(name, shape, dt):
        return es.enter_context(nc.sbuf_tensor(name, list(shape), dt))

    hn_raw = sb("hn_raw", [128, max(DC * c.L, DC * D)], BF16)
    yt_raw = sb("yt_raw", [128, max(DC * SEQ, 6 * D)], BF16)
    WS = 3
    wk_elems = max(WS * DC * 256 + (2 * SEQ + c.L + NTT * 129 + NT * 128), 2 * 2 * D * 2 + (2 * D if D > 512 else 0))
    wk_elems = (wk_elems + 63) // 64 * 64
    wk_raw = sb("wk_raw", [128, wk_elems], BF16)
    pT = [sb(f"pT{i}", [128, QC], BF16) for i in range(5)]
    osb = sb("osb", [128, 516], F32)
    ta_sb = sb("ta_sb", [128, 2 * QC - 1], F32)
    rope_sb = [sb(f"rope{i}", [128, 256], F32) for i in range(2)]
    ident = sb("ident_sb", [128, 128], BF16)
    gpre = sb("gpre_sb", [128, DC], F32)
    gvec = sb("gvec_sb", [128, 384], F32)
    sgs = sb("sgs", [128, 128], F32)
    lamv = sb("lamv_sb", [128, 256], F32)
    lamt = sb("lamt", [128, 8], F32)
    stat = sb("stat", [128, 3 * (NTT + 1)], F32)
    stat2 = sb("stat2", [128, 3 * (NTT + 1)], F32)
    NTMP = 4
    tA = [sb(f"tA{i}", [128, 128], F32) for i in range(NTMP)]
    tB = [sb(f"tB{i}", [128, 128], F32) for i in range(NTMP)]
    tC = [sb(f"tC{i}", [128, 128], F32) for i in range(NTMP)]
    tN = [sb(f"tN{i}", [128, 128], BF16) for i in range(NTMP)]
    tS = [sb(f"tS{i}", [128, 4], F32) for i in range(NTMP)]
    tY = [sb(f"tY{i}", [128, 128], BF16) for i in range(4)]
    tD = [sb(f"tD{i}", [128, 128], F32) for i in range(NTMP)]
    eA = [sb(f"eA{i}", [128, 128], F32) for i in range(4)]
    eS = [sb(f"eS{i}", [128, 4], F32) for i in range(4)]
    junk = sb("junk", [128, 128], BF16)
    o1buf = sb("o1buf", [128, QT * 128], F32)
    tE = [sb(f"tE{i}", [128, 4], F32) for i in range(4)]
    psA = es.enter_context(nc.psum_tensor("psA", [128, 2048], F32))
    psB = es.enter_context(nc.psum_tensor("psB", [128, 2048], F32))

    def bank(i):
        t = psA if i < 4 else psB
        return t[:, 512 * (i % 4):512 * (i % 4) + 512]

    hnT = hn_raw[:, 0:DC * c.L].rearrange("p (c t) -> p c t", c=DC)
    YT = yt_raw[:, 0:DC * SEQ].rearrange("p (c t) -> p c t", c=DC)
    wslot = [wk_raw[:, i * DC * 256:(i + 1) * DC * 256].rearrange("p (c n) -> p c n", c=DC) for i in range(WS)]
    o = WS * DC * 256
    qT = wk_raw[:, o:o + SEQ]; o += SEQ
    qT2 = wk_raw[:, o:o + SEQ]; o += SEQ
    kT = wk_raw[:, o:o + c.L]; o += c.L
    vU = wk_raw[:, o:o + NTT * 129].rearrange("p (t n) -> p t n", n=129); o += NTT * 129
    gate = wk_raw[:, o:o + NT * 128].rearrange("p (t n) -> p t n", n=128); o += NT * 128
    vAv = vU
    kTA = kT
    xs = [yt_raw[:, i * 2 * D:(i + 1) * 2 * D].bitcast(F32) for i in range(2)]
    xn = [yt_raw[:, 4 * D + i * D:4 * D + (i + 1) * D] for i in range(2)]
    xr = [wk_raw[:, i * 2 * D:(i + 1) * 2 * D].bitcast(F32) for i in range(2)]
    ot = [wk_raw[:, 4 * D + i * 2 * D:4 * D + (i + 1) * 2 * D].bitcast(F32) for i in range(2)]
    gpost = None
    wout = hn_raw[:, 0:DC * D].rearrange("p (c n) -> p c n", c=DC)

    def tok(t):
        if t < NT:
            return slice(128 * t, 128 * t + 128), 128
        return slice(SEQ, SEQ + c.NM), c.NM

    B = {}

    def buf(name):
        if name not in B:
            B[name] = Buf(name)
        return B[name]

    b_bank = [buf(f"bank{i}") for i in range(8)]
    for b_ in b_bank:
        b_.excl = True
    b_hn = [buf(f"hn{t}") for t in range(NTT)]
    b_yt = [buf(f"yt{t}") for t in range(NT)]
    b_w = [buf(f"w{i}") for i in range(WS)]
    b_const = buf("const")

    def dma(eng, out, in_, buf_, reads=(), writes=()):
        return P.add(eng, lambda e, out=out, in_=in_: e.dma_start(out=out, in_=in_), reads=reads, writes=writes, dma=buf_)

    b_c = [buf(f"c{i}") for i in range(6)]
    dma("sp", gpre[:], gpre_d, b_c[0], writes=[b_c[0]])
    dma("sp", gvec[:], gvec_d, b_c[1], writes=[b_c[1]])
    dma("sp", lamv[:], lam_d, b_c[2], writes=[b_c[2]])
    dma("sp", ta_sb[:], ta_d, b_c[3], writes=[b_c[3]])
    dma("pool", ident[:], ident_d, b_c[4], writes=[b_c[4]])
    b_gpre, b_gvec, b_lamv, b_ta, b_ident = b_c[0], b_c[1], b_c[2], b_c[3], b_c[4]

    b_vU = buf("vU"); b_vA = b_vU; b_lamt = buf("lamt"); b_sgs = buf("sgs")
    P.add("pool", lambda e: e.memset(vAv[:, :, 128:129], 1.0), writes=[b_vA])
    lv = lamv[:].rearrange("p (a n) -> p a n", a=4)
    P.add("dve", lambda e: e.tensor_tensor(out=tA[0][:, 0:64], in0=lv[:, 0, :], in1=lv[:, 1, :], op=ALU.mult), reads=[b_lamv], writes=[buf("tA0")])
    P.add("dve", lambda e: e.tensor_tensor(out=tA[0][:, 64:128], in0=lv[:, 2, :], in1=lv[:, 3, :], op=ALU.mult), reads=[b_lamv, buf("tA0")], writes=[buf("tA0")])
    P.add("dve", lambda e: e.tensor_reduce(out=lamt[:, 0:2], in_=tA[0][:].rearrange("p (a n) -> p a n", a=2), axis=mybir.AxisListType.X, op=ALU.add), reads=[buf("tA0")], writes=[b_lamt])
    P.add("act", lambda e: e.activation(out=lamt[:, 2:4], in_=lamt[:, 0:2], func=AF.Exp), reads=[b_lamt], writes=[b_lamt])
    P.add("dve", lambda e: e.tensor_tensor(out=lamt[:, 4:5], in0=lamt[:, 3:4], in1=lamt[:, 2:3], op=ALU.subtract), reads=[b_lamt], writes=[b_lamt])
    P.add("dve", lambda e: e.tensor_scalar(out=lamt[:, 5:6], in0=lamt[:, 4:5], scalar1=-LAMBDA_INIT, scalar2=None, op0=ALU.add), reads=[b_lamt], writes=[b_lamt])
    neglam = lamt[:, 5:6]
    P.add("dve", lambda e: e.tensor_scalar(out=sgs[:], in0=gvec[:, 256:384], scalar1=(1.0 - LAMBDA_INIT), scalar2=None, op0=ALU.mult), reads=[b_gvec], writes=[b_sgs])

    b_xs = [buf(f"xs{i}") for i in range(2)]
    b_xn = [buf(f"xn{i}") for i in range(2)]
    b_stat = buf("stat")
    b_ph0 = buf("ph0")
    HB = max(1, DC // 8)
    for t in range(NTT):
        sl, rows = tok(t)
        s = t % 2
        src = x_d[128 * t:128 * t + 128, :] if t < NT else meta_d
        dma("sp", xs[s][0:rows, :], src, b_xs[s], writes=[b_xs[s]])
        ss = stat[0:rows, 3 * t:3 * t + 1]; ln_ = stat[0:rows, 3 * t + 1:3 * t + 2]; rs = stat[0:rows, 3 * t + 2:3 * t + 3]
        P.add("act", lambda e, s=s, rows=rows, ss=ss: e.activation(out=xn[s][0:rows, :], in_=xs[s][0:rows, :], func=AF.Square, accum_out=ss),
              reads=[b_xs[s], b_ph0], writes=[b_xn[s], b_stat])
        P.add("act", lambda e, ss=ss, ln_=ln_: e.activation(out=ln_, in_=ss, func=AF.Ln, scale=1.0 / D, bias=NORM_EPS), reads=[b_stat], writes=[b_stat])
        P.add("act", lambda e, rs=rs, ln_=ln_: e.activation(out=rs, in_=ln_, func=AF.Exp, scale=-0.5), reads=[b_stat], writes=[b_stat])
        P.add("dve", lambda e, s=s, rows=rows, rs=rs: e.tensor_scalar(out=xn[s][0:rows, :], in0=xs[s][0:rows, :], scalar1=rs, scalar2=None, op0=ALU.mult),
              reads=[b_xs[s], b_stat, b_ph0], writes=[b_xn[s]])
        for hb in range(HB):
            bk = (2 * (t % 2) + hb) % 8 if HB <= 2 else hb % 8
            bkv = bank(bk).bitcast(BF16)
            nchb = min(8, DC - 8 * hb)
            for cc in range(nchb):
                ch = 8 * hb + cc
                P.add("pe", lambda e, bkv=bkv, cc=cc, ch=ch, s=s, rows=rows: e.transpose(bkv[:, cc * 128:cc * 128 + rows], xn[s][0:rows, ch * 128:(ch + 1) * 128], ident[0:rows, 0:rows]),
                      reads=[b_xn[s], b_ident, b_ph0], writes=[b_bank[bk]])
            srcv = bkv[:, 0:nchb * 128].rearrange("p (c r) -> p c r", c=nchb)[:, :, 0:rows]
            gv = gpre[:, 8 * hb:8 * hb + nchb].unsqueeze(2).to_broadcast([128, nchb, rows])
            P.add("dve", lambda e, srcv=srcv, gv=gv, hb=hb, nchb=nchb, sl=sl: e.tensor_tensor(out=hnT[:, 8 * hb:8 * hb + nchb, sl], in0=srcv, in1=gv, op=ALU.mult),
                  reads=[b_bank[bk], b_gpre], writes=[b_hn[t]])

    MARK = {}
    MARK['ph0'] = len(P.ops)
    wstate = {"next": 0}
    b_rope = [buf(f"rope{i}") for i in range(2)]
    rope_ctr = {"n": 0}
    tmp_ctr = {"n": 0}
    tn_ctr = {"n": 0}
    ty_ctr = {"n": 0}
    b_tY = [buf(f"tY{i}") for i in range(4)]
    b_tA = [buf(f"tA{i}") for i in range(NTMP)]; b_tB = [buf(f"tB{i}") for i in range(NTMP)]
    b_tC = [buf(f"tC{i}") for i in range(NTMP)]; b_tN = [buf(f"tN{i}") for i in range(NTMP)]
    b_tS = [buf(f"tS{i}") for i in range(NTMP)]
    b_junk = buf("junk")
    proj_banks = [0, 1, 2, 3, 4]
    pb_ctr = {"n": 0}
    tr_ctr = {"n": 0}
    b_qT = buf("qT"); b_qT2 = buf("qT2"); b_kT = buf("kT"); b_kTA = b_kT; b_gate = buf("gate")

    pend = []
    tk = {"n": 0, "seq": 0}

    def defer(fn, delay, keys=()):
        tk["seq"] += 1
        pend.append((tk["n"] + delay, tk["seq"], fn, tuple(keys)))
        pend.sort(key=lambda p: (p[0], p[1]))

    def acquire(key):
        while any(key in p[3] for p in pend):
            pend.pop(0)[2]()

    def tick():
        tk["n"] += 1
        while pend and pend[0][0] <= tk["n"]:
            pend.pop(0)[2]()

    def flush_all():
        while pend:
            pend.pop(0)[2]()

    def load_chunk(ci):
        s = ci % WS
        dst = wslot[s].rearrange("p c n -> p (c n)").rearrange("p (a b) -> p a b", b=512)
        srcv = wch_d[ci].rearrange("p (a b) -> p a b", b=512)
        dma("pool", dst, srcv, b_w[s], writes=[b_w[s]])

    PREFETCH = WS - 1
    for ci in range(min(PREFETCH, c.NCH)):
        load_chunk(ci)
    wstate["loaded"] = min(PREFETCH, c.NCH)

    def next_chunk():
        ci = wstate["next"]
        wstate["next"] += 1
        return ci % WS

    def prefetch_more():
        if wstate["loaded"] < c.NCH:
            load_chunk(wstate["loaded"])
            wstate["loaded"] += 1

    def proj_token_major(ws, t, ncols=256):
        sl, rows = tok(t)
        bk = proj_banks[pb_ctr["n"] % len(proj_banks)]; pb_ctr["n"] += 1
        for ch in range(DC):
            P.add("pe", lambda e, bk=bk, ch=ch, ws=ws, sl=sl, rows=rows: e.matmul(bank(bk)[0:rows, 0:ncols], hnT[:, ch, sl], wslot[ws][:, ch, 0:ncols], start=(ch == 0), stop=(ch == DC - 1)),
                  reads=[b_hn[t], b_w[ws]], writes=[b_bank[bk]])
        return bk

    def y_transpose(ybf, b_y, mix_chunk, tq, on_dve=False):
        k = tr_ctr["n"] % 4; tr_ctr["n"] += 1
        tv = bank(7).bitcast(BF16)[:, k * 256:k * 256 + 128]
        P.add("pe", lambda e: e.transpose(tv, ybf, ident[:, :]), reads=[b_y, b_ident], writes=[b_bank[7]])
        first = not ytr_ctr.get("started")
        ytr_ctr["started"] = True
        wr = [b_yt[tq]] + ([b_ph0, b_xs[0], b_xs[1], b_xn[0], b_xn[1]] if first else [])
        if on_dve:
            P.add("dve", lambda e: e.tensor_copy(out=YT[:, mix_chunk, 128 * tq:128 * tq + 128], in_=tv), reads=[b_bank[7]], writes=wr)
        else:
            P.add("act", lambda e: e.activation(out=YT[:, mix_chunk, 128 * tq:128 * tq + 128], in_=tv, func=AF.Copy), reads=[b_bank[7]], writes=wr)

    qk_ctr = {"n": 0}
    b_tD = [buf(f"tD{i}") for i in range(NTMP)]

    def qk_tile(bk, rows, t, gcol, dstT, b_dst, sl, other):
        i = qk_ctr["n"] % NTMP; qk_ctr["n"] += 1
        key = ("qk", i)
        acquire(key)
        if other == "gate":
            acquire(("gate",))
        acquire(("sg", i))
        r = rope_ctr["n"] % 2; rope_ctr["n"] += 1
        srcp = bank(bk)[0:rows, 0:128]
        osrc = bank(bk)[0:rows, 128:256]
        dma("sp", rope_sb[r][0:rows, :], rope_d[t, 0:rows, :], b_rope[r], writes=[b_rope[r]])
        ss = tS[i][0:rows, 0:1]; ln_ = tS[i][0:rows, 1:2]; rs = tS[i][0:rows, 2:3]
        xg = tA[i][0:rows, :]
        t1 = tB[i][0:rows, :]
        t2 = tD[i][0:rows, :]
        xb = tN[i][0:rows, :]

        def st2():
            if other == "gate":
                P.add("act", lambda e: e.activation(out=tC[i][:], in_=osrc, func=AF.Exp, scale=-1.0), reads=[b_bank[bk]], writes=[b_tC[i]])
            else:
                _, vview, b_v = other
                P.add("act", lambda e: e.activation(out=vview[0:rows, t, 0:128], in_=osrc, func=AF.Copy), reads=[b_bank[bk]], writes=[b_v])
            P.add("act", lambda e: e.activation(out=junk[0:rows, :], in_=srcp, func=AF.Square, accum_out=ss), reads=[b_bank[bk]], writes=[b_junk, b_tS[i]])
            P.add("act", lambda e: e.activation(out=ln_, in_=ss, func=AF.Ln, scale=1.0 / 128, bias=NORM_EPS), reads=[b_tS[i]], writes=[b_tS[i]])
            P.add("act", lambda e: e.activation(out=rs, in_=ln_, func=AF.Exp, scale=-0.5), reads=[b_tS[i]], writes=[b_tS[i]])
            P.add("dve", lambda e: e.tensor_tensor(out=xg, in0=srcp, in1=gvec[0:rows, gcol:gcol + 128], op=ALU.mult), reads=[b_bank[bk], b_gvec], writes=[b_tA[i]])

        def st3():
            if other == "gate":
                P.add("act", lambda e: e.activation(out=tC[i][:], in_=tC[i][:], func=AF.Ln, bias=1.0), reads=[b_tC[i]], writes=[b_tC[i]])
                P.add("act", lambda e: e.activation(out=tC[i][:], in_=tC[i][:], func=AF.Exp, scale=-1.0), reads=[b_tC[i]], writes=[b_tC[i]])
            P.add("dve", lambda e: e.tensor_tensor(out=t1, in0=xg, in1=rope_sb[r][0:rows, 0:128], op=ALU.mult), reads=[b_tA[i], b_rope[r]], writes=[b_tB[i]])
            xsw = xg.rearrange("p (a s j) -> p a s j", a=2, s=2)[:, :, ::-1, :]
            P.add("pool", lambda e: e.tensor_tensor(out=t2.rearrange("p (a s j) -> p a s j", a=2, s=2), in0=xsw, in1=rope_sb[r][0:rows, 128:256].rearrange("p (a s j) -> p a s j", a=2, s=2), op=ALU.mult),
                  reads=[b_tA[i], b_rope[r]], writes=[b_tD[i]])

        def st4():
            if other == "gate":
                P.add("dve", lambda e: e.tensor_tensor(out=gate[:, t, :], in0=osrc, in1=tC[i][:], op=ALU.mult), reads=[b_bank[bk], b_tC[i]], writes=[b_gate])
            P.add("pool", lambda e: e.tensor_tensor(out=t1, in0=t1, in1=t2, op=ALU.add), reads=[b_tB[i], b_tD[i]], writes=[b_tB[i]])
            P.add("dve", lambda e: e.tensor_scalar(out=xb, in0=t1, scalar1=rs, scalar2=None, op0=ALU.mult), reads=[b_tB[i], b_tS[i]], writes=[b_tN[i]])

        def st5():
            k = tr_ctr["n"] % 4; tr_ctr["n"] += 1
            tv = bank(7).bitcast(BF16)[:, k * 256:k * 256 + rows]
            P.add("pe", lambda e: e.transpose(tv, xb, ident[0:rows, 0:rows]), reads=[b_tN[i], b_ident], writes=[b_bank[7]])
            P.add("dve", lambda e: e.tensor_copy(out=dstT[:, sl], in_=tv), reads=[b_bank[7]], writes=[b_dst])

        st2()
        defer(st3, 1, [key])
        defer(st4, 2, [key])
        defer(st5, 3, [key])
        tick()

    sg_ctr = {"n": 0}

    def silu_gate(bk, col0, t):
        i = sg_ctr["n"] % NTMP; sg_ctr["n"] += 1
        key = ("sg", i)
        acquire(key)
        acquire(("qk", i))
        acquire(("gate",))
        gsrc = bank(bk)[:, col0:col0 + 128]
        P.add("act", lambda e: e.activation(out=tC[i][:], in_=gsrc, func=AF.Exp, scale=-1.0), reads=[b_bank[bk]], writes=[b_tC[i]])

        def st():
            P.add("act", lambda e: e.activation(out=tC[i][:], in_=tC[i][:], func=AF.Ln, bias=1.0), reads=[b_tC[i]], writes=[b_tC[i]])
            P.add("act", lambda e: e.activation(out=tC[i][:], in_=tC[i][:], func=AF.Exp, scale=-1.0), reads=[b_tC[i]], writes=[b_tC[i]])

        def st_b():
            P.add("dve", lambda e: e.tensor_tensor(out=gate[:, t, :], in0=gsrc, in1=tC[i][:], op=ALU.mult), reads=[b_bank[bk], b_tC[i]], writes=[b_gate])
        defer(st, 1, [key])
        defer(st_b, 2, [key])

    SCALE_A = 1.0 / math.sqrt(128.0)
    NPT = len(pT)
    b_pT = [buf(f"pT{i}") for i in range(NPT)]
    b_o1s = [buf(f"o1_{j}") for j in range(QT)]
    b_osbs = [buf("osb0"), buf("osb1")]
    e_ctr = {"n": 0}
    b_eA = [buf(f"eA{i}") for i in range(4)]
    b_eS = [buf(f"eS{i}") for i in range(4)]
    b_tE = [buf(f"tE{i}") for i in range(4)]
    te_ctr = {"n": 0}
    ytr_ctr = {}
    SBANKS = [0, 1, 2, 3, 4]
    OBANKS = [5, 6]

    def attention(kq_aps, b_k, b_qs, v_ap, b_v, tiles_fn, epilogue, dve_heavy=False):
        nm = len(kq_aps)
        seq = [(cq, m, kt) for cq in range(NQC) for m in range(nm) for kt in range(NTT)]
        LOOK = len(SBANKS) - 1
        issued = {}

        def issue_S(n):
            cq, m, kt = seq[n]
            sl, rows = tok(kt)
            bk = SBANKS[n % len(SBANKS)]
            kT_ap, qT_ap = kq_aps[m]
            P.add("pe", lambda e: e.matmul(bank(bk)[0:rows, 0:QC], kT_ap[:, sl], qT_ap[:, cq * QC:(cq + 1) * QC], start=True, stop=True),
                  reads=[b_k, b_qs[m]], writes=[b_bank[bk]])
            issued[n] = bk

        LA = 2

        def tile_ops(n):
            cq, m, kt = seq[n]
            sl, rows = tok(kt)
            pi = n % NPT
            tiles_fn(m, issued[n], rows, cq, kt, pT[pi][0:rows, :], b_pT[pi])

        for n in range(min(LOOK, len(seq))):
            issue_S(n)
        for n in range(min(LA, len(seq))):
            tile_ops(n)
        for n, (cq, m, kt) in enumerate(seq):
            sl, rows = tok(kt)
            pi = n % NPT
            if n + LA < len(seq):
                tile_ops(n + LA)
            for j in range(QT):
                ob = OBANKS[j // 2]
                oc = (j % 2) * 129
                first_in_bank = (kt == 0 and (j % 2 == 0))
                P.add("pe", lambda e, ob=ob, oc=oc, j=j, rows=rows, kt=kt, pi=pi, fib=first_in_bank: e.matmul(
                    bank(ob)[:, oc:oc + 129], pT[pi][0:rows, j * 128:(j + 1) * 128], v_ap[0:rows, kt, :],
                    start=fib, stop=(kt == NTT - 1), skip_group_check=True),
                    reads=[b_pT[pi], b_v], writes=[b_bank[ob]])
            if n + LOOK < len(seq):
                issue_S(n + LOOK)
            if kt == NTT - 1:
                acquire(("osb",))
                nob = (QT + 1) // 2
                for bi in range(nob):
                    w = 129 * min(2, QT - 2 * bi)
                    if bi == 0:
                        P.add("act", lambda e, bi=bi, w=w: e.activation(out=osb[:, bi * 258:bi * 258 + w], in_=bank(OBANKS[bi])[:, 0:w], func=AF.Copy),
                              reads=[b_bank[OBANKS[bi]]], writes=[b_osbs[bi]])
                    elif dve_heavy:
                        P.add("act", lambda e, bi=bi, w=w: e.activation(out=osb[:, bi * 258:bi * 258 + w], in_=bank(OBANKS[bi])[:, 0:w], func=AF.Copy),
                              reads=[b_bank[OBANKS[bi]]], writes=[b_osbs[bi]])
                    else:
                        P.add("dve", lambda e, bi=bi, w=w: e.tensor_copy(out=osb[:, bi * 258:bi * 258 + w], in_=bank(OBANKS[bi])[:, 0:w]),
                              reads=[b_bank[OBANKS[bi]]], writes=[b_osbs[bi]])
                for j in range(QT):
                    epilogue(m, cq, j, osb[:, j * 129:(j + 1) * 129], b_osbs[j // 2])
            tick()

    for hq in range(c.AH):
        g = hq // c.AG
        if hq % c.AG == 0:
            ws = next_chunk()
            for t in range(NTT):
                sl, rows = tok(t)
                bk = proj_token_major(ws, t)
                qk_tile(bk, rows, t, 128, kTA, b_kTA, sl, ("v", vAv, b_vA))
            prefetch_more()
        ws = next_chunk()
        for t in range(NT):
            sl, rows = tok(t)
            bk = proj_token_major(ws, t)
            qk_tile(bk, rows, t, 0, qT, b_qT, sl, "gate")
        prefetch_more()
        flush_all()

        def tiles_A(m, bk, rows, cq, kt, pdst, b_p):
            P.add("act", lambda e: e.activation(out=pdst, in_=bank(bk)[0:rows, 0:QC], func=AF.Exp, scale=SCALE_A), reads=[b_bank[bk]], writes=[b_p])

        def epi_A(m, cq, j, ops, b_ops, hq=hq):
            tq = cq * QT + j
            i = e_ctr["n"] % 4; e_ctr["n"] += 1
            key = ("e", i)
            acquire(key)

            def e2():
                P.add("dve", lambda e: e.reciprocal(out=tE[i][:, 0:1], in_=ops[:, 128:129]), reads=[b_ops], writes=[b_tE[i]])
                P.add("dve", lambda e: e.scalar_tensor_tensor(out=tY[i][:], in0=ops[:, 0:128], scalar=tE[i][:, 0:1], in1=gate[:, tq, :], op0=ALU.mult, op1=ALU.mult),
                      reads=[b_ops, b_tE[i], b_gate], writes=[b_tY[i]])
            defer(e2, 1 + j, [key, ("osb",), ("gate",)])
            defer(lambda: y_transpose(tY[i][:], b_tY[i], hq, tq, on_dve=True), 3 + j, [key])

        attention([(kTA, qT)], b_kTA, [b_qT], vAv, b_vA, tiles_A, epi_A)

    MARK['A'] = len(P.ops)
    WO = {}

    def load_wout():
        b_wout = buf("wout")
        NWD = 4 if DC >= 4 else 1
        cpd = DC // NWD
        b_wo = [buf(f"wo{i}") for i in range(NWD)]
        WO['b_wo'] = b_wo; WO['cpd'] = cpd
        for i in range(NWD):
            dstv = wout[:, i * cpd:(i + 1) * cpd, :].rearrange("p c n -> p (c n)").rearrange("p (a b) -> p a b", b=512)
            srcv = wout_d[:, i * cpd * D:(i + 1) * cpd * D].rearrange("p (a b) -> p a b", b=512)
            dma("pool", dstv, srcv, b_wo[i], reads=[], writes=[b_wo[i]] + b_hn)

    P.add("pool", lambda e: e.memset(qT[64:128, :], 0.0), writes=[b_qT])
    P.add("pool", lambda e: e.memset(qT2[0:64, :], 0.0), writes=[b_qT2])
    YW = 2 * QC - 1
    Y0 = QC - 128
    WL = Y0 + 127
    for h in range(c.BH):
        slope = 2.0 ** (-8.0 * (h + 1) / c.BH)
        ws = next_chunk()
        ncht = NQC
        for cq in range(NQC):
            bk = proj_banks[pb_ctr["n"] % len(proj_banks)]; pb_ctr["n"] += 1
            for ch in range(DC):
                P.add("pe", lambda e, bk=bk, ch=ch, ws=ws, cq=cq: e.matmul(bank(bk)[:, 0:QC], wslot[ws][:, ch, 0:128], hnT[:, ch, cq * QC:(cq + 1) * QC], start=(ch == 0), stop=(ch == DC - 1)),
                      reads=[b_hn[tt] for tt in range(cq * QT, (cq + 1) * QT)] + [b_w[ws]], writes=[b_bank[bk]])
            P.add("act", lambda e, bk=bk, cq=cq: e.activation(out=qT[0:64, cq * QC:(cq + 1) * QC], in_=bank(bk)[0:64, 0:QC], func=AF.Copy, scale=0.125), reads=[b_bank[bk]], writes=[b_qT])
            P.add("act", lambda e, bk=bk, cq=cq: e.activation(out=qT2[64:128, cq * QC:(cq + 1) * QC], in_=bank(bk)[64:128, 0:QC], func=AF.Copy, scale=0.125), reads=[b_bank[bk]], writes=[b_qT2])
            tick()
        for cq in range(NQC + 1):
            bk = proj_banks[pb_ctr["n"] % len(proj_banks)]; pb_ctr["n"] += 1
            csl = slice(cq * QC, (cq + 1) * QC) if cq < NQC else slice(SEQ, SEQ + c.NM)
            n = QC if cq < NQC else c.NM
            rd = [b_hn[tt] for tt in range(cq * QT, (cq + 1) * QT)] if cq < NQC else [b_hn[NT]]
            for ch in range(DC):
                P.add("pe", lambda e, bk=bk, ch=ch, ws=ws, csl=csl, n=n: e.matmul(bank(bk)[:, 0:n], wslot[ws][:, ch, 128:256], hnT[:, ch, csl], start=(ch == 0), stop=(ch == DC - 1)),
                      reads=rd + [b_w[ws]], writes=[b_bank[bk]])
            P.add("dve", lambda e, bk=bk, csl=csl, n=n: e.tensor_copy(out=kT[:, csl], in_=bank(bk)[:, 0:n]), reads=[b_bank[bk]], writes=[b_kT])
            tick()
        prefetch_more()
        ws = next_chunk()
        for t in range(NTT):
            sl, rows = tok(t)
            ncols = 256 if t < NT else 128
            bk = proj_token_major(ws, t, ncols)
            P.add("act", lambda e, bk=bk, rows=rows, t=t: e.activation(out=vU[0:rows, t, 0:128], in_=bank(bk)[0:rows, 0:128], func=AF.Copy), reads=[b_bank[bk]], writes=[b_vU])
            if t < NT:
                silu_gate(bk, 128, t)
            tick()
        prefetch_more()
        flush_all()
        if h == c.BH - 1:
            load_wout()

        ratio = slope if h == 0 else slope / (2.0 ** (-8.0 * h / c.BH))
        P.add("dve", lambda e, ratio=ratio: e.tensor_scalar(out=ta_sb[:], in0=ta_sb[:], scalar1=float(ratio), scalar2=None, op0=ALU.mult), reads=[b_ta], writes=[b_ta])

        def tiles_B(m, bk, rows, cq, kt, pdst, b_p, slope=slope):
            if kt < NT:
                delta = QC * cq - 128 * kt
                if -QC < delta < 128:
                    j = (-delta) // 128
                    ws_ = Y0 - 128 * j
                    coef, cst = -slope, 0.0
                elif delta >= 128:
                    ws_ = WL
                    coef, cst = -slope, -slope * (delta - 127)
                else:
                    ws_ = WL
                    coef, cst = slope, slope * (delta - 127)
            else:
                ws_ = WL
                delta = QC * cq + c.NM
                coef, cst = -slope, -slope * (delta - 127)
            sv = bank(bk)[0:rows, 0:QC]
            op1 = ALU.subtract if coef < 0 else ALU.add
            P.add("dve", lambda e: e.scalar_tensor_tensor(out=sv, in0=sv, scalar=float(cst), in1=ta_sb[0:rows, ws_:ws_ + QC], op0=ALU.add, op1=op1),
                  reads=[b_bank[bk], b_ta], writes=[b_bank[bk]])
            P.add("act", lambda e: e.activation(out=pdst, in_=sv, func=AF.Exp), reads=[b_bank[bk]], writes=[b_p])

        def epi_B(m, cq, j, ops, b_ops, h=h):
            tq = cq * QT + j
            i = e_ctr["n"] % 4; e_ctr["n"] += 1
            key = ("e", i)
            acquire(key)
            od = eA[i][:]
            ss = eS[i][:, 0:1]; ln_ = eS[i][:, 1:2]; rs = eS[i][:, 2:3]

            def e2():
                P.add("dve", lambda e: e.reciprocal(out=tE[i][:, 0:1], in_=ops[:, 128:129]), reads=[b_ops], writes=[b_tE[i]])
                if m == 0:
                    P.add("pool", lambda e: e.tensor_scalar(out=o1buf[:, j * 128:(j + 1) * 128], in0=ops[:, 0:128], scalar1=tE[i][:, 0:1], scalar2=1.0, op0=ALU.mult, op1=ALU.mult),
                          reads=[b_ops, b_tE[i]], writes=[b_o1s[j]])
                    return
                P.add("dve", lambda e: e.tensor_tensor(out=tE[i][:, 1:2], in0=tE[i][:, 0:1], in1=neglam, op=ALU.mult), reads=[b_tE[i], b_lamt], writes=[b_tE[i]])
                P.add("pool", lambda e: e.tensor_scalar(out=od, in0=ops[:, 0:128], scalar1=tE[i][:, 1:2], scalar2=1.0, op0=ALU.mult, op1=ALU.mult),
                      reads=[b_ops, b_tE[i]], writes=[b_eA[i]])
                P.add("pool", lambda e: e.tensor_tensor(out=od, in0=od, in1=o1buf[:, j * 128:(j + 1) * 128], op=ALU.add),
                      reads=[b_eA[i], b_o1s[j]], writes=[b_eA[i]])

            def e3():
                P.add("act", lambda e: e.activation(out=junk[:], in_=od, func=AF.Square, accum_out=ss), reads=[b_eA[i]], writes=[b_junk, b_eS[i]])
                P.add("act", lambda e: e.activation(out=ln_, in_=ss, func=AF.Ln, scale=1.0 / 128, bias=NORM_EPS), reads=[b_eS[i]], writes=[b_eS[i]])
                P.add("act", lambda e: e.activation(out=rs, in_=ln_, func=AF.Exp, scale=-0.5), reads=[b_eS[i]], writes=[b_eS[i]])

            def e4():
                P.add("pool", lambda e: e.tensor_tensor(out=od, in0=od, in1=sgs[:], op=ALU.mult), reads=[b_eA[i], b_sgs], writes=[b_eA[i]])
                P.add("pool", lambda e: e.tensor_scalar(out=od, in0=od, scalar1=rs, scalar2=1.0, op0=ALU.mult, op1=ALU.mult),
                      reads=[b_eA[i], b_eS[i]], writes=[b_eA[i]])
                P.add("pool", lambda e: e.tensor_tensor(out=tY[i][:], in0=od, in1=gate[:, tq, :], op=ALU.mult),
                      reads=[b_eA[i], b_gate], writes=[b_tY[i]])

            defer(e2, 1 + j, [key, ("osb",)])
            if m == 1:
                defer(e3, 4 + j, [key])
                defer(e4, 7 + j, [key, ("gate",)])
                defer(lambda: y_transpose(tY[i][:], b_tY[i], c.AH + h, tq, on_dve=True), 10 + j, [key])

        attention([(kT, qT), (kT, qT2)], b_kT, [b_qT, b_qT2], vU, b_vU, tiles_B, epi_B, dve_heavy=False)

    flush_all()
    MARK['B'] = len(P.ops)
    gpost_sb = None
    b_xr = [buf(f"xr{i}") for i in range(2)]
    b_ot = [buf(f"ot{i}") for i in range(2)]
    b_unit = [b_w[i] for i in range(WS)] + [b_qT, b_qT2, b_kT, b_vU, b_gate]
    b_gpost = buf("gpost")
    gpost_sb = sb("gpost_sb", [128, D], F32) if D <= 512 else None
    if gpost_sb is None:
        off = 8 * D
        assert wk_elems >= off + 2 * D, (wk_elems, off + 2 * D)
        gpost_v = wk_raw[:, off:off + 2 * D].bitcast(F32)
    else:
        gpost_v = gpost_sb[:]
    dma("sp", gpost_v, gpost_d, b_gpost, writes=[b_gpost] + b_unit)
    NB = D // 512 if D >= 512 else 1
    BW_ = min(512, D)
    for t in range(NT):
        s = t % 2
        sl, rows = tok(t)
        dma("sp", xr[s][:], x_d[128 * t:128 * t + 128, :], b_xr[s], writes=[b_xr[s]] + (b_unit if t < 2 else []))
        base = 4 * (t % 2)
        if NB * 1 > 4:
            raise NotImplementedError
        for nb in range(NB):
            bk = base + nb
            for ch in range(DC):
                P.add("pe", lambda e, bk=bk, ch=ch, nb=nb, sl=sl: e.matmul(bank(bk)[:, 0:BW_], YT[:, ch, sl], wout[:, ch, nb * BW_:(nb + 1) * BW_], start=(ch == 0), stop=(ch == DC - 1)),
                      reads=[b_yt[t], WO['b_wo'][ch // WO['cpd']]], writes=[b_bank[bk]])
        pst = (psA if base == 0 else psB)[:, 0:D] if D >= 512 else (psA if base == 0 else psB)[:, 0:D]
        bks = [b_bank[base + nb] for nb in range(NB)]
        ss = stat2[:, 3 * t:3 * t + 1]; ln_ = stat2[:, 3 * t + 1:3 * t + 2]; rs = stat2[:, 3 * t + 2:3 * t + 3]
        b_st2 = buf(f"st2_{t % 2}")
        P.add("act", lambda e, s=s, pst=pst, ss=ss: e.activation(out=ot[s][:], in_=pst, func=AF.Square, accum_out=ss), reads=bks, writes=[b_ot[s], b_st2] + (b_unit if t < 2 else []))
        P.add("act", lambda e, ss=ss, ln_=ln_: e.activation(out=ln_, in_=ss, func=AF.Ln, scale=1.0 / D, bias=NORM_EPS), reads=[b_st2], writes=[b_st2])
        P.add("act", lambda e, rs=rs, ln_=ln_: e.activation(out=rs, in_=ln_, func=AF.Exp, scale=-0.5), reads=[b_st2], writes=[b_st2])
        P.add("dve", lambda e, s=s, pst=pst: e.tensor_tensor(out=ot[s][:], in0=pst, in1=gpost_v, op=ALU.mult), reads=bks + [b_gpost], writes=[b_ot[s]])
        P.add("dve", lambda e, s=s, rs=rs: e.scalar_tensor_tensor(out=ot[s][:], in0=ot[s][:], scalar=rs, in1=xr[s][:], op0=ALU.mult, op1=ALU.add),
              reads=[b_ot[s], b_st2, b_xr[s]], writes=[b_ot[s]])
        dma("sp", out_d[128 * t:128 * t + 128, :], ot[s][:], b_ot[s], reads=[b_ot[s]], writes=[buf(f"outd{t}")])
    fin_reads = [buf(f"outd{t}") for t in range(NT)]
    P.add("sp", lambda e: e.nop(), reads=fin_reads)

    MARK['end'] = len(P.ops)
    import os as _os
    _tr = _os.environ.get('KTRUNC')
    if _tr:
        P.ops = P.ops[:(MARK[_tr] if _tr in MARK else int(_tr))]
        print('TRUNC', MARK, len(P.ops))
    eng_sems = {e: es.enter_context(nc.semaphore(f"s_{e}")) for e in ENGS}
    dma_sems = {b.name: es.enter_context(nc.semaphore(f"d_{b.name}")) for b in P.dma_bufs}
    block = es.enter_context(nc.Block())
    engines = {}

    def make(ename):
        def body(eng):
            engines_local = {ename: eng}
            emit_one(ename, eng)
        return body

    cnt = {e: 0 for e in ENGS}
    for op in P.ops:
        if op.dma is None and op.needed:
            cnt[op.eng] += 1
            op.ticket = (op.eng, cnt[op.eng])

    def sem_of(t):
        return eng_sems[t[0]] if isinstance(t[0], str) else dma_sems[t[0].name]

    def emit_one(e, eng):
        waited = {}
        for op in P.ops:
            if op.eng != e:
                continue
            for d in op.deps:
                if d.dma is None and d.eng == "pe" and e == "pe":
                    continue
                key = d.ticket[0] if isinstance(d.ticket[0], str) else d.ticket[0].name
                if waited.get(key, 0) >= d.ticket[1]:
                    continue
                eng.wait_ge(sem_of(d.ticket), d.ticket[1])
                waited[key] = d.ticket[1]
            inst = op.fn(eng)
            if op.dma is not None:
                inst.then_inc(dma_sems[op.dma.name], 16)
            elif op.needed:
                inst.then_inc(eng_sems[e], 1)

    @block.tensor
    def _(eng):
        emit_one("pe", eng)

    @block.scalar
    def _(eng):
        emit_one("act", eng)

    @block.vector
    def _(eng):
        emit_one("dve", eng)

    @block.gpsimd
    def _(eng):
        emit_one("pool", eng)

    @block.sync
    def _(eng):
        emit_one("sp", eng)

    es.close()
    return nc, cnt


def host_consts(cfg: Cfg):
    c = cfg
    NTT = c.NT + 1
    inv_freq = ROPE_THETA ** (-np.arange(0, 64, 2, dtype=np.float64) / 64.0)
    rope = np.zeros((NTT, 128, 256), np.float32)
    n = np.arange(c.SEQ)
    gr = (n // 64).astype(np.float64)[:, None] * inv_freq[None, :]
    gc = (n % 64).astype(np.float64)[:, None] * inv_freq[None, :]
    cr, sr, cc, sc = np.cos(gr), np.sin(gr), np.cos(gc), np.sin(gc)
    COS = np.concatenate([cr, cr, cc, cc], axis=1).astype(np.float32)
    SINS = np.concatenate([-sr, sr, -sc, sc], axis=1).astype(np.float32)
    rope[:c.NT, :, 0:128] = COS.reshape(c.NT, 128, 128)
    rope[:c.NT, :, 128:256] = SINS.reshape(c.NT, 128, 128)
    rope[c.NT, :, 0:128] = 1.0
    QC = c.QC
    Y0 = QC - 128
    y = np.arange(2 * QC - 1)[None, :]
    k = np.arange(128)[:, None]
    ta = np.abs(y - Y0 - k).astype(np.float32)
    ident = np.eye(128, dtype=np.float32)
    return rope, ta, ident


def host_inputs(cfg: Cfg, x, meta_tokens, pre_norm_g, w_in, q_norm_g, k_norm_g, lambda_q1, lambda_k1,
                lambda_q2, lambda_k2, subln_g, w_out, post_norm_g):
    c = cfg
    f = np.float32
    rope, ta, ident = host_consts(c)
    w_in0 = np.asarray(w_in[0], f)
    wr = w_in0.reshape(c.DC, 128, c.INW)
    wch = np.empty((c.NCH, 128, c.DC * 256), f)
    for ci, (kind, i) in enumerate(c.chunks):
        cols = c.chunk_cols(kind, i)
        wch[ci] = wr[:, :, cols].transpose(1, 0, 2).reshape(128, c.DC * 256)
    wout = np.ascontiguousarray(np.asarray(w_out[0], f).reshape(c.DC, 128, c.D).transpose(1, 0, 2).reshape(128, c.DC * c.D))
    gpre = np.ascontiguousarray(np.asarray(pre_norm_g[0], f).reshape(c.DC, 128).T)
    gvec = np.ascontiguousarray(np.broadcast_to(np.concatenate([np.asarray(q_norm_g[0], f), np.asarray(k_norm_g[0], f), np.asarray(subln_g[0], f)])[None, :], (128, 384)))
    gpost = np.ascontiguousarray(np.broadcast_to(np.asarray(post_norm_g[0], f)[None, :], (128, c.D)))
    lamv = np.ascontiguousarray(np.broadcast_to(np.concatenate([np.asarray(lambda_q1[0], f), np.asarray(lambda_k1[0], f), np.asarray(lambda_q2[0], f), np.asarray(lambda_k2[0], f)])[None, :], (128, 256)))
    shared = {"meta": np.ascontiguousarray(np.asarray(meta_tokens, f)), "gpre": gpre, "wch": wch, "wout": wout, "gvec": gvec,
              "gpost": gpost, "lamv": lamv, "ident": ident, "rope": rope, "ta": ta}
    xs = np.asarray(x, f)
    return [dict(shared, x=np.ascontiguousarray(xs[b])) for b in range(xs.shape[0])]


_NC_CACHE = {}


def kernel(x, meta_tokens, pre_norm_g, w_in, q_norm_g, k_norm_g, lambda_q1, lambda_k1,
           lambda_q2, lambda_k2, subln_g, w_out, post_norm_g):
    cfg = Cfg(2048, 2048)
    in_maps = host_inputs(cfg, x, meta_tokens, pre_norm_g, w_in, q_norm_g, k_norm_g, lambda_q1, lambda_k1,
                          lambda_q2, lambda_k2, subln_g, w_out, post_norm_g)
    nc, _ = build_nc(cfg)
    res = run_bass_kernel_spmd(nc, in_maps, core_ids=list(range(len(in_maps))))
    return np.stack([np.asarray(r["out"], np.float32) for r in res.results], axis=0)
```

```python
import math
import numpy as np
import concourse.bass as bass
import concourse.mybir as mybir
from concourse.bass_utils import run_bass_kernel_spmd

F32 = mybir.dt.float32
BF16 = mybir.dt.bfloat16
AF = mybir.ActivationFunctionType
ALU = mybir.AluOpType

NORM_EPS = 1e-6
ROPE_THETA = 10000.0
LAMBDA_INIT = 0.8 - 0.6 * math.exp(-0.3 * 0)


class Cfg:
    def __init__(self, D=2048, SEQ=2048):
        self.D, self.SEQ = D, SEQ
        self.NM = 16
        self.AW = D // 2
        self.AH = self.AW // 128
        self.AKV = 2
        self.AG = self.AH // self.AKV
        self.BW = D - self.AW
        self.BH = self.BW // 128
        self.DC = D // 128
        self.NT = SEQ // 128
        self.QC = min(512, SEQ)
        self.NQC = SEQ // self.QC
        self.QT = self.QC // 128
        self.L = SEQ + self.NM
        self.o_qa = 0
        self.o_ka = self.o_qa + self.AH * 128
        self.o_va = self.o_ka + self.AKV * 128
        self.o_ga = self.o_va + self.AKV * 128
        self.o_qb = self.o_ga + self.AW
        self.o_kb = self.o_qb + self.BH * 128
        self.o_vb = self.o_kb + self.BH * 128
        self.o_gb = self.o_vb + self.BH * 128
        self.INW = self.o_gb + self.BW
        self.chunks = []
        for hq in range(self.AH):
            if hq % self.AG == 0:
                g = hq // self.AG
                self.chunks.append(("akv", g))
            self.chunks.append(("aqg", hq))
        for h in range(self.BH):
            self.chunks.append(("bqk", h))
            self.chunks.append(("bvg", h))
        self.NCH = len(self.chunks)

    def chunk_cols(self, kind, i):
        r = np.arange(128)
        if kind == "akv":
            return np.concatenate([self.o_ka + 128 * i + r, self.o_va + 128 * i + r])
        if kind == "aqg":
            return np.concatenate([self.o_qa + 128 * i + r, self.o_ga + 128 * i + r])
        if kind == "bqk":
            return np.concatenate([self.o_qb + 128 * i + r, self.o_kb + 128 * i + r])
        if kind == "bvg":
            return np.concatenate([self.o_vb + 128 * i + r, self.o_gb + 128 * i + r])
        raise ValueError(kind)


class Buf:
    __slots__ = ("name", "wr", "rd", "sem", "cnt", "excl")

    def __init__(self, name):
        self.name = name
        self.excl = False
        self.wr = []
        self.rd = []
        self.sem = None
        self.cnt = 0


class Op:
    __slots__ = ("eng", "fn", "deps", "ticket", "needed", "dma", "idx")


ENGS = ("pe", "act", "dve", "pool", "sp")


class Prog:
    def __init__(self):
        self.ops = []
        self.dma_bufs = []

    def add(self, eng, fn, reads=(), writes=(), dma=None):
        op = Op()
        op.eng, op.fn, op.dma, op.needed, op.ticket = eng, fn, dma, False, None
        op.idx = len(self.ops)
        deps = []
        for b in reads:
            deps += b.wr
            if b.excl:
                deps += [r for r in b.rd if r.eng != eng]
        for b in writes:
            deps += b.wr
            deps += b.rd
        seen = set()
        op.deps = []
        for d in deps:
            if id(d) not in seen and d is not op:
                seen.add(id(d))
                if eng == "pe" and dma is None and d.eng == "pe" and d.dma is None:
                    continue
                op.deps.append(d)
        for d in op.deps:
            d.needed = True
        for b in reads:
            if dma is None:
                b.rd = [r for r in b.rd if not (r.dma is None and r.eng == eng)]
            b.rd.append(op)
        for b in writes:
            b.wr = [op]
            b.rd = []
        if dma is not None:
            if dma not in self.dma_bufs:
                self.dma_bufs.append(dma)
            dma.cnt += 16
            op.ticket = (dma, dma.cnt)
            op.needed = True
        self.ops.append(op)
        return op

    def barrier_bufs(self, bufs_all):
        pass

    def emit(self, nc, engines, eng_sems, dma_sems):
        cnt = {e: 0 for e in ENGS}
        for op in self.ops:
            if op.dma is None and op.needed:
                cnt[op.eng] += 1
                op.ticket = (op.eng, cnt[op.eng])

        def sem_of(t):
            return eng_sems[t[0]] if isinstance(t[0], str) else dma_sems[t[0].name]

        for e in ENGS:
            eng = engines[e]
            waited = {}
            for op in self.ops:
                if op.eng != e:
                    continue
                for d in op.deps:
                    if d.dma is None and d.eng == "pe" and e == "pe":
                        continue
                    key = d.ticket[0] if isinstance(d.ticket[0], str) else d.ticket[0].name
                    if waited.get(key, 0) >= d.ticket[1]:
                        continue
                    eng.wait_ge(sem_of(d.ticket), d.ticket[1])
                    waited[key] = d.ticket[1]
                inst = op.fn(eng)
                if op.dma is not None:
                    inst.then_inc(dma_sems[op.dma.name], 16)
                elif op.needed:
                    inst.then_inc(eng_sems[e], 1)
        return cnt


def build_nc(cfg: Cfg):
    c = cfg
    D, SEQ, DC, NT, QC, NQC, QT = c.D, c.SEQ, c.DC, c.NT, c.QC, c.NQC, c.QT
    NTT = NT + 1
    nc = bass.Bass("TRN2", target_bir_lowering=False)

    def din(name, shape, dt=F32):
        return nc.dram_tensor(name, list(shape), dt, kind="ExternalInput").ap()

    x_d = din("x", [SEQ, D])
    meta_d = din("meta", [c.NM, D])
    gpre_d = din("gpre", [128, DC])
    wch_d = din("wch", [c.NCH, 128, DC * 256])
    wout_d = din("wout", [128, DC * D])
    gvec_d = din("gvec", [128, 3 * 128])
    gpost_d = din("gpost", [128, D])
    lam_d = din("lamv", [128, 4 * 64])
    ident_d = din("ident", [128, 128])
    rope_d = din("rope", [NTT, 128, 256])
    ta_d = din("ta", [128, 2 * QC - 1])
    out_d = nc.dram_tensor("out", [SEQ, D], F32, kind="ExternalOutput").ap()

    P = Prog()
    from contextlib import ExitStack
    es = ExitStack()

    def sb(name, shape, dt):
        return es.enter_context(nc.sbuf_tensor(name, list(shape), dt))

    hn_raw = sb("hn_raw", [128, max(DC * c.L, DC * D)], BF16)
    yt_raw = sb("yt_raw", [128, max(DC * SEQ, 6 * D)], BF16)
    WS = 3
    wk_elems = max(WS * DC * 256 + (2 * SEQ + c.L + NTT * 129 + NT * 128), 2 * 2 * D * 2 + (2 * D if D > 512 else 0))
    wk_elems = (wk_elems + 63) // 64 * 64
    wk_raw = sb("wk_raw", [128, wk_elems], BF16)
    pT = [sb(f"pT{i}", [128, QC], BF16) for i in range(5)]
    osb = sb("osb", [128, 516], F32)
    ta_sb = sb("ta_sb", [128, 2 * QC - 1], F32)
    rope_sb = [sb(f"rope{i}", [128, 256], F32) for i in range(2)]
    ident = sb("ident_sb", [128, 128], BF16)
    gpre = sb("gpre_sb", [128, DC], F32)
    gvec = sb("gvec_sb", [128, 384], F32)
    sgs = sb("sgs", [128, 128], F32)
    lamv = sb("lamv_sb", [128, 256], F32)
    lamt = sb("lamt", [128, 8], F32)
    stat = sb("stat", [128, 3 * (NTT + 1)], F32)
    stat2 = sb("stat2", [128, 3 * (NTT + 1)], F32)
    NTMP = 4
    tA = [sb(f"tA{i}", [128, 128], F32) for i in range(NTMP)]
    tB = [sb(f"tB{i}", [128, 128], F32) for i in range(NTMP)]
    tC = [sb(f"tC{i}", [128, 128], F32) for i in range(NTMP)]
    tN = [sb(f"tN{i}", [128, 128], BF16) for i in range(NTMP)]
    tS = [sb(f"tS{i}", [128, 4], F32) for i in range(NTMP)]
    tY = [sb(f"tY{i}", [128, 128], BF16) for i in range(4)]
    tD = [sb(f"tD{i}", [128, 128], F32) for i in range(NTMP)]
    eA = [sb(f"eA{i}", [128, 128], F32) for i in range(4)]
    eS = [sb(f"eS{i}", [128, 4], F32) for i in range(4)]
    junk = sb("junk", [128, 128], BF16)
    o1buf = sb("o1buf", [128, QT * 128], F32)
    tE = [sb(f"tE{i}", [128, 4], F32) for i in range(4)]
    psA = es.enter_context(nc.psum_tensor("psA", [128, 2048], F32))
    psB = es.enter_context(nc.psum_tensor("psB", [128, 2048], F32))

    def bank(i):
        t = psA if i < 4 else psB
        return t[:, 512 * (i % 4):512 * (i % 4) + 512]

    hnT = hn_raw[:, 0:DC * c.L].rearrange("p (c t) -> p c t", c=DC)
    YT = yt_raw[:, 0:DC * SEQ].rearrange("p (c t) -> p c t", c=DC)
    wslot = [wk_raw[:, i * DC * 256:(i + 1) * DC * 256].rearrange("p (c n) -> p c n", c=DC) for i in range(WS)]
    o = WS * DC * 256
    qT = wk_raw[:, o:o + SEQ]; o += SEQ
    qT2 = wk_raw[:, o:o + SEQ]; o += SEQ
    kT = wk_raw[:, o:o + c.L]; o += c.L
    vU = wk_raw[:, o:o + NTT * 129].rearrange("p (t n) -> p t n", n=129); o += NTT * 129
    gate = wk_raw[:, o:o + NT * 128].rearrange("p (t n) -> p t n", n=128); o += NT * 128
    vAv = vU
    kTA = kT
    xs = [yt_raw[:, i * 2 * D:(i + 1) * 2 * D].bitcast(F32) for i in range(2)]
    xn = [yt_raw[:, 4 * D + i * D:4 * D + (i + 1) * D] for i in range(2)]
    xr = [wk_raw[:, i * 2 * D:(i + 1) * 2 * D].bitcast(F32) for i in range(2)]
    ot = [wk_raw[:, 4 * D + i * 2 * D:4 * D + (i + 1) * 2 * D].bitcast(F32) for i in range(2)]
    gpost = None
    wout = hn_raw[:, 0:DC * D].rearrange("p (c n) -> p c n", c=DC)

    def tok(t):
        if t < NT:
            return slice(128 * t, 128 * t + 128), 128
        return slice(SEQ, SEQ + c.NM), c.NM

    B = {}

    def buf(name):
        if name not in B:
            B[name] = Buf(name)
        return B[name]

    b_bank = [buf(f"bank{i}") for i in range(8)]
    for b_ in b_bank:
        b_.excl = True
    b_hn = [buf(f"hn{t}") for t in range(NTT)]
    b_yt = [buf(f"yt{t}") for t in range(NT)]
    b_w = [buf(f"w{i}") for i in range(WS)]
    b_const = buf("const")

    def dma(eng, out, in_, buf_, reads=(), writes=()):
        return P.add(eng, lambda e, out=out, in_=in_: e.dma_start(out=out, in_=in_), reads=reads, writes=writes, dma=buf_)

    b_c = [buf(f"c{i}") for i in range(6)]
    dma("sp", gpre[:], gpre_d, b_c[0], writes=[b_c[0]])
    dma("sp", gvec[:], gvec_d, b_c[1], writes=[b_c[1]])
    dma("sp", lamv[:], lam_d, b_c[2], writes=[b_c[2]])
    dma("sp", ta_sb[:], ta_d, b_c[3], writes=[b_c[3]])
    dma("pool", ident[:], ident_d, b_c[4], writes=[b_c[4]])
    b_gpre, b_gvec, b_lamv, b_ta, b_ident = b_c[0], b_c[1], b_c[2], b_c[3], b_c[4]

    b_vU = buf("vU"); b_vA = b_vU; b_lamt = buf("lamt"); b_sgs = buf("sgs")
    P.add("pool", lambda e: e.memset(vAv[:, :, 128:129], 1.0), writes=[b_vA])
    lv = lamv[:].rearrange("p (a n) -> p a n", a=4)
    P.add("dve", lambda e: e.tensor_tensor(out=tA[0][:, 0:64], in0=lv[:, 0, :], in1=lv[:, 1, :], op=ALU.mult), reads=[b_lamv], writes=[buf("tA0")])
    P.add("dve", lambda e: e.tensor_tensor(out=tA[0][:, 64:128], in0=lv[:, 2, :], in1=lv[:, 3, :], op=ALU.mult), reads=[b_lamv, buf("tA0")], writes=[buf("tA0")])
    P.add("dve", lambda e: e.tensor_reduce(out=lamt[:, 0:2], in_=tA[0][:].rearrange("p (a n) -> p a n", a=2), axis=mybir.AxisListType.X, op=ALU.add), reads=[buf("tA0")], writes=[b_lamt])
    P.add("act", lambda e: e.activation(out=lamt[:, 2:4], in_=lamt[:, 0:2], func=AF.Exp), reads=[b_lamt], writes=[b_lamt])
    P.add("dve", lambda e: e.tensor_tensor(out=lamt[:, 4:5], in0=lamt[:, 3:4], in1=lamt[:, 2:3], op=ALU.subtract), reads=[b_lamt], writes=[b_lamt])
    P.add("dve", lambda e: e.tensor_scalar(out=lamt[:, 5:6], in0=lamt[:, 4:5], scalar1=-LAMBDA_INIT, scalar2=None, op0=ALU.add), reads=[b_lamt], writes=[b_lamt])
    neglam = lamt[:, 5:6]
    P.add("dve", lambda e: e.tensor_scalar(out=sgs[:], in0=gvec[:, 256:384], scalar1=(1.0 - LAMBDA_INIT), scalar2=None, op0=ALU.mult), reads=[b_gvec], writes=[b_sgs])

    b_xs = [buf(f"xs{i}") for i in range(2)]
    b_xn = [buf(f"xn{i}") for i in range(2)]
    b_stat = buf("stat")
    b_ph0 = buf("ph0")
    HB = max(1, DC // 8)
    for t in range(NTT):
        sl, rows = tok(t)
        s = t % 2
        src = x_d[128 * t:128 * t + 128, :] if t < NT else meta_d
        dma("sp", xs[s][0:rows, :], src, b_xs[s], writes=[b_xs[s]])
        ss = stat[0:rows, 3 * t:3 * t + 1]; ln_ = stat[0:rows, 3 * t + 1:3 * t + 2]; rs = stat[0:rows, 3 * t + 2:3 * t + 3]
        P.add("act", lambda e, s=s, rows=rows, ss=ss: e.activation(out=xn[s][0:rows, :], in_=xs[s][0:rows, :], func=AF.Square, accum_out=ss),
              reads=[b_xs[s], b_ph0], writes=[b_xn[s], b_stat])
        P.add("act", lambda e, ss=ss, ln_=ln_: e.activation(out=ln_, in_=ss, func=AF.Ln, scale=1.0 / D, bias=NORM_EPS), reads=[b_stat], writes=[b_stat])
        P.add("act", lambda e, rs=rs, ln_=ln_: e.activation(out=rs, in_=ln_, func=AF.Exp, scale=-0.5), reads=[b_stat], writes=[b_stat])
        P.add("dve", lambda e, s=s, rows=rows, rs=rs: e.tensor_scalar(out=xn[s][0:rows, :], in0=xs[s][0:rows, :], scalar1=rs, scalar2=None, op0=ALU.mult),
              reads=[b_xs[s], b_stat, b_ph0], writes=[b_xn[s]])
        for hb in range(HB):
            bk = (2 * (t % 2) + hb) % 8 if HB <= 2 else hb % 8
            bkv = bank(bk).bitcast(BF16)
            nchb = min(8, DC - 8 * hb)
            for cc in range(nchb):
                ch = 8 * hb + cc
                P.add("pe", lambda e, bkv=bkv, cc=cc, ch=ch, s=s, rows=rows: e.transpose(bkv[:, cc * 128:cc * 128 + rows], xn[s][0:rows, ch * 128:(ch + 1) * 128], ident[0:rows, 0:rows]),
                      reads=[b_xn[s], b_ident, b_ph0], writes=[b_bank[bk]])
            srcv = bkv[:, 0:nchb * 128].rearrange("p (c r) -> p c r", c=nchb)[:, :, 0:rows]
            gv = gpre[:, 8 * hb:8 * hb + nchb].unsqueeze(2).to_broadcast([128, nchb, rows])
            P.add("dve", lambda e, srcv=srcv, gv=gv, hb=hb, nchb=nchb, sl=sl: e.tensor_tensor(out=hnT[:, 8 * hb:8 * hb + nchb, sl], in0=srcv, in1=gv, op=ALU.mult),
                  reads=[b_bank[bk], b_gpre], writes=[b_hn[t]])

    MARK = {}
    MARK['ph0'] = len(P.ops)
    wstate = {"next": 0}
    b_rope = [buf(f"rope{i}") for i in range(2)]
    rope_ctr = {"n": 0}
    tmp_ctr = {"n": 0}
    tn_ctr = {"n": 0}
    ty_ctr = {"n": 0}
    b_tY = [buf(f"tY{i}") for i in range(4)]
    b_tA = [buf(f"tA{i}") for i in range(NTMP)]; b_tB = [buf(f"tB{i}") for i in range(NTMP)]
    b_tC = [buf(f"tC{i}") for i in range(NTMP)]; b_tN = [buf(f"tN{i}") for i in range(NTMP)]
    b_tS = [buf(f"tS{i}") for i in range(NTMP)]
    b_junk = buf("junk")
    proj_banks = [0, 1, 2, 3, 4]
    pb_ctr = {"n": 0}
    tr_ctr = {"n": 0}
    b_qT = buf("qT"); b_qT2 = buf("qT2"); b_kT = buf("kT"); b_kTA = b_kT; b_gate = buf("gate")

    pend = []
    tk = {"n": 0, "seq": 0}

    def defer(fn, delay, keys=()):
        tk["seq"] += 1
        pend.append((tk["n"] + delay, tk["seq"], fn, tuple(keys)))
        pend.sort(key=lambda p: (p[0], p[1]))

    def acquire(key):
        while any(key in p[3] for p in pend):
            pend.pop(0)[2]()

    def tick():
        tk["n"] += 1
        while pend and pend[0][0] <= tk["n"]:
            pend.pop(0)[2]()

    def flush_all():
        while pend:
            pend.pop(0)[2]()

    def load_chunk(ci):
        s = ci % WS
        dst = wslot[s].rearrange("p c n -> p (c n)").rearrange("p (a b) -> p a b", b=512)
        srcv = wch_d[ci].rearrange("p (a b) -> p a b", b=512)
        dma("pool", dst, srcv, b_w[s], writes=[b_w[s]])

    PREFETCH = WS - 1
    for ci in range(min(PREFETCH, c.NCH)):
        load_chunk(ci)
    wstate["loaded"] = min(PREFETCH, c.NCH)

    def next_chunk():
        ci = wstate["next"]
        wstate["next"] += 1
        return ci % WS

    def prefetch_more():
        if wstate["loaded"] < c.NCH:
            load_chunk(wstate["loaded"])
            wstate["loaded"] += 1

    def proj_token_major(ws, t, ncols=256):
        sl, rows = tok(t)
        bk = proj_banks[pb_ctr["n"] % len(proj_banks)]; pb_ctr["n"] += 1
        for ch in range(DC):
            P.add("pe", lambda e, bk=bk, ch=ch, ws=ws, sl=sl, rows=rows: e.matmul(bank(bk)[0:rows, 0:ncols], hnT[:, ch, sl], wslot[ws][:, ch, 0:ncols], start=(ch == 0), stop=(ch == DC - 1)),
                  reads=[b_hn[t], b_w[ws]], writes=[b_bank[bk]])
        return bk

    def y_transpose(ybf, b_y, mix_chunk, tq, on_dve=False):
        k = tr_ctr["n"] % 4; tr_ctr["n"] += 1
        tv = bank(7).bitcast(BF16)[:, k * 256:k * 256 + 128]
        P.add("pe", lambda e: e.transpose(tv, ybf, ident[:, :]), reads=[b_y, b_ident], writes=[b_bank[7]])
        first = not ytr_ctr.get("started")
        ytr_ctr["started"] = True
        wr = [b_yt[tq]] + ([b_ph0, b_xs[0], b_xs[1], b_xn[0], b_xn[1]] if first else [])
        if on_dve:
            P.add("dve", lambda e: e.tensor_copy(out=YT[:, mix_chunk, 128 * tq:128 * tq + 128], in_=tv), reads=[b_bank[7]], writes=wr)
        else:
            P.add("act", lambda e: e.activation(out=YT[:, mix_chunk, 128 * tq:128 * tq + 128], in_=tv, func=AF.Copy), reads=[b_bank[7]], writes=wr)

    qk_ctr = {"n": 0}
    b_tD = [buf(f"tD{i}") for i in range(NTMP)]

    def qk_tile(bk, rows, t, gcol, dstT, b_dst, sl, other):
        i = qk_ctr["n"] % NTMP; qk_ctr["n"] += 1
        key = ("qk", i)
        acquire(key)
        if other == "gate":
            acquire(("gate",))
        acquire(("sg", i))
        r = rope_ctr["n"] % 2; rope_ctr["n"] += 1
        srcp = bank(bk)[0:rows, 0:128]
        osrc = bank(bk)[0:rows, 128:256]
        dma("sp", rope_sb[r][0:rows, :], rope_d[t, 0:rows, :], b_rope[r], writes=[b_rope[r]])
        ss = tS[i][0:rows, 0:1]; ln_ = tS[i][0:rows, 1:2]; rs = tS[i][0:rows, 2:3]
        xg = tA[i][0:rows, :]
        t1 = tB[i][0:rows, :]
        t2 = tD[i][0:rows, :]
        xb = tN[i][0:rows, :]

        def st2():
            if other == "gate":
                P.add("act", lambda e: e.activation(out=tC[i][:], in_=osrc, func=AF.Exp, scale=-1.0), reads=[b_bank[bk]], writes=[b_tC[i]])
            else:
                _, vview, b_v = other
                P.add("act", lambda e: e.activation(out=vview[0:rows, t, 0:128], in_=osrc, func=AF.Copy), reads=[b_bank[bk]], writes=[b_v])
            P.add("act", lambda e: e.activation(out=junk[0:rows, :], in_=srcp, func=AF.Square, accum_out=ss), reads=[b_bank[bk]], writes=[b_junk, b_tS[i]])
            P.add("act", lambda e: e.activation(out=ln_, in_=ss, func=AF.Ln, scale=1.0 / 128, bias=NORM_EPS), reads=[b_tS[i]], writes=[b_tS[i]])
            P.add("act", lambda e: e.activation(out=rs, in_=ln_, func=AF.Exp, scale=-0.5), reads=[b_tS[i]], writes=[b_tS[i]])
            P.add("dve", lambda e: e.tensor_tensor(out=xg, in0=srcp, in1=gvec[0:rows, gcol:gcol + 128], op=ALU.mult), reads=[b_bank[bk], b_gvec], writes=[b_tA[i]])

        def st3():
            if other == "gate":
                P.add("act", lambda e: e.activation(out=tC[i][:], in_=tC[i][:], func=AF.Ln, bias=1.0), reads=[b_tC[i]], writes=[b_tC[i]])
                P.add("act", lambda e: e.activation(out=tC[i][:], in_=tC[i][:], func=AF.Exp, scale=-1.0), reads=[b_tC[i]], writes=[b_tC[i]])
            P.add("dve", lambda e: e.tensor_tensor(out=t1, in0=xg, in1=rope_sb[r][0:rows, 0:128], op=ALU.mult), reads=[b_tA[i], b_rope[r]], writes=[b_tB[i]])
            xsw = xg.rearrange("p (a s j) -> p a s j", a=2, s=2)[:, :, ::-1, :]
            P.add("pool", lambda e: e.tensor_tensor(out=t2.rearrange("p (a s j) -> p a s j", a=2, s=2), in0=xsw, in1=rope_sb[r][0:rows, 128:256].rearrange("p (a s j) -> p a s j", a=2, s=2), op=ALU.mult),
                  reads=[b_tA[i], b_rope[r]], writes=[b_tD[i]])

        def st4():
            if other == "gate":
                P.add("dve", lambda e: e.tensor_tensor(out=gate[:, t, :], in0=osrc, in1=tC[i][:], op=ALU.mult), reads=[b_bank[bk], b_tC[i]], writes=[b_gate])
            P.add("pool", lambda e: e.tensor_tensor(out=t1, in0=t1, in1=t2, op=ALU.add), reads=[b_tB[i], b_tD[i]], writes=[b_tB[i]])
            P.add("dve", lambda e: e.tensor_scalar(out=xb, in0=t1, scalar1=rs, scalar2=None, op0=ALU.mult), reads=[b_tB[i], b_tS[i]], writes=[b_tN[i]])

        def st5():
            k = tr_ctr["n"] % 4; tr_ctr["n"] += 1
            tv = bank(7).bitcast(BF16)[:, k * 256:k * 256 + rows]
            P.add("pe", lambda e: e.transpose(tv, xb, ident[0:rows, 0:rows]), reads=[b_tN[i], b_ident], writes=[b_bank[7]])
            P.add("dve", lambda e: e.tensor_copy(out=dstT[:, sl], in_=tv), reads=[b_bank[7]], writes=[b_dst])

        st2()
        defer(st3, 1, [key])
        defer(st4, 2, [key])
        defer(st5, 3, [key])
        tick()

    sg_ctr = {"n": 0}

    def silu_gate(bk, col0, t):
        i = sg_ctr["n"] % NTMP; sg_ctr["n"] += 1
        key = ("sg", i)
        acquire(key)
        acquire(("qk", i))
        acquire(("gate",))
        gsrc = bank(bk)[:, col0:col0 + 128]
        P.add("act", lambda e: e.activation(out=tC[i][:], in_=gsrc, func=AF.Exp, scale=-1.0), reads=[b_bank[bk]], writes=[b_tC[i]])

        def st():
            P.add("act", lambda e: e.activation(out=tC[i][:], in_=tC[i][:], func=AF.Ln, bias=1.0), reads=[b_tC[i]], writes=[b_tC[i]])
            P.add("act", lambda e: e.activation(out=tC[i][:], in_=tC[i][:], func=AF.Exp, scale=-1.0), reads=[b_tC[i]], writes=[b_tC[i]])

        def st_b():
            P.add("dve", lambda e: e.tensor_tensor(out=gate[:, t, :], in0=gsrc, in1=tC[i][:], op=ALU.mult), reads=[b_bank[bk], b_tC[i]], writes=[b_gate])
        defer(st, 1, [key])
        defer(st_b, 2, [key])

    SCALE_A = 1.0 / math.sqrt(128.0)
    NPT = len(pT)
    b_pT = [buf(f"pT{i}") for i in range(NPT)]
    b_o1s = [buf(f"o1_{j}") for j in range(QT)]
    b_osbs = [buf("osb0"), buf("osb1")]
    e_ctr = {"n": 0}
    b_eA = [buf(f"eA{i}") for i in range(4)]
    b_eS = [buf(f"eS{i}") for i in range(4)]
    b_tE = [buf(f"tE{i}") for i in range(4)]
    te_ctr = {"n": 0}
    ytr_ctr = {}
    SBANKS = [0, 1, 2, 3, 4]
    OBANKS = [5, 6]

    def attention(kq_aps, b_k, b_qs, v_ap, b_v, tiles_fn, epilogue, dve_heavy=False):
        nm = len(kq_aps)
        seq = [(cq, m, kt) for cq in range(NQC) for m in range(nm) for kt in range(NTT)]
        LOOK = len(SBANKS) - 1
        issued = {}

        def issue_S(n):
            cq, m, kt = seq[n]
            sl, rows = tok(kt)
            bk = SBANKS[n % len(SBANKS)]
            kT_ap, qT_ap = kq_aps[m]
            P.add("pe", lambda e: e.matmul(bank(bk)[0:rows, 0:QC], kT_ap[:, sl], qT_ap[:, cq * QC:(cq + 1) * QC], start=True, stop=True),
                  reads=[b_k, b_qs[m]], writes=[b_bank[bk]])
            issued[n] = bk

        LA = 3

        def tile_ops(n):
            cq, m, kt = seq[n]
            sl, rows = tok(kt)
            pi = n % NPT
            tiles_fn(m, issued[n], rows, cq, kt, pT[pi][0:rows, :], b_pT[pi])

        for n in range(min(LOOK, len(seq))):
            issue_S(n)
        for n in range(min(LA, len(seq))):
            tile_ops(n)
        for n, (cq, m, kt) in enumerate(seq):
            sl, rows = tok(kt)
            pi = n % NPT
            if n + LA < len(seq):
                tile_ops(n + LA)
            for j in range(QT):
                ob = OBANKS[j // 2]
                oc = (j % 2) * 129
                first_in_bank = (kt == 0 and (j % 2 == 0))
                P.add("pe", lambda e, ob=ob, oc=oc, j=j, rows=rows, kt=kt, pi=pi, fib=first_in_bank: e.matmul(
                    bank(ob)[:, oc:oc + 129], pT[pi][0:rows, j * 128:(j + 1) * 128], v_ap[0:rows, kt, :],
                    start=fib, stop=(kt == NTT - 1), skip_group_check=True),
                    reads=[b_pT[pi], b_v], writes=[b_bank[ob]])
            if n + LOOK < len(seq):
                issue_S(n + LOOK)
            if kt == NTT - 1:
                acquire(("osb",))
                nob = (QT + 1) // 2
                for bi in range(nob):
                    w = 129 * min(2, QT - 2 * bi)
                    if bi == 0:
                        P.add("act", lambda e, bi=bi, w=w: e.activation(out=osb[:, bi * 258:bi * 258 + w], in_=bank(OBANKS[bi])[:, 0:w], func=AF.Copy),
                              reads=[b_bank[OBANKS[bi]]], writes=[b_osbs[bi]])
                    elif dve_heavy:
                        P.add("act", lambda e, bi=bi, w=w: e.activation(out=osb[:, bi * 258:bi * 258 + w], in_=bank(OBANKS[bi])[:, 0:w], func=AF.Copy),
                              reads=[b_bank[OBANKS[bi]]], writes=[b_osbs[bi]])
                    else:
                        P.add("dve", lambda e, bi=bi, w=w: e.tensor_copy(out=osb[:, bi * 258:bi * 258 + w], in_=bank(OBANKS[bi])[:, 0:w]),
                              reads=[b_bank[OBANKS[bi]]], writes=[b_osbs[bi]])
                for j in range(QT):
                    epilogue(m, cq, j, osb[:, j * 129:(j + 1) * 129], b_osbs[j // 2])
            tick()

    for hq in range(c.AH):
        g = hq // c.AG
        if hq % c.AG == 0:
            ws = next_chunk()
            for t in range(NTT):
                sl, rows = tok(t)
                bk = proj_token_major(ws, t)
                qk_tile(bk, rows, t, 128, kTA, b_kTA, sl, ("v", vAv, b_vA))
            prefetch_more()
        ws = next_chunk()
        for t in range(NT):
            sl, rows = tok(t)
            bk = proj_token_major(ws, t)
            qk_tile(bk, rows, t, 0, qT, b_qT, sl, "gate")
        prefetch_more()
        flush_all()

        def tiles_A(m, bk, rows, cq, kt, pdst, b_p):
            P.add("act", lambda e: e.activation(out=pdst, in_=bank(bk)[0:rows, 0:QC], func=AF.Exp, scale=SCALE_A), reads=[b_bank[bk]], writes=[b_p])

        def epi_A(m, cq, j, ops, b_ops, hq=hq):
            tq = cq * QT + j
            i = e_ctr["n"] % 4; e_ctr["n"] += 1
            key = ("e", i)
            acquire(key)

            def e2():
                P.add("dve", lambda e: e.reciprocal(out=tE[i][:, 0:1], in_=ops[:, 128:129]), reads=[b_ops], writes=[b_tE[i]])
                P.add("dve", lambda e: e.scalar_tensor_tensor(out=tY[i][:], in0=ops[:, 0:128], scalar=tE[i][:, 0:1], in1=gate[:, tq, :], op0=ALU.mult, op1=ALU.mult),
                      reads=[b_ops, b_tE[i], b_gate], writes=[b_tY[i]])
            defer(e2, 1 + j, [key, ("osb",), ("gate",)])
            defer(lambda: y_transpose(tY[i][:], b_tY[i], hq, tq, on_dve=True), 3 + j, [key])

        attention([(kTA, qT)], b_kTA, [b_qT], vAv, b_vA, tiles_A, epi_A)

    MARK['A'] = len(P.ops)
    WO = {}

    def load_wout():
        NWD = DC
        cpd = 1
        b_wo = [buf(f"wo{i}") for i in range(NWD)]
        WO['b_wo'] = b_wo; WO['cpd'] = cpd
        for i in range(NWD):
            dstv = wout[:, i, :].rearrange("p (a b) -> p a b", b=512)
            srcv = wout_d[:, i * D:(i + 1) * D].rearrange("p (a b) -> p a b", b=512)
            defer(lambda dstv=dstv, srcv=srcv, i=i: dma("pool", dstv, srcv, b_wo[i], reads=[], writes=[b_wo[i]] + b_hn), 1 + 6 * i)

    P.add("pool", lambda e: e.memset(qT[64:128, :], 0.0), writes=[b_qT])
    P.add("pool", lambda e: e.memset(qT2[0:64, :], 0.0), writes=[b_qT2])
    YW = 2 * QC - 1
    Y0 = QC - 128
    WL = Y0 + 127
    for h in range(c.BH):
        slope = 2.0 ** (-8.0 * (h + 1) / c.BH)
        ws = next_chunk()
        ncht = NQC
        for cq in range(NQC):
            bk = proj_banks[pb_ctr["n"] % len(proj_banks)]; pb_ctr["n"] += 1
            for ch in range(DC):
                P.add("pe", lambda e, bk=bk, ch=ch, ws=ws, cq=cq: e.matmul(bank(bk)[:, 0:QC], wslot[ws][:, ch, 0:128], hnT[:, ch, cq * QC:(cq + 1) * QC], start=(ch == 0), stop=(ch == DC - 1)),
                      reads=[b_hn[tt] for tt in range(cq * QT, (cq + 1) * QT)] + [b_w[ws]], writes=[b_bank[bk]])
            P.add("act", lambda e, bk=bk, cq=cq: e.activation(out=qT[0:64, cq * QC:(cq + 1) * QC], in_=bank(bk)[0:64, 0:QC], func=AF.Copy, scale=0.125), reads=[b_bank[bk]], writes=[b_qT])
            P.add("act", lambda e, bk=bk, cq=cq: e.activation(out=qT2[64:128, cq * QC:(cq + 1) * QC], in_=bank(bk)[64:128, 0:QC], func=AF.Copy, scale=0.125), reads=[b_bank[bk]], writes=[b_qT2])
            tick()
        for cq in range(NQC + 1):
            bk = proj_banks[pb_ctr["n"] % len(proj_banks)]; pb_ctr["n"] += 1
            csl = slice(cq * QC, (cq + 1) * QC) if cq < NQC else slice(SEQ, SEQ + c.NM)
            n = QC if cq < NQC else c.NM
            rd = [b_hn[tt] for tt in range(cq * QT, (cq + 1) * QT)] if cq < NQC else [b_hn[NT]]
            for ch in range(DC):
                P.add("pe", lambda e, bk=bk, ch=ch, ws=ws, csl=csl, n=n: e.matmul(bank(bk)[:, 0:n], wslot[ws][:, ch, 128:256], hnT[:, ch, csl], start=(ch == 0), stop=(ch == DC - 1)),
                      reads=rd + [b_w[ws]], writes=[b_bank[bk]])
            P.add("dve", lambda e, bk=bk, csl=csl, n=n: e.tensor_copy(out=kT[:, csl], in_=bank(bk)[:, 0:n]), reads=[b_bank[bk]], writes=[b_kT])
            tick()
        prefetch_more()
        ws = next_chunk()
        for t in range(NTT):
            sl, rows = tok(t)
            ncols = 256 if t < NT else 128
            bk = proj_token_major(ws, t, ncols)
            P.add("act", lambda e, bk=bk, rows=rows, t=t: e.activation(out=vU[0:rows, t, 0:128], in_=bank(bk)[0:rows, 0:128], func=AF.Copy), reads=[b_bank[bk]], writes=[b_vU])
            if t < NT:
                silu_gate(bk, 128, t)
            tick()
        prefetch_more()
        flush_all()
        if h == c.BH - 1:
            load_wout()

        def tiles_B(m, bk, rows, cq, kt, pdst, b_p, slope=slope):
            if kt < NT:
                delta = QC * cq - 128 * kt
                if -QC < delta < 128:
                    j = (-delta) // 128
                    ws_ = Y0 - 128 * j
                    coef, cst = -slope, 0.0
                elif delta >= 128:
                    ws_ = WL
                    coef, cst = -slope, -slope * (delta - 127)
                else:
                    ws_ = WL
                    coef, cst = slope, slope * (delta - 127)
            else:
                ws_ = WL
                delta = QC * cq + c.NM
                coef, cst = -slope, -slope * (delta - 127)
            sv = bank(bk)[0:rows, 0:QC]
            P.add("dve", lambda e: e.scalar_tensor_tensor(out=sv, in0=ta_sb[0:rows, ws_:ws_ + QC], scalar=coef, in1=sv, op0=ALU.mult, op1=ALU.add),
                  reads=[b_bank[bk], b_ta], writes=[b_bank[bk]])
            P.add("act", lambda e: e.activation(out=pdst, in_=sv, func=AF.Exp, bias=float(cst)), reads=[b_bank[bk]], writes=[b_p])

        def epi_B(m, cq, j, ops, b_ops, h=h):
            tq = cq * QT + j
            i = e_ctr["n"] % 4; e_ctr["n"] += 1
            key = ("e", i)
            acquire(key)
            od = eA[i][:]
            ss = eS[i][:, 0:1]; ln_ = eS[i][:, 1:2]; rs = eS[i][:, 2:3]

            def e2():
                P.add("dve", lambda e: e.reciprocal(out=tE[i][:, 0:1], in_=ops[:, 128:129]), reads=[b_ops], writes=[b_tE[i]])
                if m == 0:
                    P.add("pool", lambda e: e.tensor_scalar(out=o1buf[:, j * 128:(j + 1) * 128], in0=ops[:, 0:128], scalar1=tE[i][:, 0:1], scalar2=1.0, op0=ALU.mult, op1=ALU.mult),
                          reads=[b_ops, b_tE[i]], writes=[b_o1s[j]])
                    return
                P.add("dve", lambda e: e.tensor_tensor(out=tE[i][:, 1:2], in0=tE[i][:, 0:1], in1=neglam, op=ALU.mult), reads=[b_tE[i], b_lamt], writes=[b_tE[i]])
                P.add("pool", lambda e: e.tensor_scalar(out=od, in0=ops[:, 0:128], scalar1=tE[i][:, 1:2], scalar2=1.0, op0=ALU.mult, op1=ALU.mult),
                      reads=[b_ops, b_tE[i]], writes=[b_eA[i]])
                P.add("pool", lambda e: e.tensor_tensor(out=od, in0=od, in1=o1buf[:, j * 128:(j + 1) * 128], op=ALU.add),
                      reads=[b_eA[i], b_o1s[j]], writes=[b_eA[i]])

            def e3():
                P.add("act", lambda e: e.activation(out=junk[:], in_=od, func=AF.Square, accum_out=ss), reads=[b_eA[i]], writes=[b_junk, b_eS[i]])
                P.add("act", lambda e: e.activation(out=ln_, in_=ss, func=AF.Ln, scale=1.0 / 128, bias=NORM_EPS), reads=[b_eS[i]], writes=[b_eS[i]])
                P.add("act", lambda e: e.activation(out=rs, in_=ln_, func=AF.Exp, scale=-0.5), reads=[b_eS[i]], writes=[b_eS[i]])

            def e4():
                P.add("pool", lambda e: e.tensor_tensor(out=od, in0=od, in1=sgs[:], op=ALU.mult), reads=[b_eA[i], b_sgs], writes=[b_eA[i]])
                P.add("pool", lambda e: e.tensor_scalar(out=od, in0=od, scalar1=rs, scalar2=1.0, op0=ALU.mult, op1=ALU.mult),
                      reads=[b_eA[i], b_eS[i]], writes=[b_eA[i]])
                P.add("pool", lambda e: e.tensor_tensor(out=tY[i][:], in0=od, in1=gate[:, tq, :], op=ALU.mult),
                      reads=[b_eA[i], b_gate], writes=[b_tY[i]])

            defer(e2, 1 + j, [key, ("osb",)])
            if m == 1:
                defer(e3, 4 + j, [key])
                defer(e4, 7 + j, [key, ("gate",)])
                defer(lambda: y_transpose(tY[i][:], b_tY[i], c.AH + h, tq, on_dve=True), 10 + j, [key])

        attention([(kT, qT), (kT, qT2)], b_kT, [b_qT, b_qT2], vU, b_vU, tiles_B, epi_B, dve_heavy=False)

    flush_all()
    MARK['B'] = len(P.ops)
    gpost_sb = None
    b_xr = [buf(f"xr{i}") for i in range(2)]
    b_ot = [buf(f"ot{i}") for i in range(2)]
    b_unit = [b_w[i] for i in range(WS)] + [b_qT, b_qT2, b_kT, b_vU, b_gate]
    b_gpost = buf("gpost")
    gpost_sb = sb("gpost_sb", [128, D], F32) if D <= 512 else None
    if gpost_sb is None:
        off = 8 * D
        assert wk_elems >= off + 2 * D, (wk_elems, off + 2 * D)
        gpost_v = wk_raw[:, off:off + 2 * D].bitcast(F32)
    else:
        gpost_v = gpost_sb[:]
    dma("sp", gpost_v, gpost_d, b_gpost, writes=[b_gpost] + b_unit)
    NB = D // 512 if D >= 512 else 1
    BW_ = min(512, D)
    for t in range(NT):
        s = t % 2
        sl, rows = tok(t)
        dma("sp", xr[s][:], x_d[128 * t:128 * t + 128, :], b_xr[s], writes=[b_xr[s]] + (b_unit if t < 2 else []))
        base = 4 * (t % 2)
        if NB * 1 > 4:
            raise NotImplementedError
        for nb in range(NB):
            bk = base + nb
            for ch in range(DC):
                P.add("pe", lambda e, bk=bk, ch=ch, nb=nb, sl=sl: e.matmul(bank(bk)[:, 0:BW_], YT[:, ch, sl], wout[:, ch, nb * BW_:(nb + 1) * BW_], start=(ch == 0), stop=(ch == DC - 1)),
                      reads=[b_yt[t], WO['b_wo'][ch // WO['cpd']]], writes=[b_bank[bk]])
        pst = (psA if base == 0 else psB)[:, 0:D] if D >= 512 else (psA if base == 0 else psB)[:, 0:D]
        bks = [b_bank[base + nb] for nb in range(NB)]
        ss = stat2[:, 3 * t:3 * t + 1]; ln_ = stat2[:, 3 * t + 1:3 * t + 2]; rs = stat2[:, 3 * t + 2:3 * t + 3]
        b_st2 = buf(f"st2_{t % 2}")
        P.add("act", lambda e, s=s, pst=pst, ss=ss: e.activation(out=ot[s][:], in_=pst, func=AF.Square, accum_out=ss), reads=bks, writes=[b_ot[s], b_st2] + (b_unit if t < 2 else []))
        P.add("act", lambda e, ss=ss, ln_=ln_: e.activation(out=ln_, in_=ss, func=AF.Ln, scale=1.0 / D, bias=NORM_EPS), reads=[b_st2], writes=[b_st2])
        P.add("act", lambda e, rs=rs, ln_=ln_: e.activation(out=rs, in_=ln_, func=AF.Exp, scale=-0.5), reads=[b_st2], writes=[b_st2])
        P.add("dve", lambda e, s=s, pst=pst: e.tensor_tensor(out=ot[s][:], in0=pst, in1=gpost_v, op=ALU.mult), reads=bks + [b_gpost], writes=[b_ot[s]])
        P.add("dve", lambda e, s=s, rs=rs: e.scalar_tensor_tensor(out=ot[s][:], in0=ot[s][:], scalar=rs, in1=xr[s][:], op0=ALU.mult, op1=ALU.add),
              reads=[b_ot[s], b_st2, b_xr[s]], writes=[b_ot[s]])
        dma("sp", out_d[128 * t:128 * t + 128, :], ot[s][:], b_ot[s], reads=[b_ot[s]], writes=[buf(f"outd{t}")])
    fin_reads = [buf(f"outd{t}") for t in range(NT)]
    P.add("sp", lambda e: e.nop(), reads=fin_reads)

    MARK['end'] = len(P.ops)
    import os as _os
    _tr = _os.environ.get('KTRUNC')
    if _tr:
        P.ops = P.ops[:(MARK[_tr] if _tr in MARK else int(_tr))]
        print('TRUNC', MARK, len(P.ops))
    eng_sems = {e: es.enter_context(nc.semaphore(f"s_{e}")) for e in ENGS}
    dma_sems = {b.name: es.enter_context(nc.semaphore(f"d_{b.name}")) for b in P.dma_bufs}
    block = es.enter_context(nc.Block())
    engines = {}

    def make(ename):
        def body(eng):
            engines_local = {ename: eng}
            emit_one(ename, eng)
        return body

    cnt = {e: 0 for e in ENGS}
    for op in P.ops:
        if op.dma is None and op.needed:
            cnt[op.eng] += 1
            op.ticket = (op.eng, cnt[op.eng])

    def sem_of(t):
        return eng_sems[t[0]] if isinstance(t[0], str) else dma_sems[t[0].name]

    def emit_one(e, eng):
        waited = {}
        for op in P.ops:
            if op.eng != e:
                continue
            for d in op.deps:
                if d.dma is None and d.eng == "pe" and e == "pe":
                    continue
                key = d.ticket[0] if isinstance(d.ticket[0], str) else d.ticket[0].name
                if waited.get(key, 0) >= d.ticket[1]:
                    continue
                eng.wait_ge(sem_of(d.ticket), d.ticket[1])
                waited[key] = d.ticket[1]
            inst = op.fn(eng)
            if op.dma is not None:
                inst.then_inc(dma_sems[op.dma.name], 16)
            elif op.needed:
                inst.then_inc(eng_sems[e], 1)

    @block.tensor
    def _(eng):
        emit_one("pe", eng)

    @block.scalar
    def _(eng):
        emit_one("act", eng)

    @block.vector
    def _(eng):
        emit_one("dve", eng)

    @block.gpsimd
    def _(eng):
        emit_one("pool", eng)

    @block.sync
    def _(eng):
        emit_one("sp", eng)

    es.close()
    return nc, cnt


def host_consts(cfg: Cfg):
    c = cfg
    NTT = c.NT + 1
    inv_freq = ROPE_THETA ** (-np.arange(0, 64, 2, dtype=np.float64) / 64.0)
    rope = np.zeros((NTT, 128, 256), np.float32)
    n = np.arange(c.SEQ)
    gr = (n // 64).astype(np.float64)[:, None] * inv_freq[None, :]
    gc = (n % 64).astype(np.float64)[:, None] * inv_freq[None, :]
    cr, sr, cc, sc = np.cos(gr), np.sin(gr), np.cos(gc), np.sin(gc)
    COS = np.concatenate([cr, cr, cc, cc], axis=1).astype(np.float32)
    SINS = np.concatenate([-sr, sr, -sc, sc], axis=1).astype(np.float32)
    rope[:c.NT, :, 0:128] = COS.reshape(c.NT, 128, 128)
    rope[:c.NT, :, 128:256] = SINS.reshape(c.NT, 128, 128)
    rope[c.NT, :, 0:128] = 1.0
    QC = c.QC
    Y0 = QC - 128
    y = np.arange(2 * QC - 1)[None, :]
    k = np.arange(128)[:, None]
    ta = np.abs(y - Y0 - k).astype(np.float32)
    ident = np.eye(128, dtype=np.float32)
    return rope, ta, ident


def host_inputs(cfg: Cfg, x, meta_tokens, pre_norm_g, w_in, q_norm_g, k_norm_g, lambda_q1, lambda_k1,
                lambda_q2, lambda_k2, subln_g, w_out, post_norm_g):
    c = cfg
    f = np.float32
    rope, ta, ident = host_consts(c)
    w_in0 = np.asarray(w_in[0], f)
    wr = w_in0.reshape(c.DC, 128, c.INW)
    wch = np.empty((c.NCH, 128, c.DC * 256), f)
    for ci, (kind, i) in enumerate(c.chunks):
        cols = c.chunk_cols(kind, i)
        wch[ci] = wr[:, :, cols].transpose(1, 0, 2).reshape(128, c.DC * 256)
    wout = np.ascontiguousarray(np.asarray(w_out[0], f).reshape(c.DC, 128, c.D).transpose(1, 0, 2).reshape(128, c.DC * c.D))
    gpre = np.ascontiguousarray(np.asarray(pre_norm_g[0], f).reshape(c.DC, 128).T)
    gvec = np.ascontiguousarray(np.broadcast_to(np.concatenate([np.asarray(q_norm_g[0], f), np.asarray(k_norm_g[0], f), np.asarray(subln_g[0], f)])[None, :], (128, 384)))
    gpost = np.ascontiguousarray(np.broadcast_to(np.asarray(post_norm_g[0], f)[None, :], (128, c.D)))
    lamv = np.ascontiguousarray(np.broadcast_to(np.concatenate([np.asarray(lambda_q1[0], f), np.asarray(lambda_k1[0], f), np.asarray(lambda_q2[0], f), np.asarray(lambda_k2[0], f)])[None, :], (128, 256)))
    shared = {"meta": np.ascontiguousarray(np.asarray(meta_tokens, f)), "gpre": gpre, "wch": wch, "wout": wout, "gvec": gvec,
              "gpost": gpost, "lamv": lamv, "ident": ident, "rope": rope, "ta": ta}
    xs = np.asarray(x, f)
    return [dict(shared, x=np.ascontiguousarray(xs[b])) for b in range(xs.shape[0])]


_NC_CACHE = {}


def kernel(x, meta_tokens, pre_norm_g, w_in, q_norm_g, k_norm_g, lambda_q1, lambda_k1,
           lambda_q2, lambda_k2, subln_g, w_out, post_norm_g):
    cfg = Cfg(2048, 2048)
    in_maps = host_inputs(cfg, x, meta_tokens, pre_norm_g, w_in, q_norm_g, k_norm_g, lambda_q1, lambda_k1,
                          lambda_q2, lambda_k2, subln_g, w_out, post_norm_g)
    nc, _ = build_nc(cfg)
    res = run_bass_kernel_spmd(nc, in_maps, core_ids=list(range(len(in_maps))))
    return np.stack([np.asarray(r["out"], np.float32) for r in res.results], axis=0)
```

```python
import math
import numpy as np
import concourse.bass as bass
import concourse.mybir as mybir
from concourse.bass_utils import run_bass_kernel_spmd

F32 = mybir.dt.float32
BF16 = mybir.dt.bfloat16
AF = mybir.ActivationFunctionType
ALU = mybir.AluOpType

NORM_EPS = 1e-6
ROPE_THETA = 10000.0
LAMBDA_INIT = 0.8 - 0.6 * math.exp(-0.3 * 0)


class Cfg:
    def __init__(self, D=2048, SEQ=2048):
        self.D, self.SEQ = D, SEQ
        self.NM = 16
        self.AW = D // 2
        self.AH = self.AW // 128
        self.AKV = 2
        self.AG = self.AH // self.AKV
        self.BW = D - self.AW
        self.BH = self.BW // 128
        self.DC = D // 128
        self.NT = SEQ // 128
        self.QC = min(512, SEQ)
        self.NQC = SEQ // self.QC
        self.QT = self.QC // 128
        self.L = SEQ + self.NM
        self.o_qa = 0
        self.o_ka = self.o_qa + self.AH * 128
        self.o_va = self.o_ka + self.AKV * 128
        self.o_ga = self.o_va + self.AKV * 128
        self.o_qb = self.o_ga + self.AW
        self.o_kb = self.o_qb + self.BH * 128
        self.o_vb = self.o_kb + self.BH * 128
        self.o_gb = self.o_vb + self.BH * 128
        self.INW = self.o_gb + self.BW
        self.chunks = []
        for hq in range(self.AH):
            if hq % self.AG == 0:
                g = hq // self.AG
                self.chunks.append(("akv", g))
            self.chunks.append(("aqg", hq))
        for h in range(self.BH):
            self.chunks.append(("bqk", h))
            self.chunks.append(("bvg", h))
        self.NCH = len(self.chunks)

    def chunk_cols(self, kind, i):
        r = np.arange(128)
        if kind == "akv":
            return np.concatenate([self.o_ka + 128 * i + r, self.o_va + 128 * i + r])
        if kind == "aqg":
            return np.concatenate([self.o_qa + 128 * i + r, self.o_ga + 128 * i + r])
        if kind == "bqk":
            return np.concatenate([self.o_qb + 128 * i + r, self.o_kb + 128 * i + r])
        if kind == "bvg":
            return np.concatenate([self.o_vb + 128 * i + r, self.o_gb + 128 * i + r])
        raise ValueError(kind)


class Buf:
    __slots__ = ("name", "wr", "rd", "sem", "cnt", "excl")

    def __init__(self, name):
        self.name = name
        self.excl = False
        self.wr = []
        self.rd = []
        self.sem = None
        self.cnt = 0


class Op:
    __slots__ = ("eng", "fn", "deps", "ticket", "needed", "dma", "idx")


ENGS = ("pe", "act", "dve", "pool", "sp")


class Prog:
    def __init__(self):
        self.ops = []
        self.dma_bufs = []

    def add(self, eng, fn, reads=(), writes=(), dma=None):
        op = Op()
        op.eng, op.fn, op.dma, op.needed, op.ticket = eng, fn, dma, False, None
        op.idx = len(self.ops)
        deps = []
        for b in reads:
            deps += b.wr
            if b.excl:
                deps += [r for r in b.rd if r.eng != eng]
        for b in writes:
            deps += b.wr
            deps += b.rd
        seen = set()
        op.deps = []
        for d in deps:
            if id(d) not in seen and d is not op:
                seen.add(id(d))
                if eng == "pe" and dma is None and d.eng == "pe" and d.dma is None:
                    continue
                op.deps.append(d)
        for d in op.deps:
            d.needed = True
        for b in reads:
            if dma is None:
                b.rd = [r for r in b.rd if not (r.dma is None and r.eng == eng)]
            b.rd.append(op)
        for b in writes:
            b.wr = [op]
            b.rd = []
        if dma is not None:
            if dma not in self.dma_bufs:
                self.dma_bufs.append(dma)
            dma.cnt += 16
            op.ticket = (dma, dma.cnt)
            op.needed = True
        self.ops.append(op)
        return op

    def barrier_bufs(self, bufs_all):
        pass

    def emit(self, nc, engines, eng_sems, dma_sems):
        cnt = {e: 0 for e in ENGS}
        for op in self.ops:
            if op.dma is None and op.needed:
                cnt[op.eng] += 1
                op.ticket = (op.eng, cnt[op.eng])

        def sem_of(t):
            return eng_sems[t[0]] if isinstance(t[0], str) else dma_sems[t[0].name]

        for e in ENGS:
            eng = engines[e]
            waited = {}
            for op in self.ops:
                if op.eng != e:
                    continue
                for d in op.deps:
                    if d.dma is None and d.eng == "pe" and e == "pe":
                        continue
                    key = d.ticket[0] if isinstance(d.ticket[0], str) else d.ticket[0].name
                    if waited.get(key, 0) >= d.ticket[1]:
                        continue
                    eng.wait_ge(sem_of(d.ticket), d.ticket[1])
                    waited[key] = d.ticket[1]
                inst = op.fn(eng)
                if op.dma is not None:
                    inst.then_inc(dma_sems[op.dma.name], 16)
                elif op.needed:
                    inst.then_inc(eng_sems[e], 1)
        return cnt


def build_nc(cfg: Cfg):
    c = cfg
    D, SEQ, DC, NT, QC, NQC, QT = c.D, c.SEQ, c.DC, c.NT, c.QC, c.NQC, c.QT
    NTT = NT + 1
    nc = bass.Bass("TRN2", target_bir_lowering=False)

    def din(name, shape, dt=F32):
        return nc.dram_tensor(name, list(shape), dt, kind="ExternalInput").ap()

    x_d = din("x", [SEQ, D])
    meta_d = din("meta", [c.NM, D])
    gpre_d = din("gpre", [128, DC])
    wch_d = din("wch", [c.NCH, 128, DC * 256])
    wout_d = din("wout", [128, DC * D])
    gvec_d = din("gvec", [128, 3 * 128])
    gpost_d = din("gpost", [128, D])
    lam_d = din("lamv", [128, 4 * 64])
    ident_d = din("ident", [128, 128])
    rope_d = din("rope", [NTT, 128, 256])
    ta_d = din("ta", [128, 2 * QC - 1])
    out_d = nc.dram_tensor("out", [SEQ, D], F32, kind="ExternalOutput").ap()

    P = Prog()
    from contextlib import ExitStack
    es = ExitStack()

    def sb(name, shape, dt):
        return es.enter_context(nc.sbuf_tensor(name, list(shape), dt))

    hn_raw = sb("hn_raw", [128, max(DC * c.L, DC * D)], BF16)
    yt_raw = sb("yt_raw", [128, max(DC * SEQ, 6 * D)], BF16)
    WS = 3
    wk_elems = max(WS * DC * 256 + (2 * SEQ + c.L + NTT * 129 + NT * 128), 2 * 2 * D * 2 + (2 * D if D > 512 else 0))
    wk_elems = (wk_elems + 63) // 64 * 64
    wk_raw = sb("wk_raw", [128, wk_elems], BF16)
    pT = [sb(f"pT{i}", [128, QC], BF16) for i in range(5)]
    osb = sb("osb", [128, 516], F32)
    ta_sb = sb("ta_sb", [128, 2 * QC - 1], F32)
    rope_sb = [sb(f"rope{i}", [128, 256], F32) for i in range(2)]
    ident = sb("ident_sb", [128, 128], BF16)
    gpre = sb("gpre_sb", [128, DC], F32)
    gvec = sb("gvec_sb", [128, 384], F32)
    sgs = sb("sgs", [128, 128], F32)
    lamv = sb("lamv_sb", [128, 256], F32)
    lamt = sb("lamt", [128, 8], F32)
    stat = sb("stat", [128, 3 * (NTT + 1)], F32)
    stat2 = sb("stat2", [128, 3 * (NTT + 1)], F32)
    NTMP = 4
    tA = [sb(f"tA{i}", [128, 128], F32) for i in range(NTMP)]
    tB = [sb(f"tB{i}", [128, 128], F32) for i in range(NTMP)]
    tC = [sb(f"tC{i}", [128, 128], F32) for i in range(NTMP)]
    tN = [sb(f"tN{i}", [128, 128], BF16) for i in range(NTMP)]
    tS = [sb(f"tS{i}", [128, 4], F32) for i in range(NTMP)]
    tY = [sb(f"tY{i}", [128, 128], BF16) for i in range(4)]
    tD = [sb(f"tD{i}", [128, 128], F32) for i in range(NTMP)]
    eA = [sb(f"eA{i}", [128, 128], F32) for i in range(4)]
    eS = [sb(f"eS{i}", [128, 4], F32) for i in range(4)]
    junk = sb("junk", [128, 128], BF16)
    o1buf = sb("o1buf", [128, QT * 128], F32)
    tE = [sb(f"tE{i}", [128, 4], F32) for i in range(4)]
    psA = es.enter_context(nc.psum_tensor("psA", [128, 2048], F32))
    psB = es.enter_context(nc.psum_tensor("psB", [128, 2048], F32))

    def bank(i):
        t = psA if i < 4 else psB
        return t[:, 512 * (i % 4):512 * (i % 4) + 512]

    hnT = hn_raw[:, 0:DC * c.L].rearrange("p (c t) -> p c t", c=DC)
    YT = yt_raw[:, 0:DC * SEQ].rearrange("p (c t) -> p c t", c=DC)
    wslot = [wk_raw[:, i * DC * 256:(i + 1) * DC * 256].rearrange("p (c n) -> p c n", c=DC) for i in range(WS)]
    o = WS * DC * 256
    qT = wk_raw[:, o:o + SEQ]; o += SEQ
    qT2 = wk_raw[:, o:o + SEQ]; o += SEQ
    kT = wk_raw[:, o:o + c.L]; o += c.L
    vU = wk_raw[:, o:o + NTT * 129].rearrange("p (t n) -> p t n", n=129); o += NTT * 129
    gate = wk_raw[:, o:o + NT * 128].rearrange("p (t n) -> p t n", n=128); o += NT * 128
    vAv = vU
    kTA = kT
    xs = [yt_raw[:, i * 2 * D:(i + 1) * 2 * D].bitcast(F32) for i in range(2)]
    xn = [yt_raw[:, 4 * D + i * D:4 * D + (i + 1) * D] for i in range(2)]
    xr = [wk_raw[:, i * 2 * D:(i + 1) * 2 * D].bitcast(F32) for i in range(2)]
    ot = [wk_raw[:, 4 * D + i * 2 * D:4 * D + (i + 1) * 2 * D].bitcast(F32) for i in range(2)]
    gpost = None
    wout = hn_raw[:, 0:DC * D].rearrange("p (c n) -> p c n", c=DC)

    def tok(t):
        if t < NT:
            return slice(128 * t, 128 * t + 128), 128
        return slice(SEQ, SEQ + c.NM), c.NM

    B = {}

    def buf(name):
        if name not in B:
            B[name] = Buf(name)
        return B[name]

    b_bank = [buf(f"bank{i}") for i in range(8)]
    for b_ in b_bank:
        b_.excl = True
    b_hn = [buf(f"hn{t}") for t in range(NTT)]
    b_yt = [buf(f"yt{t}") for t in range(NT)]
    b_w = [buf(f"w{i}") for i in range(WS)]
    b_const = buf("const")

    def dma(eng, out, in_, buf_, reads=(), writes=()):
        return P.add(eng, lambda e, out=out, in_=in_: e.dma_start(out=out, in_=in_), reads=reads, writes=writes, dma=buf_)

    b_c = [buf(f"c{i}") for i in range(6)]
    dma("sp", gpre[:], gpre_d, b_c[0], writes=[b_c[0]])
    dma("sp", gvec[:], gvec_d, b_c[1], writes=[b_c[1]])
    dma("sp", lamv[:], lam_d, b_c[2], writes=[b_c[2]])
    dma("sp", ta_sb[:], ta_d, b_c[3], writes=[b_c[3]])
    dma("pool", ident[:], ident_d, b_c[4], writes=[b_c[4]])
    b_gpre, b_gvec, b_lamv, b_ta, b_ident = b_c[0], b_c[1], b_c[2], b_c[3], b_c[4]

    b_vU = buf("vU"); b_vA = b_vU; b_lamt = buf("lamt"); b_sgs = buf("sgs")
    P.add("pool", lambda e: e.memset(vAv[:, :, 128:129], 1.0), writes=[b_vA])
    lv = lamv[:].rearrange("p (a n) -> p a n", a=4)
    P.add("dve", lambda e: e.tensor_tensor(out=tA[0][:, 0:64], in0=lv[:, 0, :], in1=lv[:, 1, :], op=ALU.mult), reads=[b_lamv], writes=[buf("tA0")])
    P.add("dve", lambda e: e.tensor_tensor(out=tA[0][:, 64:128], in0=lv[:, 2, :], in1=lv[:, 3, :], op=ALU.mult), reads=[b_lamv, buf("tA0")], writes=[buf("tA0")])
    P.add("dve", lambda e: e.tensor_reduce(out=lamt[:, 0:2], in_=tA[0][:].rearrange("p (a n) -> p a n", a=2), axis=mybir.AxisListType.X, op=ALU.add), reads=[buf("tA0")], writes=[b_lamt])
    P.add("act", lambda e: e.activation(out=lamt[:, 2:4], in_=lamt[:, 0:2], func=AF.Exp), reads=[b_lamt], writes=[b_lamt])
    P.add("dve", lambda e: e.tensor_tensor(out=lamt[:, 4:5], in0=lamt[:, 3:4], in1=lamt[:, 2:3], op=ALU.subtract), reads=[b_lamt], writes=[b_lamt])
    P.add("dve", lambda e: e.tensor_scalar(out=lamt[:, 5:6], in0=lamt[:, 4:5], scalar1=-LAMBDA_INIT, scalar2=None, op0=ALU.add), reads=[b_lamt], writes=[b_lamt])
    neglam = lamt[:, 5:6]
    P.add("dve", lambda e: e.tensor_scalar(out=sgs[:], in0=gvec[:, 256:384], scalar1=(1.0 - LAMBDA_INIT), scalar2=None, op0=ALU.mult), reads=[b_gvec], writes=[b_sgs])

    b_xs = [buf(f"xs{i}") for i in range(2)]
    b_xn = [buf(f"xn{i}") for i in range(2)]
    b_stat = buf("stat")
    b_ph0 = buf("ph0")
    HB = max(1, DC // 8)
    for t in range(NTT):
        sl, rows = tok(t)
        s = t % 2
        src = x_d[128 * t:128 * t + 128, :] if t < NT else meta_d
        dma("sp", xs[s][0:rows, :], src, b_xs[s], writes=[b_xs[s]])
        ss = stat[0:rows, 3 * t:3 * t + 1]; ln_ = stat[0:rows, 3 * t + 1:3 * t + 2]; rs = stat[0:rows, 3 * t + 2:3 * t + 3]
        P.add("act", lambda e, s=s, rows=rows, ss=ss: e.activation(out=xn[s][0:rows, :], in_=xs[s][0:rows, :], func=AF.Square, accum_out=ss),
              reads=[b_xs[s], b_ph0], writes=[b_xn[s], b_stat])
        P.add("act", lambda e, ss=ss, ln_=ln_: e.activation(out=ln_, in_=ss, func=AF.Ln, scale=1.0 / D, bias=NORM_EPS), reads=[b_stat], writes=[b_stat])
        P.add("act", lambda e, rs=rs, ln_=ln_: e.activation(out=rs, in_=ln_, func=AF.Exp, scale=-0.5), reads=[b_stat], writes=[b_stat])
        P.add("dve", lambda e, s=s, rows=rows, rs=rs: e.tensor_scalar(out=xn[s][0:rows, :], in0=xs[s][0:rows, :], scalar1=rs, scalar2=None, op0=ALU.mult),
              reads=[b_xs[s], b_stat, b_ph0], writes=[b_xn[s]])
        for hb in range(HB):
            bk = (2 * (t % 2) + hb) % 8 if HB <= 2 else hb % 8
            bkv = bank(bk).bitcast(BF16)
            nchb = min(8, DC - 8 * hb)
            for cc in range(nchb):
                ch = 8 * hb + cc
                P.add("pe", lambda e, bkv=bkv, cc=cc, ch=ch, s=s, rows=rows: e.transpose(bkv[:, cc * 128:cc * 128 + rows], xn[s][0:rows, ch * 128:(ch + 1) * 128], ident[0:rows, 0:rows]),
                      reads=[b_xn[s], b_ident, b_ph0], writes=[b_bank[bk]])
            srcv = bkv[:, 0:nchb * 128].rearrange("p (c r) -> p c r", c=nchb)[:, :, 0:rows]
            gv = gpre[:, 8 * hb:8 * hb + nchb].unsqueeze(2).to_broadcast([128, nchb, rows])
            P.add("dve", lambda e, srcv=srcv, gv=gv, hb=hb, nchb=nchb, sl=sl: e.tensor_tensor(out=hnT[:, 8 * hb:8 * hb + nchb, sl], in0=srcv, in1=gv, op=ALU.mult),
                  reads=[b_bank[bk], b_gpre], writes=[b_hn[t]])

    MARK = {}
    MARK['ph0'] = len(P.ops)
    wstate = {"next": 0}
    b_rope = [buf(f"rope{i}") for i in range(2)]
    rope_ctr = {"n": 0}
    tmp_ctr = {"n": 0}
    tn_ctr = {"n": 0}
    ty_ctr = {"n": 0}
    b_tY = [buf(f"tY{i}") for i in range(4)]
    b_tA = [buf(f"tA{i}") for i in range(NTMP)]; b_tB = [buf(f"tB{i}") for i in range(NTMP)]
    b_tC = [buf(f"tC{i}") for i in range(NTMP)]; b_tN = [buf(f"tN{i}") for i in range(NTMP)]
    b_tS = [buf(f"tS{i}") for i in range(NTMP)]
    b_junk = buf("junk")
    proj_banks = [0, 1, 2, 3, 4]
    pb_ctr = {"n": 0}
    tr_ctr = {"n": 0}
    b_qT = buf("qT"); b_qT2 = buf("qT2"); b_kT = buf("kT"); b_kTA = b_kT; b_gate = buf("gate")

    pend = []
    tk = {"n": 0, "seq": 0}

    def defer(fn, delay, keys=()):
        tk["seq"] += 1
        pend.append((tk["n"] + delay, tk["seq"], fn, tuple(keys)))
        pend.sort(key=lambda p: (p[0], p[1]))

    def acquire(key):
        while any(key in p[3] for p in pend):
            pend.pop(0)[2]()

    def tick():
        tk["n"] += 1
        while pend and pend[0][0] <= tk["n"]:
            pend.pop(0)[2]()

    def flush_all():
        while pend:
            pend.pop(0)[2]()

    def load_chunk(ci):
        s = ci % WS
        dst = wslot[s].rearrange("p c n -> p (c n)").rearrange("p (a b) -> p a b", b=512)
        srcv = wch_d[ci].rearrange("p (a b) -> p a b", b=512)
        dma("pool", dst, srcv, b_w[s], writes=[b_w[s]])

    PREFETCH = WS - 1
    for ci in range(min(PREFETCH, c.NCH)):
        load_chunk(ci)
    wstate["loaded"] = min(PREFETCH, c.NCH)

    def next_chunk():
        ci = wstate["next"]
        wstate["next"] += 1
        return ci % WS

    def prefetch_more():
        if wstate["loaded"] < c.NCH:
            load_chunk(wstate["loaded"])
            wstate["loaded"] += 1

    def proj_token_major(ws, t, ncols=256):
        sl, rows = tok(t)
        bk = proj_banks[pb_ctr["n"] % len(proj_banks)]; pb_ctr["n"] += 1
        for ch in range(DC):
            P.add("pe", lambda e, bk=bk, ch=ch, ws=ws, sl=sl, rows=rows: e.matmul(bank(bk)[0:rows, 0:ncols], hnT[:, ch, sl], wslot[ws][:, ch, 0:ncols], start=(ch == 0), stop=(ch == DC - 1)),
                  reads=[b_hn[t], b_w[ws]], writes=[b_bank[bk]])
        return bk

    def y_transpose(ybf, b_y, mix_chunk, tq, on_dve=False):
        k = tr_ctr["n"] % 4; tr_ctr["n"] += 1
        tv = bank(7).bitcast(BF16)[:, k * 256:k * 256 + 128]
        P.add("pe", lambda e: e.transpose(tv, ybf, ident[:, :]), reads=[b_y, b_ident], writes=[b_bank[7]])
        first = not ytr_ctr.get("started")
        ytr_ctr["started"] = True
        wr = [b_yt[tq]] + ([b_ph0, b_xs[0], b_xs[1], b_xn[0], b_xn[1]] if first else [])
        if on_dve:
            P.add("dve", lambda e: e.tensor_copy(out=YT[:, mix_chunk, 128 * tq:128 * tq + 128], in_=tv), reads=[b_bank[7]], writes=wr)
        else:
            P.add("act", lambda e: e.activation(out=YT[:, mix_chunk, 128 * tq:128 * tq + 128], in_=tv, func=AF.Copy), reads=[b_bank[7]], writes=wr)

    qk_ctr = {"n": 0}
    b_tD = [buf(f"tD{i}") for i in range(NTMP)]

    def qk_tile(bk, rows, t, gcol, dstT, b_dst, sl, other):
        i = qk_ctr["n"] % NTMP; qk_ctr["n"] += 1
        key = ("qk", i)
        acquire(key)
        if other == "gate":
            acquire(("gate",))
        acquire(("sg", i))
        r = rope_ctr["n"] % 2; rope_ctr["n"] += 1
        srcp = bank(bk)[0:rows, 0:128]
        osrc = bank(bk)[0:rows, 128:256]
        dma("sp", rope_sb[r][0:rows, :], rope_d[t, 0:rows, :], b_rope[r], writes=[b_rope[r]])
        ss = tS[i][0:rows, 0:1]; ln_ = tS[i][0:rows, 1:2]; rs = tS[i][0:rows, 2:3]
        xg = tA[i][0:rows, :]
        t1 = tB[i][0:rows, :]
        t2 = tD[i][0:rows, :]
        xb = tN[i][0:rows, :]

        def st2():
            if other == "gate":
                P.add("act", lambda e: e.activation(out=tC[i][:], in_=osrc, func=AF.Exp, scale=-1.0), reads=[b_bank[bk]], writes=[b_tC[i]])
            else:
                _, vview, b_v = other
                P.add("act", lambda e: e.activation(out=vview[0:rows, t, 0:128], in_=osrc, func=AF.Copy), reads=[b_bank[bk]], writes=[b_v])
            P.add("act", lambda e: e.activation(out=junk[0:rows, :], in_=srcp, func=AF.Square, accum_out=ss), reads=[b_bank[bk]], writes=[b_junk, b_tS[i]])
            P.add("act", lambda e: e.activation(out=ln_, in_=ss, func=AF.Ln, scale=1.0 / 128, bias=NORM_EPS), reads=[b_tS[i]], writes=[b_tS[i]])
            P.add("act", lambda e: e.activation(out=rs, in_=ln_, func=AF.Exp, scale=-0.5), reads=[b_tS[i]], writes=[b_tS[i]])
            P.add("dve", lambda e: e.tensor_tensor(out=xg, in0=srcp, in1=gvec[0:rows, gcol:gcol + 128], op=ALU.mult), reads=[b_bank[bk], b_gvec], writes=[b_tA[i]])

        def st3():
            if other == "gate":
                P.add("act", lambda e: e.activation(out=tC[i][:], in_=tC[i][:], func=AF.Ln, bias=1.0), reads=[b_tC[i]], writes=[b_tC[i]])
                P.add("act", lambda e: e.activation(out=tC[i][:], in_=tC[i][:], func=AF.Exp, scale=-1.0), reads=[b_tC[i]], writes=[b_tC[i]])
            P.add("dve", lambda e: e.tensor_tensor(out=t1, in0=xg, in1=rope_sb[r][0:rows, 0:128], op=ALU.mult), reads=[b_tA[i], b_rope[r]], writes=[b_tB[i]])
            xsw = xg.rearrange("p (a s j) -> p a s j", a=2, s=2)[:, :, ::-1, :]
            P.add("pool", lambda e: e.tensor_tensor(out=t2.rearrange("p (a s j) -> p a s j", a=2, s=2), in0=xsw, in1=rope_sb[r][0:rows, 128:256].rearrange("p (a s j) -> p a s j", a=2, s=2), op=ALU.mult),
                  reads=[b_tA[i], b_rope[r]], writes=[b_tD[i]])

        def st4():
            if other == "gate":
                P.add("dve", lambda e: e.tensor_tensor(out=gate[:, t, :], in0=osrc, in1=tC[i][:], op=ALU.mult), reads=[b_bank[bk], b_tC[i]], writes=[b_gate])
            P.add("pool", lambda e: e.tensor_tensor(out=t1, in0=t1, in1=t2, op=ALU.add), reads=[b_tB[i], b_tD[i]], writes=[b_tB[i]])
            P.add("dve", lambda e: e.tensor_scalar(out=xb, in0=t1, scalar1=rs, scalar2=None, op0=ALU.mult), reads=[b_tB[i], b_tS[i]], writes=[b_tN[i]])

        def st5():
            k = tr_ctr["n"] % 4; tr_ctr["n"] += 1
            tv = bank(7).bitcast(BF16)[:, k * 256:k * 256 + rows]
            P.add("pe", lambda e: e.transpose(tv, xb, ident[0:rows, 0:rows]), reads=[b_tN[i], b_ident], writes=[b_bank[7]])
            P.add("dve", lambda e: e.tensor_copy(out=dstT[:, sl], in_=tv), reads=[b_bank[7]], writes=[b_dst])

        st2()
        defer(st3, 1, [key])
        defer(st4, 2, [key])
        defer(st5, 3, [key])
        tick()

    sg_ctr = {"n": 0}

    def silu_gate(bk, col0, t):
        i = sg_ctr["n"] % NTMP; sg_ctr["n"] += 1
        key = ("sg", i)
        acquire(key)
        acquire(("qk", i))
        acquire(("gate",))
        gsrc = bank(bk)[:, col0:col0 + 128]
        P.add("act", lambda e: e.activation(out=tC[i][:], in_=gsrc, func=AF.Exp, scale=-1.0), reads=[b_bank[bk]], writes=[b_tC[i]])

        def st():
            P.add("act", lambda e: e.activation(out=tC[i][:], in_=tC[i][:], func=AF.Ln, bias=1.0), reads=[b_tC[i]], writes=[b_tC[i]])
            P.add("act", lambda e: e.activation(out=tC[i][:], in_=tC[i][:], func=AF.Exp, scale=-1.0), reads=[b_tC[i]], writes=[b_tC[i]])

        def st_b():
            P.add("dve", lambda e: e.tensor_tensor(out=gate[:, t, :], in0=gsrc, in1=tC[i][:], op=ALU.mult), reads=[b_bank[bk], b_tC[i]], writes=[b_gate])
        defer(st, 1, [key])
        defer(st_b, 2, [key])

    SCALE_A = 1.0 / math.sqrt(128.0)
    NPT = len(pT)
    b_pT = [buf(f"pT{i}") for i in range(NPT)]
    b_o1s = [buf(f"o1_{j}") for j in range(QT)]
    b_osbs = [buf("osb0"), buf("osb1")]
    e_ctr = {"n": 0}
    b_eA = [buf(f"eA{i}") for i in range(4)]
    b_eS = [buf(f"eS{i}") for i in range(4)]
    b_tE = [buf(f"tE{i}") for i in range(4)]
    te_ctr = {"n": 0}
    ytr_ctr = {}
    SBANKS = [0, 1, 2, 3, 4]
    OBANKS = [5, 6]

    def attention(kq_aps, b_k, b_qs, v_ap, b_v, tiles_fn, epilogue, dve_heavy=False):
        nm = len(kq_aps)
        seq = [(cq, m, kt) for cq in range(NQC) for m in range(nm) for kt in range(NTT)]
        LOOK = len(SBANKS) - 1
        issued = {}

        def issue_S(n):
            cq, m, kt = seq[n]
            sl, rows = tok(kt)
            bk = SBANKS[n % len(SBANKS)]
            kT_ap, qT_ap = kq_aps[m]
            P.add("pe", lambda e: e.matmul(bank(bk)[0:rows, 0:QC], kT_ap[:, sl], qT_ap[:, cq * QC:(cq + 1) * QC], start=True, stop=True),
                  reads=[b_k, b_qs[m]], writes=[b_bank[bk]])
            issued[n] = bk

        LA = 2

        def tile_ops(n):
            cq, m, kt = seq[n]
            sl, rows = tok(kt)
            pi = n % NPT
            tiles_fn(m, issued[n], rows, cq, kt, pT[pi][0:rows, :], b_pT[pi])

        for n in range(min(LOOK, len(seq))):
            issue_S(n)
        for n in range(min(LA, len(seq))):
            tile_ops(n)
        for n, (cq, m, kt) in enumerate(seq):
            sl, rows = tok(kt)
            pi = n % NPT
            if n + LA < len(seq):
                tile_ops(n + LA)
            for j in range(QT):
                ob = OBANKS[j // 2]
                oc = (j % 2) * 129
                first_in_bank = (kt == 0 and (j % 2 == 0))
                P.add("pe", lambda e, ob=ob, oc=oc, j=j, rows=rows, kt=kt, pi=pi, fib=first_in_bank: e.matmul(
                    bank(ob)[:, oc:oc + 129], pT[pi][0:rows, j * 128:(j + 1) * 128], v_ap[0:rows, kt, :],
                    start=fib, stop=(kt == NTT - 1), skip_group_check=True),
                    reads=[b_pT[pi], b_v], writes=[b_bank[ob]])
            if n + LOOK < len(seq):
                issue_S(n + LOOK)
            if kt == NTT - 1:
                acquire(("osb",))
                nob = (QT + 1) // 2
                for bi in range(nob):
                    w = 129 * min(2, QT - 2 * bi)
                    if bi == 0:
                        P.add("act", lambda e, bi=bi, w=w: e.activation(out=osb[:, bi * 258:bi * 258 + w], in_=bank(OBANKS[bi])[:, 0:w], func=AF.Copy),
                              reads=[b_bank[OBANKS[bi]]], writes=[b_osbs[bi]])
                    elif dve_heavy:
                        P.add("act", lambda e, bi=bi, w=w: e.activation(out=osb[:, bi * 258:bi * 258 + w], in_=bank(OBANKS[bi])[:, 0:w], func=AF.Copy),
                              reads=[b_bank[OBANKS[bi]]], writes=[b_osbs[bi]])
                    else:
                        P.add("dve", lambda e, bi=bi, w=w: e.tensor_copy(out=osb[:, bi * 258:bi * 258 + w], in_=bank(OBANKS[bi])[:, 0:w]),
                              reads=[b_bank[OBANKS[bi]]], writes=[b_osbs[bi]])
                for j in range(QT):
                    epilogue(m, cq, j, osb[:, j * 129:(j + 1) * 129], b_osbs[j // 2])
            tick()

    for hq in range(c.AH):
        g = hq // c.AG
        if hq % c.AG == 0:
            ws = next_chunk()
            for t in range(NTT):
                sl, rows = tok(t)
                bk = proj_token_major(ws, t)
                qk_tile(bk, rows, t, 128, kTA, b_kTA, sl, ("v", vAv, b_vA))
            prefetch_more()
        ws = next_chunk()
        for t in range(NT):
            sl, rows = tok(t)
            bk = proj_token_major(ws, t)
            qk_tile(bk, rows, t, 0, qT, b_qT, sl, "gate")
        prefetch_more()
        flush_all()

        def tiles_A(m, bk, rows, cq, kt, pdst, b_p):
            P.add("act", lambda e: e.activation(out=pdst, in_=bank(bk)[0:rows, 0:QC], func=AF.Exp, scale=SCALE_A), reads=[b_bank[bk]], writes=[b_p])

        def epi_A(m, cq, j, ops, b_ops, hq=hq):
            tq = cq * QT + j
            i = e_ctr["n"] % 4; e_ctr["n"] += 1
            key = ("e", i)
            acquire(key)

            def e2():
                P.add("dve", lambda e: e.reciprocal(out=tE[i][:, 0:1], in_=ops[:, 128:129]), reads=[b_ops], writes=[b_tE[i]])
                P.add("dve", lambda e: e.scalar_tensor_tensor(out=tY[i][:], in0=ops[:, 0:128], scalar=tE[i][:, 0:1], in1=gate[:, tq, :], op0=ALU.mult, op1=ALU.mult),
                      reads=[b_ops, b_tE[i], b_gate], writes=[b_tY[i]])
            defer(e2, 1 + j, [key, ("osb",), ("gate",)])
            defer(lambda: y_transpose(tY[i][:], b_tY[i], hq, tq, on_dve=True), 3 + j, [key])

        attention([(kTA, qT)], b_kTA, [b_qT], vAv, b_vA, tiles_A, epi_A)

    MARK['A'] = len(P.ops)
    WO = {}

    def load_wout():
        NWD = DC
        cpd = 1
        b_wo = [buf(f"wo{i}") for i in range(NWD)]
        WO['b_wo'] = b_wo; WO['cpd'] = cpd
        for i in range(NWD):
            dstv = wout[:, i, :].rearrange("p (a b) -> p a b", b=512)
            srcv = wout_d[:, i * D:(i + 1) * D].rearrange("p (a b) -> p a b", b=512)
            defer(lambda dstv=dstv, srcv=srcv, i=i: dma("pool", dstv, srcv, b_wo[i], reads=[], writes=[b_wo[i]] + b_hn), 1 + 6 * i)

    P.add("pool", lambda e: e.memset(qT[64:128, :], 0.0), writes=[b_qT])
    P.add("pool", lambda e: e.memset(qT2[0:64, :], 0.0), writes=[b_qT2])
    YW = 2 * QC - 1
    Y0 = QC - 128
    WL = Y0 + 127
    for h in range(c.BH):
        slope = 2.0 ** (-8.0 * (h + 1) / c.BH)
        ws = next_chunk()
        ncht = NQC
        for cq in range(NQC):
            bk = proj_banks[pb_ctr["n"] % len(proj_banks)]; pb_ctr["n"] += 1
            for ch in range(DC):
                P.add("pe", lambda e, bk=bk, ch=ch, ws=ws, cq=cq: e.matmul(bank(bk)[:, 0:QC], wslot[ws][:, ch, 0:128], hnT[:, ch, cq * QC:(cq + 1) * QC], start=(ch == 0), stop=(ch == DC - 1)),
                      reads=[b_hn[tt] for tt in range(cq * QT, (cq + 1) * QT)] + [b_w[ws]], writes=[b_bank[bk]])
            P.add("act", lambda e, bk=bk, cq=cq: e.activation(out=qT[0:64, cq * QC:(cq + 1) * QC], in_=bank(bk)[0:64, 0:QC], func=AF.Copy, scale=0.125), reads=[b_bank[bk]], writes=[b_qT])
            P.add("act", lambda e, bk=bk, cq=cq: e.activation(out=qT2[64:128, cq * QC:(cq + 1) * QC], in_=bank(bk)[64:128, 0:QC], func=AF.Copy, scale=0.125), reads=[b_bank[bk]], writes=[b_qT2])
            tick()
        for cq in range(NQC + 1):
            bk = proj_banks[pb_ctr["n"] % len(proj_banks)]; pb_ctr["n"] += 1
            csl = slice(cq * QC, (cq + 1) * QC) if cq < NQC else slice(SEQ, SEQ + c.NM)
            n = QC if cq < NQC else c.NM
            rd = [b_hn[tt] for tt in range(cq * QT, (cq + 1) * QT)] if cq < NQC else [b_hn[NT]]
            for ch in range(DC):
                P.add("pe", lambda e, bk=bk, ch=ch, ws=ws, csl=csl, n=n: e.matmul(bank(bk)[:, 0:n], wslot[ws][:, ch, 128:256], hnT[:, ch, csl], start=(ch == 0), stop=(ch == DC - 1)),
                      reads=rd + [b_w[ws]], writes=[b_bank[bk]])
            P.add("dve", lambda e, bk=bk, csl=csl, n=n: e.tensor_copy(out=kT[:, csl], in_=bank(bk)[:, 0:n]), reads=[b_bank[bk]], writes=[b_kT])
            tick()
        prefetch_more()
        ws = next_chunk()
        for t in range(NTT):
            sl, rows = tok(t)
            ncols = 256 if t < NT else 128
            bk = proj_token_major(ws, t, ncols)
            P.add("act", lambda e, bk=bk, rows=rows, t=t: e.activation(out=vU[0:rows, t, 0:128], in_=bank(bk)[0:rows, 0:128], func=AF.Copy), reads=[b_bank[bk]], writes=[b_vU])
            if t < NT:
                silu_gate(bk, 128, t)
            tick()
        prefetch_more()
        flush_all()
        if h == c.BH - 1:
            load_wout()

        def tiles_B(m, bk, rows, cq, kt, pdst, b_p, slope=slope):
            if kt < NT:
                delta = QC * cq - 128 * kt
                if -QC < delta < 128:
                    j = (-delta) // 128
                    ws_ = Y0 - 128 * j
                    coef, cst = -slope, 0.0
                elif delta >= 128:
                    ws_ = WL
                    coef, cst = -slope, -slope * (delta - 127)
                else:
                    ws_ = WL
                    coef, cst = slope, slope * (delta - 127)
            else:
                ws_ = WL
                delta = QC * cq + c.NM
                coef, cst = -slope, -slope * (delta - 127)
            sv = bank(bk)[0:rows, 0:QC]
            P.add("dve", lambda e: e.scalar_tensor_tensor(out=sv, in0=ta_sb[0:rows, ws_:ws_ + QC], scalar=coef, in1=sv, op0=ALU.mult, op1=ALU.add),
                  reads=[b_bank[bk], b_ta], writes=[b_bank[bk]])
            P.add("act", lambda e: e.activation(out=pdst, in_=sv, func=AF.Exp, bias=float(cst)), reads=[b_bank[bk]], writes=[b_p])

        def epi_B(m, cq, j, ops, b_ops, h=h):
            tq = cq * QT + j
            i = e_ctr["n"] % 4; e_ctr["n"] += 1
            key = ("e", i)
            acquire(key)
            od = eA[i][:]
            ss = eS[i][:, 0:1]; ln_ = eS[i][:, 1:2]; rs = eS[i][:, 2:3]

            def e2():
                P.add("dve", lambda e: e.reciprocal(out=tE[i][:, 0:1], in_=ops[:, 128:129]), reads=[b_ops], writes=[b_tE[i]])
                if m == 0:
                    P.add("pool", lambda e: e.tensor_scalar(out=o1buf[:, j * 128:(j + 1) * 128], in0=ops[:, 0:128], scalar1=tE[i][:, 0:1], scalar2=1.0, op0=ALU.mult, op1=ALU.mult),
                          reads=[b_ops, b_tE[i]], writes=[b_o1s[j]])
                    return
                P.add("dve", lambda e: e.tensor_tensor(out=tE[i][:, 1:2], in0=tE[i][:, 0:1], in1=neglam, op=ALU.mult), reads=[b_tE[i], b_lamt], writes=[b_tE[i]])
                P.add("pool", lambda e: e.tensor_scalar(out=od, in0=ops[:, 0:128], scalar1=tE[i][:, 1:2], scalar2=1.0, op0=ALU.mult, op1=ALU.mult),
                      reads=[b_ops, b_tE[i]], writes=[b_eA[i]])
                P.add("pool", lambda e: e.tensor_tensor(out=od, in0=od, in1=o1buf[:, j * 128:(j + 1) * 128], op=ALU.add),
                      reads=[b_eA[i], b_o1s[j]], writes=[b_eA[i]])

            def e3():
                P.add("act", lambda e: e.activation(out=junk[:], in_=od, func=AF.Square, accum_out=ss), reads=[b_eA[i]], writes=[b_junk, b_eS[i]])
                P.add("act", lambda e: e.activation(out=ln_, in_=ss, func=AF.Ln, scale=1.0 / 128, bias=NORM_EPS), reads=[b_eS[i]], writes=[b_eS[i]])
                P.add("act", lambda e: e.activation(out=rs, in_=ln_, func=AF.Exp, scale=-0.5), reads=[b_eS[i]], writes=[b_eS[i]])

            def e4():
                P.add("pool", lambda e: e.tensor_tensor(out=od, in0=od, in1=sgs[:], op=ALU.mult), reads=[b_eA[i], b_sgs], writes=[b_eA[i]])
                P.add("pool", lambda e: e.tensor_scalar(out=od, in0=od, scalar1=rs, scalar2=1.0, op0=ALU.mult, op1=ALU.mult),
                      reads=[b_eA[i], b_eS[i]], writes=[b_eA[i]])
                P.add("pool", lambda e: e.tensor_tensor(out=tY[i][:], in0=od, in1=gate[:, tq, :], op=ALU.mult),
                      reads=[b_eA[i], b_gate], writes=[b_tY[i]])

            defer(e2, 1 + j, [key, ("osb",)])
            if m == 1:
                defer(e3, 4 + j, [key])
                defer(e4, 7 + j, [key, ("gate",)])
                defer(lambda: y_transpose(tY[i][:], b_tY[i], c.AH + h, tq, on_dve=True), 10 + j, [key])

        attention([(kT, qT), (kT, qT2)], b_kT, [b_qT, b_qT2], vU, b_vU, tiles_B, epi_B, dve_heavy=False)

    flush_all()
    MARK['B'] = len(P.ops)
    gpost_sb = None
    b_xr = [buf(f"xr{i}") for i in range(2)]
    b_ot = [buf(f"ot{i}") for i in range(2)]
    b_unit = [b_w[i] for i in range(WS)] + [b_qT, b_qT2, b_kT, b_vU, b_gate]
    b_gpost = buf("gpost")
    gpost_sb = sb("gpost_sb", [128, D], F32) if D <= 512 else None
    if gpost_sb is None:
        off = 8 * D
        assert wk_elems >= off + 2 * D, (wk_elems, off + 2 * D)
        gpost_v = wk_raw[:, off:off + 2 * D].bitcast(F32)
    else:
        gpost_v = gpost_sb[:]
    dma("sp", gpost_v, gpost_d, b_gpost, writes=[b_gpost] + b_unit)
    NB = D // 512 if D >= 512 else 1
    BW_ = min(512, D)
    for t in range(NT):
        s = t % 2
        sl, rows = tok(t)
        dma("sp", xr[s][:], x_d[128 * t:128 * t + 128, :], b_xr[s], writes=[b_xr[s]] + (b_unit if t < 2 else []))
        base = 4 * (t % 2)
        if NB * 1 > 4:
            raise NotImplementedError
        for nb in range(NB):
            bk = base + nb
            for ch in range(DC):
                P.add("pe", lambda e, bk=bk, ch=ch, nb=nb, sl=sl: e.matmul(bank(bk)[:, 0:BW_], YT[:, ch, sl], wout[:, ch, nb * BW_:(nb + 1) * BW_], start=(ch == 0), stop=(ch == DC - 1)),
                      reads=[b_yt[t], WO['b_wo'][ch // WO['cpd']]], writes=[b_bank[bk]])
        pst = (psA if base == 0 else psB)[:, 0:D] if D >= 512 else (psA if base == 0 else psB)[:, 0:D]
        bks = [b_bank[base + nb] for nb in range(NB)]
        ss = stat2[:, 3 * t:3 * t + 1]; ln_ = stat2[:, 3 * t + 1:3 * t + 2]; rs = stat2[:, 3 * t + 2:3 * t + 3]
        b_st2 = buf(f"st2_{t % 2}")
        P.add("act", lambda e, s=s, pst=pst, ss=ss: e.activation(out=ot[s][:], in_=pst, func=AF.Square, accum_out=ss), reads=bks, writes=[b_ot[s], b_st2] + (b_unit if t < 2 else []))
        P.add("act", lambda e, ss=ss, ln_=ln_: e.activation(out=ln_, in_=ss, func=AF.Ln, scale=1.0 / D, bias=NORM_EPS), reads=[b_st2], writes=[b_st2])
        P.add("act", lambda e, rs=rs, ln_=ln_: e.activation(out=rs, in_=ln_, func=AF.Exp, scale=-0.5), reads=[b_st2], writes=[b_st2])
        P.add("dve", lambda e, s=s, pst=pst: e.tensor_tensor(out=ot[s][:], in0=pst, in1=gpost_v, op=ALU.mult), reads=bks + [b_gpost], writes=[b_ot[s]])
        P.add("dve", lambda e, s=s, rs=rs: e.scalar_tensor_tensor(out=ot[s][:], in0=ot[s][:], scalar=rs, in1=xr[s][:], op0=ALU.mult, op1=ALU.add),
              reads=[b_ot[s], b_st2, b_xr[s]], writes=[b_ot[s]])
        dma("sp", out_d[128 * t:128 * t + 128, :], ot[s][:], b_ot[s], reads=[b_ot[s]], writes=[buf(f"outd{t}")])
    fin_reads = [buf(f"outd{t}") for t in range(NT)]
    P.add("sp", lambda e: e.nop(), reads=fin_reads)

    MARK['end'] = len(P.ops)
    import os as _os
    _tr = _os.environ.get('KTRUNC')
    if _tr:
        P.ops = P.ops[:(MARK[_tr] if _tr in MARK else int(_tr))]
        print('TRUNC', MARK, len(P.ops))
    eng_sems = {e: es.enter_context(nc.semaphore(f"s_{e}")) for e in ENGS}
    dma_sems = {b.name: es.enter_context(nc.semaphore(f"d_{b.name}")) for b in P.dma_bufs}
    block = es.enter_context(nc.Block())
    engines = {}

    def make(ename):
        def body(eng):
            engines_local = {ename: eng}
            emit_one(ename, eng)
        return body

    cnt = {e: 0 for e in ENGS}
    for op in P.ops:
        if op.dma is None and op.needed:
            cnt[op.eng] += 1
            op.ticket = (op.eng, cnt[op.eng])

    def sem_of(t):
        return eng_sems[t[0]] if isinstance(t[0], str) else dma_sems[t[0].name]

    def emit_one(e, eng):
        waited = {}
        for op in P.ops:
            if op.eng != e:
                continue
            for d in op.deps:
                if d.dma is None and d.eng == "pe" and e == "pe":
                    continue
                key = d.ticket[0] if isinstance(d.ticket[0], str) else d.ticket[0].name
                if waited.get(key, 0) >= d.ticket[1]:
                    continue
                eng.wait_ge(sem_of(d.ticket), d.ticket[1])
                waited[key] = d.ticket[1]
            inst = op.fn(eng)
            if op.dma is not None:
                inst.then_inc(dma_sems[op.dma.name], 16)
            elif op.needed:
                inst.then_inc(eng_sems[e], 1)

    @block.tensor
    def _(eng):
        emit_one("pe", eng)

    @block.scalar
    def _(eng):
        emit_one("act", eng)

    @block.vector
    def _(eng):
        emit_one("dve", eng)

    @block.gpsimd
    def _(eng):
        emit_one("pool", eng)

    @block.sync
    def _(eng):
        emit_one("sp", eng)

    es.close()
    return nc, cnt


def host_consts(cfg: Cfg):
    c = cfg
    NTT = c.NT + 1
    inv_freq = ROPE_THETA ** (-np.arange(0, 64, 2, dtype=np.float64) / 64.0)
    rope = np.zeros((NTT, 128, 256), np.float32)
    n = np.arange(c.SEQ)
    gr = (n // 64).astype(np.float64)[:, None] * inv_freq[None, :]
    gc = (n % 64).astype(np.float64)[:, None] * inv_freq[None, :]
    cr, sr, cc, sc = np.cos(gr), np.sin(gr), np.cos(gc), np.sin(gc)
    COS = np.concatenate([cr, cr, cc, cc], axis=1).astype(np.float32)
    SINS = np.concatenate([-sr, sr, -sc, sc], axis=1).astype(np.float32)
    rope[:c.NT, :, 0:128] = COS.reshape(c.NT, 128, 128)
    rope[:c.NT, :, 128:256] = SINS.reshape(c.NT, 128, 128)
    rope[c.NT, :, 0:128] = 1.0
    QC = c.QC
    Y0 = QC - 128
    y = np.arange(2 * QC - 1)[None, :]
    k = np.arange(128)[:, None]
    ta = np.abs(y - Y0 - k).astype(np.float32)
    ident = np.eye(128, dtype=np.float32)
    return rope, ta, ident


def host_inputs(cfg: Cfg, x, meta_tokens, pre_norm_g, w_in, q_norm_g, k_norm_g, lambda_q1, lambda_k1,
                lambda_q2, lambda_k2, subln_g, w_out, post_norm_g):
    c = cfg
    f = np.float32
    rope, ta, ident = host_consts(c)
    w_in0 = np.asarray(w_in[0], f)
    wr = w_in0.reshape(c.DC, 128, c.INW)
    wch = np.empty((c.NCH, 128, c.DC * 256), f)
    for ci, (kind, i) in enumerate(c.chunks):
        cols = c.chunk_cols(kind, i)
        wch[ci] = wr[:, :, cols].transpose(1, 0, 2).reshape(128, c.DC * 256)
    wout = np.ascontiguousarray(np.asarray(w_out[0], f).reshape(c.DC, 128, c.D).transpose(1, 0, 2).reshape(128, c.DC * c.D))
    gpre = np.ascontiguousarray(np.asarray(pre_norm_g[0], f).reshape(c.DC, 128).T)
    gvec = np.ascontiguousarray(np.broadcast_to(np.concatenate([np.asarray(q_norm_g[0], f), np.asarray(k_norm_g[0], f), np.asarray(subln_g[0], f)])[None, :], (128, 384)))
    gpost = np.ascontiguousarray(np.broadcast_to(np.asarray(post_norm_g[0], f)[None, :], (128, c.D)))
    lamv = np.ascontiguousarray(np.broadcast_to(np.concatenate([np.asarray(lambda_q1[0], f), np.asarray(lambda_k1[0], f), np.asarray(lambda_q2[0], f), np.asarray(lambda_k2[0], f)])[None, :], (128, 256)))
    shared = {"meta": np.ascontiguousarray(np.asarray(meta_tokens, f)), "gpre": gpre, "wch": wch, "wout": wout, "gvec": gvec,
              "gpost": gpost, "lamv": lamv, "ident": ident, "rope": rope, "ta": ta}
    xs = np.asarray(x, f)
    return [dict(shared, x=np.ascontiguousarray(xs[b])) for b in range(xs.shape[0])]


_NC_CACHE = {}


def kernel(x, meta_tokens, pre_norm_g, w_in, q_norm_g, k_norm_g, lambda_q1, lambda_k1,
           lambda_q2, lambda_k2, subln_g, w_out, post_norm_g):
    cfg = Cfg(2048, 2048)
    in_maps = host_inputs(cfg, x, meta_tokens, pre_norm_g, w_in, q_norm_g, k_norm_g, lambda_q1, lambda_k1,
                          lambda_q2, lambda_k2, subln_g, w_out, post_norm_g)
    nc, _ = build_nc(cfg)
    res = run_bass_kernel_spmd(nc, in_maps, core_ids=list(range(len(in_maps))))
    return np.stack([np.asarray(r["out"], np.float32) for r in res.results], axis=0)
```

```python
import math
import numpy as np
import concourse.bass as bass
import concourse.mybir as mybir
from concourse.bass_utils import run_bass_kernel_spmd

F32 = mybir.dt.float32
BF16 = mybir.dt.bfloat16
AF = mybir.ActivationFunctionType
ALU = mybir.AluOpType

NORM_EPS = 1e-6
ROPE_THETA = 10000.0
LAMBDA_INIT = 0.8 - 0.6 * math.exp(-0.3 * 0)


class Cfg:
    def __init__(self, D=2048, SEQ=2048):
        self.D, self.SEQ = D, SEQ
        self.NM = 16
        self.AW = D // 2
        self.AH = self.AW // 128
        self.AKV = 2
        self.AG = self.AH // self.AKV
        self.BW = D - self.AW
        self.BH = self.BW // 128
        self.DC = D // 128
        self.NT = SEQ // 128
        self.QC = min(512, SEQ)
        self.NQC = SEQ // self.QC
        self.QT = self.QC // 128
        self.L = SEQ + self.NM
        self.o_qa = 0
        self.o_ka = self.o_qa + self.AH * 128
        self.o_va = self.o_ka + self.AKV * 128
        self.o_ga = self.o_va + self.AKV * 128
        self.o_qb = self.o_ga + self.AW
        self.o_kb = self.o_qb + self.BH * 128
        self.o_vb = self.o_kb + self.BH * 128
        self.o_gb = self.o_vb + self.BH * 128
        self.INW = self.o_gb + self.BW
        self.chunks = []
        for hq in range(self.AH):
            if hq % self.AG == 0:
                g = hq // self.AG
                self.chunks.append(("akv", g))
            self.chunks.append(("aqg", hq))
        for h in range(self.BH):
            self.chunks.append(("bqk", h))
            self.chunks.append(("bvg", h))
        self.NCH = len(self.chunks)

    def chunk_cols(self, kind, i):
        r = np.arange(128)
        if kind == "akv":
            return np.concatenate([self.o_ka + 128 * i + r, self.o_va + 128 * i + r])
        if kind == "aqg":
            return np.concatenate([self.o_qa + 128 * i + r, self.o_ga + 128 * i + r])
        if kind == "bqk":
            return np.concatenate([self.o_qb + 128 * i + r, self.o_kb + 128 * i + r])
        if kind == "bvg":
            return np.concatenate([self.o_vb + 128 * i + r, self.o_gb + 128 * i + r])
        raise ValueError(kind)


class Buf:
    __slots__ = ("name", "wr", "rd", "sem", "cnt", "excl")

    def __init__(self, name):
        self.name = name
        self.excl = False
        self.wr = []
        self.rd = []
        self.sem = None
        self.cnt = 0


class Op:
    __slots__ = ("eng", "fn", "deps", "ticket", "needed", "dma", "idx")


ENGS = ("pe", "act", "dve", "pool", "sp")


class Prog:
    def __init__(self):
        self.ops = []
        self.dma_bufs = []

    def add(self, eng, fn, reads=(), writes=(), dma=None):
        op = Op()
        op.eng, op.fn, op.dma, op.needed, op.ticket = eng, fn, dma, False, None
        op.idx = len(self.ops)
        deps = []
        for b in reads:
            deps += b.wr
            if b.excl:
                deps += [r for r in b.rd if r.eng != eng]
        for b in writes:
            deps += b.wr
            deps += b.rd
        seen = set()
        op.deps = []
        for d in deps:
            if id(d) not in seen and d is not op:
                seen.add(id(d))
                if eng == "pe" and dma is None and d.eng == "pe" and d.dma is None:
                    continue
                op.deps.append(d)
        for d in op.deps:
            d.needed = True
        for b in reads:
            if dma is None:
                b.rd = [r for r in b.rd if not (r.dma is None and r.eng == eng)]
            b.rd.append(op)
        for b in writes:
            b.wr = [op]
            b.rd = []
        if dma is not None:
            if dma not in self.dma_bufs:
                self.dma_bufs.append(dma)
            dma.cnt += 16
            op.ticket = (dma, dma.cnt)
            op.needed = True
        self.ops.append(op)
        return op

    def barrier_bufs(self, bufs_all):
        pass

    def emit(self, nc, engines, eng_sems, dma_sems):
        cnt = {e: 0 for e in ENGS}
        for op in self.ops:
            if op.dma is None and op.needed:
                cnt[op.eng] += 1
                op.ticket = (op.eng, cnt[op.eng])

        def sem_of(t):
            return eng_sems[t[0]] if isinstance(t[0], str) else dma_sems[t[0].name]

        for e in ENGS:
            eng = engines[e]
            waited = {}
            for op in self.ops:
                if op.eng != e:
                    continue
                for d in op.deps:
                    if d.dma is None and d.eng == "pe" and e == "pe":
                        continue
                    key = d.ticket[0] if isinstance(d.ticket[0], str) else d.ticket[0].name
                    if waited.get(key, 0) >= d.ticket[1]:
                        continue
                    eng.wait_ge(sem_of(d.ticket), d.ticket[1])
                    waited[key] = d.ticket[1]
                inst = op.fn(eng)
                if op.dma is not None:
                    inst.then_inc(dma_sems[op.dma.name], 16)
                elif op.needed:
                    inst.then_inc(eng_sems[e], 1)
        return cnt


def build_nc(cfg: Cfg):
    c = cfg
    D, SEQ, DC, NT, QC, NQC, QT = c.D, c.SEQ, c.DC, c.NT, c.QC, c.NQC, c.QT
    NTT = NT + 1
    nc = bass.Bass("TRN2", target_bir_lowering=False)

    def din(name, shape, dt=F32):
        return nc.dram_tensor(name, list(shape), dt, kind="ExternalInput").ap()

    x_d = din("x", [SEQ, D])
    meta_d = din("meta", [c.NM, D])
    gpre_d = din("gpre", [128, DC])
    wch_d = din("wch", [c.NCH, 128, DC * 256])
    wout_d = din("wout", [128, DC * D])
    gvec_d = din("gvec", [128, 3 * 128])
    gpost_d = din("gpost", [128, D])
    lam_d = din("lamv", [128, 4 * 64])
    ident_d = din("ident", [128, 128])
    rope_d = din("rope", [NTT, 128, 256])
    ta_d = din("ta", [128, 2 * QC - 1])
    out_d = nc.dram_tensor("out", [SEQ, D], F32, kind="ExternalOutput").ap()

    P = Prog()
    from contextlib import ExitStack
    es = ExitStack()

    def sb(name, shape, dt):
        return es.enter_context(nc.sbuf_tensor(name, list(shape), dt))

    hn_raw = sb("hn_raw", [128, max(DC * c.L, DC * D)], BF16)
    yt_raw = sb("yt_raw", [128, max(DC * SEQ, 9 * D)], BF16)
    WS = 3
    wk_elems = max(WS * DC * 256 + (2 * SEQ + c.L + NTT * 129 + NT * 128), 2 * 2 * D * 2 + (2 * D if D > 512 else 0))
    wk_elems = (wk_elems + 63) // 64 * 64
    wk_raw = sb("wk_raw", [128, wk_elems], BF16)
    pT = [sb(f"pT{i}", [128, QC], BF16) for i in range(5)]
    osb = sb("osb", [128, 516], F32)
    ta_sb = sb("ta_sb", [128, 2 * QC - 1], F32)
    rope_sb = [sb(f"rope{i}", [128, 256], F32) for i in range(2)]
    ident = sb("ident_sb", [128, 128], BF16)
    gpre = sb("gpre_sb", [128, DC], F32)
    gvec = sb("gvec_sb", [128, 384], F32)
    sgs = sb("sgs", [128, 128], F32)
    lamv = sb("lamv_sb", [128, 256], F32)
    lamt = sb("lamt", [128, 8], F32)
    stat = sb("stat", [128, 3 * (NTT + 1)], F32)
    stat2 = sb("stat2", [128, 3 * (NTT + 1)], F32)
    NTMP = 4
    tA = [sb(f"tA{i}", [128, 128], F32) for i in range(NTMP)]
    tB = [sb(f"tB{i}", [128, 128], F32) for i in range(NTMP)]
    tC = [sb(f"tC{i}", [128, 128], F32) for i in range(NTMP)]
    tN = [sb(f"tN{i}", [128, 128], BF16) for i in range(NTMP)]
    tS = [sb(f"tS{i}", [128, 4], F32) for i in range(NTMP)]
    tY = [sb(f"tY{i}", [128, 128], BF16) for i in range(4)]
    tD = [sb(f"tD{i}", [128, 128], F32) for i in range(NTMP)]
    eA = [sb(f"eA{i}", [128, 128], F32) for i in range(4)]
    eS = [sb(f"eS{i}", [128, 4], F32) for i in range(4)]
    junk = sb("junk", [128, 128], BF16)
    o1buf = sb("o1buf", [128, QT * 128], F32)
    tE = [sb(f"tE{i}", [128, 4], F32) for i in range(4)]
    psA = es.enter_context(nc.psum_tensor("psA", [128, 2048], F32))
    psB = es.enter_context(nc.psum_tensor("psB", [128, 2048], F32))

    def bank(i):
        t = psA if i < 4 else psB
        return t[:, 512 * (i % 4):512 * (i % 4) + 512]

    hnT = hn_raw[:, 0:DC * c.L].rearrange("p (c t) -> p c t", c=DC)
    YT = yt_raw[:, 0:DC * SEQ].rearrange("p (c t) -> p c t", c=DC)
    wslot = [wk_raw[:, i * DC * 256:(i + 1) * DC * 256].rearrange("p (c n) -> p c n", c=DC) for i in range(WS)]
    o = WS * DC * 256
    qT = wk_raw[:, o:o + SEQ]; o += SEQ
    qT2 = wk_raw[:, o:o + SEQ]; o += SEQ
    kT = wk_raw[:, o:o + c.L]; o += c.L
    vU = wk_raw[:, o:o + NTT * 129].rearrange("p (t n) -> p t n", n=129); o += NTT * 129
    gate = wk_raw[:, o:o + NT * 128].rearrange("p (t n) -> p t n", n=128); o += NT * 128
    vAv = vU
    kTA = kT
    NS0 = 3
    xs = [yt_raw[:, i * 2 * D:(i + 1) * 2 * D].bitcast(F32) for i in range(NS0)]
    xn = [yt_raw[:, 2 * NS0 * D + i * D:2 * NS0 * D + (i + 1) * D] for i in range(NS0)]
    xr = [wk_raw[:, i * 2 * D:(i + 1) * 2 * D].bitcast(F32) for i in range(2)]
    ot = [wk_raw[:, 4 * D + i * 2 * D:4 * D + (i + 1) * 2 * D].bitcast(F32) for i in range(2)]
    gpost = None
    wout = hn_raw[:, 0:DC * D].rearrange("p (c n) -> p c n", c=DC)

    def tok(t):
        if t < NT:
            return slice(128 * t, 128 * t + 128), 128
        return slice(SEQ, SEQ + c.NM), c.NM

    B = {}

    def buf(name):
        if name not in B:
            B[name] = Buf(name)
        return B[name]

    b_bank = [buf(f"bank{i}") for i in range(8)]
    for b_ in b_bank:
        b_.excl = True
    b_hn = [buf(f"hn{t}") for t in range(NTT)]
    b_yt = [buf(f"yt{t}") for t in range(NT)]
    b_w = [buf(f"w{i}") for i in range(WS)]
    b_const = buf("const")

    def dma(eng, out, in_, buf_, reads=(), writes=()):
        return P.add(eng, lambda e, out=out, in_=in_: e.dma_start(out=out, in_=in_), reads=reads, writes=writes, dma=buf_)

    b_c = [buf(f"c{i}") for i in range(6)]
    dma("sp", gpre[:], gpre_d, b_c[0], writes=[b_c[0]])
    dma("sp", gvec[:], gvec_d, b_c[1], writes=[b_c[1]])
    dma("sp", lamv[:], lam_d, b_c[2], writes=[b_c[2]])
    dma("sp", ta_sb[:], ta_d, b_c[3], writes=[b_c[3]])
    dma("pool", ident[:], ident_d, b_c[4], writes=[b_c[4]])
    b_gpre, b_gvec, b_lamv, b_ta, b_ident = b_c[0], b_c[1], b_c[2], b_c[3], b_c[4]

    b_vU = buf("vU"); b_vA = b_vU; b_lamt = buf("lamt"); b_sgs = buf("sgs")
    P.add("pool", lambda e: e.memset(vAv[:, :, 128:129], 1.0), writes=[b_vA])
    lv = lamv[:].rearrange("p (a n) -> p a n", a=4)
    P.add("dve", lambda e: e.tensor_tensor(out=tA[0][:, 0:64], in0=lv[:, 0, :], in1=lv[:, 1, :], op=ALU.mult), reads=[b_lamv], writes=[buf("tA0")])
    P.add("dve", lambda e: e.tensor_tensor(out=tA[0][:, 64:128], in0=lv[:, 2, :], in1=lv[:, 3, :], op=ALU.mult), reads=[b_lamv, buf("tA0")], writes=[buf("tA0")])
    P.add("dve", lambda e: e.tensor_reduce(out=lamt[:, 0:2], in_=tA[0][:].rearrange("p (a n) -> p a n", a=2), axis=mybir.AxisListType.X, op=ALU.add), reads=[buf("tA0")], writes=[b_lamt])
    P.add("act", lambda e: e.activation(out=lamt[:, 2:4], in_=lamt[:, 0:2], func=AF.Exp), reads=[b_lamt], writes=[b_lamt])
    P.add("dve", lambda e: e.tensor_tensor(out=lamt[:, 4:5], in0=lamt[:, 3:4], in1=lamt[:, 2:3], op=ALU.subtract), reads=[b_lamt], writes=[b_lamt])
    P.add("dve", lambda e: e.tensor_scalar(out=lamt[:, 5:6], in0=lamt[:, 4:5], scalar1=-LAMBDA_INIT, scalar2=None, op0=ALU.add), reads=[b_lamt], writes=[b_lamt])
    neglam = lamt[:, 5:6]
    P.add("dve", lambda e: e.tensor_scalar(out=sgs[:], in0=gvec[:, 256:384], scalar1=(1.0 - LAMBDA_INIT), scalar2=None, op0=ALU.mult), reads=[b_gvec], writes=[b_sgs])

    b_xs = [buf(f"xs{i}") for i in range(NS0)]
    b_xn = [buf(f"xn{i}") for i in range(NS0)]
    b_stat = buf("stat")
    pend_evac = []
    b_ph0 = buf("ph0")
    HB = max(1, DC // 8)
    for t in range(NTT):
        sl, rows = tok(t)
        s = t % NS0
        src = x_d[128 * t:128 * t + 128, :] if t < NT else meta_d
        dma("sp", xs[s][0:rows, :], src, b_xs[s], writes=[b_xs[s]])
        ss = stat[0:rows, 3 * t:3 * t + 1]; ln_ = stat[0:rows, 3 * t + 1:3 * t + 2]; rs = stat[0:rows, 3 * t + 2:3 * t + 3]
        P.add("act", lambda e, s=s, rows=rows, ss=ss: e.activation(out=xn[s][0:rows, :], in_=xs[s][0:rows, :], func=AF.Square, accum_out=ss),
              reads=[b_xs[s], b_ph0], writes=[b_xn[s], b_stat])
        P.add("act", lambda e, ss=ss, ln_=ln_: e.activation(out=ln_, in_=ss, func=AF.Ln, scale=1.0 / D, bias=NORM_EPS), reads=[b_stat], writes=[b_stat])
        P.add("act", lambda e, rs=rs, ln_=ln_: e.activation(out=rs, in_=ln_, func=AF.Exp, scale=-0.5), reads=[b_stat], writes=[b_stat])
        P.add("dve", lambda e, s=s, rows=rows, rs=rs: e.tensor_scalar(out=xn[s][0:rows, :], in0=xs[s][0:rows, :], scalar1=rs, scalar2=None, op0=ALU.mult),
              reads=[b_xs[s], b_stat, b_ph0], writes=[b_xn[s]])
        for hb in range(HB):
            bk = (2 * (t % NS0) + hb) % 8 if HB <= 2 else hb % 8
            bkv = bank(bk).bitcast(BF16)
            nchb = min(8, DC - 8 * hb)
            for cc in range(nchb):
                ch = 8 * hb + cc
                P.add("pe", lambda e, bkv=bkv, cc=cc, ch=ch, s=s, rows=rows: e.transpose(bkv[:, cc * 128:cc * 128 + rows], xn[s][0:rows, ch * 128:(ch + 1) * 128], ident[0:rows, 0:rows]),
                      reads=[b_xn[s], b_ident, b_ph0], writes=[b_bank[bk]])
            srcv = bkv[:, 0:nchb * 128].rearrange("p (c r) -> p c r", c=nchb)[:, :, 0:rows]
            gv = gpre[:, 8 * hb:8 * hb + nchb].unsqueeze(2).to_broadcast([128, nchb, rows])
            pend_evac.append(lambda srcv=srcv, gv=gv, hb=hb, nchb=nchb, sl=sl, bk=bk, t=t: P.add(
                "dve", lambda e: e.tensor_tensor(out=hnT[:, 8 * hb:8 * hb + nchb, sl], in0=srcv, in1=gv, op=ALU.mult),
                reads=[b_bank[bk], b_gpre], writes=[b_hn[t]]))
        while len(pend_evac) > HB:
            pend_evac.pop(0)()
    while pend_evac:
        pend_evac.pop(0)()

    MARK = {}
    MARK['ph0'] = len(P.ops)
    wstate = {"next": 0}
    b_rope = [buf(f"rope{i}") for i in range(2)]
    rope_ctr = {"n": 0}
    tmp_ctr = {"n": 0}
    tn_ctr = {"n": 0}
    ty_ctr = {"n": 0}
    b_tY = [buf(f"tY{i}") for i in range(4)]
    b_tA = [buf(f"tA{i}") for i in range(NTMP)]; b_tB = [buf(f"tB{i}") for i in range(NTMP)]
    b_tC = [buf(f"tC{i}") for i in range(NTMP)]; b_tN = [buf(f"tN{i}") for i in range(NTMP)]
    b_tS = [buf(f"tS{i}") for i in range(NTMP)]
    b_junk = buf("junk")
    proj_banks = [0, 1, 2, 3, 4]
    pb_ctr = {"n": 0}
    tr_ctr = {"n": 0}
    b_qT = buf("qT"); b_qT2 = buf("qT2"); b_kT = buf("kT"); b_kTA = b_kT; b_gate = buf("gate")

    pend = []
    tk = {"n": 0, "seq": 0}

    def defer(fn, delay, keys=()):
        tk["seq"] += 1
        pend.append((tk["n"] + delay, tk["seq"], fn, tuple(keys)))
        pend.sort(key=lambda p: (p[0], p[1]))

    def acquire(key):
        while any(key in p[3] for p in pend):
            pend.pop(0)[2]()

    def tick():
        tk["n"] += 1
        while pend and pend[0][0] <= tk["n"]:
            pend.pop(0)[2]()

    def flush_all():
        while pend:
            pend.pop(0)[2]()

    def load_chunk(ci):
        s = ci % WS
        dst = wslot[s].rearrange("p c n -> p (c n)").rearrange("p (a b) -> p a b", b=512)
        srcv = wch_d[ci].rearrange("p (a b) -> p a b", b=512)
        dma("pool", dst, srcv, b_w[s], writes=[b_w[s]])

    PREFETCH = WS - 1
    for ci in range(min(PREFETCH, c.NCH)):
        load_chunk(ci)
    wstate["loaded"] = min(PREFETCH, c.NCH)

    def next_chunk():
        ci = wstate["next"]
        wstate["next"] += 1
        return ci % WS

    def prefetch_more():
        if wstate["loaded"] < c.NCH:
            load_chunk(wstate["loaded"])
            wstate["loaded"] += 1

    def proj_token_major(ws, t, ncols=256):
        sl, rows = tok(t)
        bk = proj_banks[pb_ctr["n"] % len(proj_banks)]; pb_ctr["n"] += 1
        for ch in range(DC):
            P.add("pe", lambda e, bk=bk, ch=ch, ws=ws, sl=sl, rows=rows: e.matmul(bank(bk)[0:rows, 0:ncols], hnT[:, ch, sl], wslot[ws][:, ch, 0:ncols], start=(ch == 0), stop=(ch == DC - 1)),
                  reads=[b_hn[t], b_w[ws]], writes=[b_bank[bk]])
        return bk

    def y_transpose(ybf, b_y, mix_chunk, tq, on_dve=False):
        k = tr_ctr["n"] % 4; tr_ctr["n"] += 1
        tv = bank(7).bitcast(BF16)[:, k * 256:k * 256 + 128]
        P.add("pe", lambda e: e.transpose(tv, ybf, ident[:, :]), reads=[b_y, b_ident], writes=[b_bank[7]])
        first = not ytr_ctr.get("started")
        ytr_ctr["started"] = True
        wr = [b_yt[tq]] + ([b_ph0] + b_xs + b_xn if first else [])
        if on_dve:
            P.add("dve", lambda e: e.tensor_copy(out=YT[:, mix_chunk, 128 * tq:128 * tq + 128], in_=tv), reads=[b_bank[7]], writes=wr)
        else:
            P.add("act", lambda e: e.activation(out=YT[:, mix_chunk, 128 * tq:128 * tq + 128], in_=tv, func=AF.Copy), reads=[b_bank[7]], writes=wr)

    qk_ctr = {"n": 0}
    b_tD = [buf(f"tD{i}") for i in range(NTMP)]

    def qk_tile(bk, rows, t, gcol, dstT, b_dst, sl, other):
        i = qk_ctr["n"] % NTMP; qk_ctr["n"] += 1
        key = ("qk", i)
        acquire(key)
        if other == "gate":
            acquire(("gate",))
        acquire(("sg", i))
        r = rope_ctr["n"] % 2; rope_ctr["n"] += 1
        srcp = bank(bk)[0:rows, 0:128]
        osrc = bank(bk)[0:rows, 128:256]
        dma("sp", rope_sb[r][0:rows, :], rope_d[t, 0:rows, :], b_rope[r], writes=[b_rope[r]])
        ss = tS[i][0:rows, 0:1]; ln_ = tS[i][0:rows, 1:2]; rs = tS[i][0:rows, 2:3]
        xg = tA[i][0:rows, :]
        t1 = tB[i][0:rows, :]
        t2 = tD[i][0:rows, :]
        xb = tN[i][0:rows, :]

        def st2():
            if other == "gate":
                P.add("act", lambda e: e.activation(out=tC[i][:], in_=osrc, func=AF.Exp, scale=-1.0), reads=[b_bank[bk]], writes=[b_tC[i]])
            else:
                _, vview, b_v = other
                P.add("act", lambda e: e.activation(out=vview[0:rows, t, 0:128], in_=osrc, func=AF.Copy), reads=[b_bank[bk]], writes=[b_v])
            P.add("act", lambda e: e.activation(out=junk[0:rows, :], in_=srcp, func=AF.Square, accum_out=ss), reads=[b_bank[bk]], writes=[b_junk, b_tS[i]])
            P.add("act", lambda e: e.activation(out=ln_, in_=ss, func=AF.Ln, scale=1.0 / 128, bias=NORM_EPS), reads=[b_tS[i]], writes=[b_tS[i]])
            P.add("act", lambda e: e.activation(out=rs, in_=ln_, func=AF.Exp, scale=-0.5), reads=[b_tS[i]], writes=[b_tS[i]])
            P.add("dve", lambda e: e.tensor_tensor(out=xg, in0=srcp, in1=gvec[0:rows, gcol:gcol + 128], op=ALU.mult), reads=[b_bank[bk], b_gvec], writes=[b_tA[i]])

        def st3():
            if other == "gate":
                P.add("act", lambda e: e.activation(out=tC[i][:], in_=tC[i][:], func=AF.Ln, bias=1.0), reads=[b_tC[i]], writes=[b_tC[i]])
                P.add("act", lambda e: e.activation(out=tC[i][:], in_=tC[i][:], func=AF.Exp, scale=-1.0), reads=[b_tC[i]], writes=[b_tC[i]])
            P.add("dve", lambda e: e.tensor_tensor(out=t1, in0=xg, in1=rope_sb[r][0:rows, 0:128], op=ALU.mult), reads=[b_tA[i], b_rope[r]], writes=[b_tB[i]])
            xsw = xg.rearrange("p (a s j) -> p a s j", a=2, s=2)[:, :, ::-1, :]
            P.add("pool", lambda e: e.tensor_tensor(out=t2.rearrange("p (a s j) -> p a s j", a=2, s=2), in0=xsw, in1=rope_sb[r][0:rows, 128:256].rearrange("p (a s j) -> p a s j", a=2, s=2), op=ALU.mult),
                  reads=[b_tA[i], b_rope[r]], writes=[b_tD[i]])

        def st4():
            if other == "gate":
                P.add("dve", lambda e: e.tensor_tensor(out=gate[:, t, :], in0=osrc, in1=tC[i][:], op=ALU.mult), reads=[b_bank[bk], b_tC[i]], writes=[b_gate])
            P.add("pool", lambda e: e.tensor_tensor(out=t1, in0=t1, in1=t2, op=ALU.add), reads=[b_tB[i], b_tD[i]], writes=[b_tB[i]])
            P.add("dve", lambda e: e.tensor_scalar(out=xb, in0=t1, scalar1=rs, scalar2=None, op0=ALU.mult), reads=[b_tB[i], b_tS[i]], writes=[b_tN[i]])

        def st5():
            k = tr_ctr["n"] % 4; tr_ctr["n"] += 1
            tv = bank(7).bitcast(BF16)[:, k * 256:k * 256 + rows]
            P.add("pe", lambda e: e.transpose(tv, xb, ident[0:rows, 0:rows]), reads=[b_tN[i], b_ident], writes=[b_bank[7]])
            P.add("dve", lambda e: e.tensor_copy(out=dstT[:, sl], in_=tv), reads=[b_bank[7]], writes=[b_dst])

        st2()
        defer(st3, 1, [key])
        defer(st4, 2, [key])
        defer(st5, 3, [key])
        tick()

    sg_ctr = {"n": 0}

    def silu_gate(bk, col0, t):
        i = sg_ctr["n"] % NTMP; sg_ctr["n"] += 1
        key = ("sg", i)
        acquire(key)
        acquire(("qk", i))
        acquire(("gate",))
        gsrc = bank(bk)[:, col0:col0 + 128]
        P.add("act", lambda e: e.activation(out=tC[i][:], in_=gsrc, func=AF.Exp, scale=-1.0), reads=[b_bank[bk]], writes=[b_tC[i]])

        def st():
            P.add("act", lambda e: e.activation(out=tC[i][:], in_=tC[i][:], func=AF.Ln, bias=1.0), reads=[b_tC[i]], writes=[b_tC[i]])
            P.add("act", lambda e: e.activation(out=tC[i][:], in_=tC[i][:], func=AF.Exp, scale=-1.0), reads=[b_tC[i]], writes=[b_tC[i]])

        def st_b():
            P.add("dve", lambda e: e.tensor_tensor(out=gate[:, t, :], in0=gsrc, in1=tC[i][:], op=ALU.mult), reads=[b_bank[bk], b_tC[i]], writes=[b_gate])
        defer(st, 1, [key])
        defer(st_b, 2, [key])

    SCALE_A = 1.0 / math.sqrt(128.0)
    NPT = len(pT)
    b_pT = [buf(f"pT{i}") for i in range(NPT)]
    b_o1s = [buf(f"o1_{j}") for j in range(QT)]
    b_osbs = [buf("osb0"), buf("osb1")]
    e_ctr = {"n": 0}
    b_eA = [buf(f"eA{i}") for i in range(4)]
    b_eS = [buf(f"eS{i}") for i in range(4)]
    b_tE = [buf(f"tE{i}") for i in range(4)]
    te_ctr = {"n": 0}
    ytr_ctr = {}
    SBANKS = [0, 1, 2, 3, 4]
    OBANKS = [5, 6]

    def attention(kq_aps, b_k, b_qs, v_ap, b_v, tiles_fn, epilogue, dve_heavy=False):
        nm = len(kq_aps)
        seq = [(cq, m, kt) for cq in range(NQC) for m in range(nm) for kt in range(NTT)]
        LOOK = len(SBANKS) - 1
        issued = {}

        def issue_S(n):
            cq, m, kt = seq[n]
            sl, rows = tok(kt)
            bk = SBANKS[n % len(SBANKS)]
            kT_ap, qT_ap = kq_aps[m]
            P.add("pe", lambda e: e.matmul(bank(bk)[0:rows, 0:QC], kT_ap[:, sl], qT_ap[:, cq * QC:(cq + 1) * QC], start=True, stop=True),
                  reads=[b_k, b_qs[m]], writes=[b_bank[bk]])
            issued[n] = bk

        LA = 2

        def tile_ops(n):
            cq, m, kt = seq[n]
            sl, rows = tok(kt)
            pi = n % NPT
            tiles_fn(m, issued[n], rows, cq, kt, pT[pi][0:rows, :], b_pT[pi])

        for n in range(min(LOOK, len(seq))):
            issue_S(n)
        for n in range(min(LA, len(seq))):
            tile_ops(n)
        for n, (cq, m, kt) in enumerate(seq):
            sl, rows = tok(kt)
            pi = n % NPT
            if n + LA < len(seq):
                tile_ops(n + LA)
            for j in range(QT):
                ob = OBANKS[j // 2]
                oc = (j % 2) * 129
                first_in_bank = (kt == 0 and (j % 2 == 0))
                P.add("pe", lambda e, ob=ob, oc=oc, j=j, rows=rows, kt=kt, pi=pi, fib=first_in_bank: e.matmul(
                    bank(ob)[:, oc:oc + 129], pT[pi][0:rows, j * 128:(j + 1) * 128], v_ap[0:rows, kt, :],
                    start=fib, stop=(kt == NTT - 1), skip_group_check=True),
                    reads=[b_pT[pi], b_v], writes=[b_bank[ob]])
            if n + LOOK < len(seq):
                issue_S(n + LOOK)
            if kt == NTT - 1:
                acquire(("osb",))
                nob = (QT + 1) // 2
                for bi in range(nob):
                    w = 129 * min(2, QT - 2 * bi)
                    if bi == 0:
                        P.add("act", lambda e, bi=bi, w=w: e.activation(out=osb[:, bi * 258:bi * 258 + w], in_=bank(OBANKS[bi])[:, 0:w], func=AF.Copy),
                              reads=[b_bank[OBANKS[bi]]], writes=[b_osbs[bi]])
                    elif dve_heavy:
                        P.add("act", lambda e, bi=bi, w=w: e.activation(out=osb[:, bi * 258:bi * 258 + w], in_=bank(OBANKS[bi])[:, 0:w], func=AF.Copy),
                              reads=[b_bank[OBANKS[bi]]], writes=[b_osbs[bi]])
                    else:
                        P.add("dve", lambda e, bi=bi, w=w: e.tensor_copy(out=osb[:, bi * 258:bi * 258 + w], in_=bank(OBANKS[bi])[:, 0:w]),
                              reads=[b_bank[OBANKS[bi]]], writes=[b_osbs[bi]])
                for j in range(QT):
                    epilogue(m, cq, j, osb[:, j * 129:(j + 1) * 129], b_osbs[j // 2])
            tick()

    for hq in range(c.AH):
        g = hq // c.AG
        if hq % c.AG == 0:
            ws = next_chunk()
            for t in range(NTT):
                sl, rows = tok(t)
                bk = proj_token_major(ws, t)
                qk_tile(bk, rows, t, 128, kTA, b_kTA, sl, ("v", vAv, b_vA))
            prefetch_more()
        ws = next_chunk()
        for t in range(NT):
            sl, rows = tok(t)
            bk = proj_token_major(ws, t)
            qk_tile(bk, rows, t, 0, qT, b_qT, sl, "gate")
        prefetch_more()
        flush_all()

        def tiles_A(m, bk, rows, cq, kt, pdst, b_p):
            P.add("act", lambda e: e.activation(out=pdst, in_=bank(bk)[0:rows, 0:QC], func=AF.Exp, scale=SCALE_A), reads=[b_bank[bk]], writes=[b_p])

        def epi_A(m, cq, j, ops, b_ops, hq=hq):
            tq = cq * QT + j
            i = e_ctr["n"] % 4; e_ctr["n"] += 1
            key = ("e", i)
            acquire(key)

            def e2():
                P.add("dve", lambda e: e.reciprocal(out=tE[i][:, 0:1], in_=ops[:, 128:129]), reads=[b_ops], writes=[b_tE[i]])
                P.add("dve", lambda e: e.scalar_tensor_tensor(out=tY[i][:], in0=ops[:, 0:128], scalar=tE[i][:, 0:1], in1=gate[:, tq, :], op0=ALU.mult, op1=ALU.mult),
                      reads=[b_ops, b_tE[i], b_gate], writes=[b_tY[i]])
            defer(e2, 1 + j, [key, ("osb",), ("gate",)])
            defer(lambda: y_transpose(tY[i][:], b_tY[i], hq, tq, on_dve=True), 3 + j, [key])

        attention([(kTA, qT)], b_kTA, [b_qT], vAv, b_vA, tiles_A, epi_A)

    MARK['A'] = len(P.ops)
    WO = {}

    def load_wout():
        b_wout = buf("wout")
        NWD = 4 if DC >= 4 else 1
        cpd = DC // NWD
        b_wo = [buf(f"wo{i}") for i in range(NWD)]
        WO['b_wo'] = b_wo; WO['cpd'] = cpd
        for i in range(NWD):
            dstv = wout[:, i * cpd:(i + 1) * cpd, :].rearrange("p c n -> p (c n)").rearrange("p (a b) -> p a b", b=512)
            srcv = wout_d[:, i * cpd * D:(i + 1) * cpd * D].rearrange("p (a b) -> p a b", b=512)
            dma("pool", dstv, srcv, b_wo[i], reads=[], writes=[b_wo[i]] + b_hn)

    P.add("pool", lambda e: e.memset(qT[64:128, :], 0.0), writes=[b_qT])
    P.add("pool", lambda e: e.memset(qT2[0:64, :], 0.0), writes=[b_qT2])
    YW = 2 * QC - 1
    Y0 = QC - 128
    WL = Y0 + 127
    for h in range(c.BH):
        slope = 2.0 ** (-8.0 * (h + 1) / c.BH)
        ws = next_chunk()
        ncht = NQC
        for cq in range(NQC):
            bk = proj_banks[pb_ctr["n"] % len(proj_banks)]; pb_ctr["n"] += 1
            for ch in range(DC):
                P.add("pe", lambda e, bk=bk, ch=ch, ws=ws, cq=cq: e.matmul(bank(bk)[:, 0:QC], wslot[ws][:, ch, 0:128], hnT[:, ch, cq * QC:(cq + 1) * QC], start=(ch == 0), stop=(ch == DC - 1)),
                      reads=[b_hn[tt] for tt in range(cq * QT, (cq + 1) * QT)] + [b_w[ws]], writes=[b_bank[bk]])
            P.add("act", lambda e, bk=bk, cq=cq: e.activation(out=qT[0:64, cq * QC:(cq + 1) * QC], in_=bank(bk)[0:64, 0:QC], func=AF.Copy, scale=0.125), reads=[b_bank[bk]], writes=[b_qT])
            P.add("act", lambda e, bk=bk, cq=cq: e.activation(out=qT2[64:128, cq * QC:(cq + 1) * QC], in_=bank(bk)[64:128, 0:QC], func=AF.Copy, scale=0.125), reads=[b_bank[bk]], writes=[b_qT2])
            tick()
        for cq in range(NQC + 1):
            bk = proj_banks[pb_ctr["n"] % len(proj_banks)]; pb_ctr["n"] += 1
            csl = slice(cq * QC, (cq + 1) * QC) if cq < NQC else slice(SEQ, SEQ + c.NM)
            n = QC if cq < NQC else c.NM
            rd = [b_hn[tt] for tt in range(cq * QT, (cq + 1) * QT)] if cq < NQC else [b_hn[NT]]
            for ch in range(DC):
                P.add("pe", lambda e, bk=bk, ch=ch, ws=ws, csl=csl, n=n: e.matmul(bank(bk)[:, 0:n], wslot[ws][:, ch, 128:256], hnT[:, ch, csl], start=(ch == 0), stop=(ch == DC - 1)),
                      reads=rd + [b_w[ws]], writes=[b_bank[bk]])
            P.add("dve", lambda e, bk=bk, csl=csl, n=n: e.tensor_copy(out=kT[:, csl], in_=bank(bk)[:, 0:n]), reads=[b_bank[bk]], writes=[b_kT])
            tick()
        prefetch_more()
        ws = next_chunk()
        for t in range(NTT):
            sl, rows = tok(t)
            ncols = 256 if t < NT else 128
            bk = proj_token_major(ws, t, ncols)
            P.add("act", lambda e, bk=bk, rows=rows, t=t: e.activation(out=vU[0:rows, t, 0:128], in_=bank(bk)[0:rows, 0:128], func=AF.Copy), reads=[b_bank[bk]], writes=[b_vU])
            if t < NT:
                silu_gate(bk, 128, t)
            tick()
        prefetch_more()
        flush_all()
        if h == c.BH - 1:
            load_wout()

        def tiles_B(m, bk, rows, cq, kt, pdst, b_p, slope=slope):
            if kt < NT:
                delta = QC * cq - 128 * kt
                if -QC < delta < 128:
                    j = (-delta) // 128
                    ws_ = Y0 - 128 * j
                    coef, cst = -slope, 0.0
                elif delta >= 128:
                    ws_ = WL
                    coef, cst = -slope, -slope * (delta - 127)
                else:
                    ws_ = WL
                    coef, cst = slope, slope * (delta - 127)
            else:
                ws_ = WL
                delta = QC * cq + c.NM
                coef, cst = -slope, -slope * (delta - 127)
            sv = bank(bk)[0:rows, 0:QC]
            P.add("dve", lambda e: e.scalar_tensor_tensor(out=sv, in0=ta_sb[0:rows, ws_:ws_ + QC], scalar=coef, in1=sv, op0=ALU.mult, op1=ALU.add),
                  reads=[b_bank[bk], b_ta], writes=[b_bank[bk]])
            P.add("act", lambda e: e.activation(out=pdst, in_=sv, func=AF.Exp, bias=float(cst)), reads=[b_bank[bk]], writes=[b_p])

        def epi_B(m, cq, j, ops, b_ops, h=h):
            tq = cq * QT + j
            i = e_ctr["n"] % 4; e_ctr["n"] += 1
            key = ("e", i)
            acquire(key)
            od = eA[i][:]
            ss = eS[i][:, 0:1]; ln_ = eS[i][:, 1:2]; rs = eS[i][:, 2:3]

            def e2():
                P.add("dve", lambda e: e.reciprocal(out=tE[i][:, 0:1], in_=ops[:, 128:129]), reads=[b_ops], writes=[b_tE[i]])
                if m == 0:
                    P.add("pool", lambda e: e.tensor_scalar(out=o1buf[:, j * 128:(j + 1) * 128], in0=ops[:, 0:128], scalar1=tE[i][:, 0:1], scalar2=1.0, op0=ALU.mult, op1=ALU.mult),
                          reads=[b_ops, b_tE[i]], writes=[b_o1s[j]])
                    return
                P.add("dve", lambda e: e.tensor_tensor(out=tE[i][:, 1:2], in0=tE[i][:, 0:1], in1=neglam, op=ALU.mult), reads=[b_tE[i], b_lamt], writes=[b_tE[i]])
                P.add("pool", lambda e: e.tensor_scalar(out=od, in0=ops[:, 0:128], scalar1=tE[i][:, 1:2], scalar2=1.0, op0=ALU.mult, op1=ALU.mult),
                      reads=[b_ops, b_tE[i]], writes=[b_eA[i]])
                P.add("pool", lambda e: e.tensor_tensor(out=od, in0=od, in1=o1buf[:, j * 128:(j + 1) * 128], op=ALU.add),
                      reads=[b_eA[i], b_o1s[j]], writes=[b_eA[i]])

            def e3():
                P.add("act", lambda e: e.activation(out=junk[:], in_=od, func=AF.Square, accum_out=ss), reads=[b_eA[i]], writes=[b_junk, b_eS[i]])
                P.add("act", lambda e: e.activation(out=ln_, in_=ss, func=AF.Ln, scale=1.0 / 128, bias=NORM_EPS), reads=[b_eS[i]], writes=[b_eS[i]])
                P.add("act", lambda e: e.activation(out=rs, in_=ln_, func=AF.Exp, scale=-0.5), reads=[b_eS[i]], writes=[b_eS[i]])

            def e4():
                P.add("pool", lambda e: e.tensor_tensor(out=od, in0=od, in1=sgs[:], op=ALU.mult), reads=[b_eA[i], b_sgs], writes=[b_eA[i]])
                P.add("pool", lambda e: e.tensor_scalar(out=od, in0=od, scalar1=rs, scalar2=1.0, op0=ALU.mult, op1=ALU.mult),
                      reads=[b_eA[i], b_eS[i]], writes=[b_eA[i]])
                P.add("pool", lambda e: e.tensor_tensor(out=tY[i][:], in0=od, in1=gate[:, tq, :], op=ALU.mult),
                      reads=[b_eA[i], b_gate], writes=[b_tY[i]])

            defer(e2, 1 + j, [key, ("osb",)])
            if m == 1:
                defer(e3, 4 + j, [key])
                defer(e4, 7 + j, [key, ("gate",)])
                defer(lambda: y_transpose(tY[i][:], b_tY[i], c.AH + h, tq, on_dve=True), 10 + j, [key])

        attention([(kT, qT), (kT, qT2)], b_kT, [b_qT, b_qT2], vU, b_vU, tiles_B, epi_B, dve_heavy=False)

    flush_all()
    MARK['B'] = len(P.ops)
    gpost_sb = None
    b_xr = [buf(f"xr{i}") for i in range(2)]
    b_ot = [buf(f"ot{i}") for i in range(2)]
    b_unit = [b_w[i] for i in range(WS)] + [b_qT, b_qT2, b_kT, b_vU, b_gate]
    b_gpost = buf("gpost")
    gpost_sb = sb("gpost_sb", [128, D], F32) if D <= 512 else None
    if gpost_sb is None:
        off = 8 * D
        assert wk_elems >= off + 2 * D, (wk_elems, off + 2 * D)
        gpost_v = wk_raw[:, off:off + 2 * D].bitcast(F32)
    else:
        gpost_v = gpost_sb[:]
    dma("sp", gpost_v, gpost_d, b_gpost, writes=[b_gpost] + b_unit)
    NB = D // 512 if D >= 512 else 1
    BW_ = min(512, D)
    for t in range(NT):
        s = t % 2
        sl, rows = tok(t)
        dma("sp", xr[s][:], x_d[128 * t:128 * t + 128, :], b_xr[s], writes=[b_xr[s]] + (b_unit if t < 2 else []))
        base = 4 * (t % 2)
        if NB * 1 > 4:
            raise NotImplementedError
        for nb in range(NB):
            bk = base + nb
            for ch in range(DC):
                P.add("pe", lambda e, bk=bk, ch=ch, nb=nb, sl=sl: e.matmul(bank(bk)[:, 0:BW_], YT[:, ch, sl], wout[:, ch, nb * BW_:(nb + 1) * BW_], start=(ch == 0), stop=(ch == DC - 1)),
                      reads=[b_yt[t], WO['b_wo'][ch // WO['cpd']]], writes=[b_bank[bk]])
        pst = (psA if base == 0 else psB)[:, 0:D] if D >= 512 else (psA if base == 0 else psB)[:, 0:D]
        bks = [b_bank[base + nb] for nb in range(NB)]
        ss = stat2[:, 3 * t:3 * t + 1]; ln_ = stat2[:, 3 * t + 1:3 * t + 2]; rs = stat2[:, 3 * t + 2:3 * t + 3]
        b_st2 = buf(f"st2_{t % 2}")
        P.add("act", lambda e, s=s, pst=pst, ss=ss: e.activation(out=ot[s][:], in_=pst, func=AF.Square, accum_out=ss), reads=bks, writes=[b_ot[s], b_st2] + (b_unit if t < 2 else []))
        P.add("act", lambda e, ss=ss, ln_=ln_: e.activation(out=ln_, in_=ss, func=AF.Ln, scale=1.0 / D, bias=NORM_EPS), reads=[b_st2], writes=[b_st2])
        P.add("act", lambda e, rs=rs, ln_=ln_: e.activation(out=rs, in_=ln_, func=AF.Exp, scale=-0.5), reads=[b_st2], writes=[b_st2])
        P.add("dve", lambda e, s=s, pst=pst: e.tensor_tensor(out=ot[s][:], in0=pst, in1=gpost_v, op=ALU.mult), reads=bks + [b_gpost], writes=[b_ot[s]])
        P.add("dve", lambda e, s=s, rs=rs: e.scalar_tensor_tensor(out=ot[s][:], in0=ot[s][:], scalar=rs, in1=xr[s][:], op0=ALU.mult, op1=ALU.add),
              reads=[b_ot[s], b_st2, b_xr[s]], writes=[b_ot[s]])
        dma("sp", out_d[128 * t:128 * t + 128, :], ot[s][:], b_ot[s], reads=[b_ot[s]], writes=[buf(f"outd{t}")])
    fin_reads = [buf(f"outd{t}") for t in range(NT)]
    P.add("sp", lambda e: e.nop(), reads=fin_reads)

    MARK['end'] = len(P.ops)
    import os as _os
    _tr = _os.environ.get('KTRUNC')
    if _tr:
        P.ops = P.ops[:(MARK[_tr] if _tr in MARK else int(_tr))]
        print('TRUNC', MARK, len(P.ops))
    eng_sems = {e: es.enter_context(nc.semaphore(f"s_{e}")) for e in ENGS}
    dma_sems = {b.name: es.enter_context(nc.semaphore(f"d_{b.name}")) for b in P.dma_bufs}
    block = es.enter_context(nc.Block())
    engines = {}

    def make(ename):
        def body(eng):
            engines_local = {ename: eng}
            emit_one(ename, eng)
        return body

    cnt = {e: 0 for e in ENGS}
    for op in P.ops:
        if op.dma is None and op.needed:
            cnt[op.eng] += 1
            op.ticket = (op.eng, cnt[op.eng])

    def sem_of(t):
        return eng_sems[t[0]] if isinstance(t[0], str) else dma_sems[t[0].name]

    def emit_one(e, eng):
        waited = {}
        for op in P.ops:
            if op.eng != e:
                continue
            for d in op.deps:
                if d.dma is None and d.eng == "pe" and e == "pe":
                    continue
                key = d.ticket[0] if isinstance(d.ticket[0], str) else d.ticket[0].name
                if waited.get(key, 0) >= d.ticket[1]:
                    continue
                eng.wait_ge(sem_of(d.ticket), d.ticket[1])
                waited[key] = d.ticket[1]
            inst = op.fn(eng)
            if op.dma is not None:
                inst.then_inc(dma_sems[op.dma.name], 16)
            elif op.needed:
                inst.then_inc(eng_sems[e], 1)

    @block.tensor
    def _(eng):
        emit_one("pe", eng)

    @block.scalar
    def _(eng):
        emit_one("act", eng)

    @block.vector
    def _(eng):
        emit_one("dve", eng)

    @block.gpsimd
    def _(eng):
        emit_one("pool", eng)

    @block.sync
    def _(eng):
        emit_one("sp", eng)

    es.close()
    return nc, cnt


def host_consts(cfg: Cfg):
    c = cfg
    NTT = c.NT + 1
    inv_freq = ROPE_THETA ** (-np.arange(0, 64, 2, dtype=np.float64) / 64.0)
    rope = np.zeros((NTT, 128, 256), np.float32)
    n = np.arange(c.SEQ)
    gr = (n // 64).astype(np.float64)[:, None] * inv_freq[None, :]
    gc = (n % 64).astype(np.float64)[:, None] * inv_freq[None, :]
    cr, sr, cc, sc = np.cos(gr), np.sin(gr), np.cos(gc), np.sin(gc)
    COS = np.concatenate([cr, cr, cc, cc], axis=1).astype(np.float32)
    SINS = np.concatenate([-sr, sr, -sc, sc], axis=1).astype(np.float32)
    rope[:c.NT, :, 0:128] = COS.reshape(c.NT, 128, 128)
    rope[:c.NT, :, 128:256] = SINS.reshape(c.NT, 128, 128)
    rope[c.NT, :, 0:128] = 1.0
    QC = c.QC
    Y0 = QC - 128
    y = np.arange(2 * QC - 1)[None, :]
    k = np.arange(128)[:, None]
    ta = np.abs(y - Y0 - k).astype(np.float32)
    ident = np.eye(128, dtype=np.float32)
    return rope, ta, ident


def host_inputs(cfg: Cfg, x, meta_tokens, pre_norm_g, w_in, q_norm_g, k_norm_g, lambda_q1, lambda_k1,
                lambda_q2, lambda_k2, subln_g, w_out, post_norm_g):
    c = cfg
    f = np.float32
    rope, ta, ident = host_consts(c)
    w_in0 = np.asarray(w_in[0], f)
    wr = w_in0.reshape(c.DC, 128, c.INW)
    wch = np.empty((c.NCH, 128, c.DC * 256), f)
    for ci, (kind, i) in enumerate(c.chunks):
        cols = c.chunk_cols(kind, i)
        wch[ci] = wr[:, :, cols].transpose(1, 0, 2).reshape(128, c.DC * 256)
    wout = np.ascontiguousarray(np.asarray(w_out[0], f).reshape(c.DC, 128, c.D).transpose(1, 0, 2).reshape(128, c.DC * c.D))
    gpre = np.ascontiguousarray(np.asarray(pre_norm_g[0], f).reshape(c.DC, 128).T)
    gvec = np.ascontiguousarray(np.broadcast_to(np.concatenate([np.asarray(q_norm_g[0], f), np.asarray(k_norm_g[0], f), np.asarray(subln_g[0], f)])[None, :], (128, 384)))
    gpost = np.ascontiguousarray(np.broadcast_to(np.asarray(post_norm_g[0], f)[None, :], (128, c.D)))
    lamv = np.ascontiguousarray(np.broadcast_to(np.concatenate([np.asarray(lambda_q1[0], f), np.asarray(lambda_k1[0], f), np.asarray(lambda_q2[0], f), np.asarray(lambda_k2[0], f)])[None, :], (128, 256)))
    shared = {"meta": np.ascontiguousarray(np.asarray(meta_tokens, f)), "gpre": gpre, "wch": wch, "wout": wout, "gvec": gvec,
              "gpost": gpost, "lamv": lamv, "ident": ident, "rope": rope, "ta": ta}
    xs = np.asarray(x, f)
    return [dict(shared, x=np.ascontiguousarray(xs[b])) for b in range(xs.shape[0])]


_NC_CACHE = {}


def kernel(x, meta_tokens, pre_norm_g, w_in, q_norm_g, k_norm_g, lambda_q1, lambda_k1,
           lambda_q2, lambda_k2, subln_g, w_out, post_norm_g):
    cfg = Cfg(2048, 2048)
    in_maps = host_inputs(cfg, x, meta_tokens, pre_norm_g, w_in, q_norm_g, k_norm_g, lambda_q1, lambda_k1,
                          lambda_q2, lambda_k2, subln_g, w_out, post_norm_g)
    nc, _ = build_nc(cfg)
    res = run_bass_kernel_spmd(nc, in_maps, core_ids=list(range(len(in_maps))))
    return np.stack([np.asarray(r["out"], np.float32) for r in res.results], axis=0)
```
